# Optimizing a Trainium2 kernel written in Bass

```python
import math, functools
import jax, jax.numpy as jnp
from jax import lax
import numpy as np

D_MODEL = 1024
BATCH = 8
SEQ = 2048
DEPTH = 2

D_FF = 2816
SSD_HEADS = 16
SSD_HEAD_DIM = 64
SSD_D_INNER = SSD_HEADS * SSD_HEAD_DIM
SSD_GROUPS = 2
SSD_STATE = 128
SSD_CONV = 4
SSD_CHUNK = 128
SSD_CONV_DIM = SSD_D_INNER + 2 * SSD_GROUPS * SSD_STATE
MLA_HEADS = 8
MLA_Q_RANK = 384
MLA_KV_RANK = 256
MLA_NOPE = 128
MLA_ROPE = 64
MLA_V = 128
ROPE_THETA = 10000.0
ATTN_BLOCK = 128
SGU_WIDTH = 2 * D_MODEL
SGU_GROUPS = 16
SGU_GROUP_DIM = SGU_WIDTH // SGU_GROUPS
SGU_CHUNK = 128
EVEN_SPLITS = [SSD_D_INNER,
               SSD_D_INNER + SSD_CONV_DIM,
               SSD_D_INNER + SSD_CONV_DIM + SSD_HEADS,
               SSD_D_INNER + SSD_CONV_DIM + SSD_HEADS + MLA_Q_RANK,
               SSD_D_INNER + SSD_CONV_DIM + SSD_HEADS + MLA_Q_RANK + MLA_KV_RANK]
EVEN_IN = EVEN_SPLITS[-1] + MLA_ROPE
EVEN_MIX = SSD_D_INNER + MLA_HEADS * MLA_V
DEEPNORM_ALPHA = (2 * DEPTH) ** 0.25
DEEPNORM_BETA = (8 * DEPTH) ** -0.25
N_MOD = 9
EPS = 1e-5

kernel_name = "conditioned_hybrid_ssd_mla_sgu_trunk"


def layer_norm(x, g, b):
    xf = x.astype(jnp.float32)
    mu = jnp.mean(xf, -1, keepdims=True)
    var = jnp.mean(jnp.square(xf - mu), -1, keepdims=True)
    return ((xf - mu) * lax.rsqrt(var + EPS) * g + b).astype(x.dtype)


def rms_norm(x, w):
    xf = x.astype(jnp.float32)
    return (xf * lax.rsqrt(jnp.mean(xf * xf, -1, keepdims=True) + EPS) * w).astype(x.dtype)


def swiglu(h, w_in, w_out):
    a, b = jnp.split(h @ w_in, 2, axis=-1)
    return (jax.nn.silu(a) * b) @ w_out


def rope_tables(seq):
    inv = 1.0 / (ROPE_THETA ** (jnp.arange(0, MLA_ROPE, 2, dtype=jnp.float32) / MLA_ROPE))
    ang = jnp.arange(seq, dtype=jnp.float32)[:, None] * inv[None, :]
    return jnp.cos(ang), jnp.sin(ang)


def apply_rope(x, cos, sin):
    x1, x2 = jnp.split(x.astype(jnp.float32), 2, axis=-1)
    return jnp.concatenate([x1 * cos - x2 * sin, x2 * cos + x1 * sin], -1).astype(x.dtype)


def causal_depthwise_conv(x, w, bias):
    out = lax.conv_general_dilated(x, w[:, None, :], window_strides=(1,),
                                   padding=[(SSD_CONV - 1, 0)],
                                   dimension_numbers=('NWC', 'WIO', 'NWC'),
                                   feature_group_count=x.shape[-1])
    return out + bias


def segsum_exp(a):
    q = a.shape[-1]
    cs = jnp.cumsum(a, axis=-1)
    diff = cs[..., :, None] - cs[..., None, :]
    mask = jnp.tril(jnp.ones((q, q), dtype=bool))
    return jnp.where(mask, jnp.exp(jnp.where(mask, diff, 0.0)), 0.0)


def ssd_chunked_scan(x, dt, a, bm, cm):
    b, s = x.shape[:2]
    nc = s // SSD_CHUNK
    r = SSD_HEADS // SSD_GROUPS
    xc = (x * dt[..., None]).reshape(b, nc, SSD_CHUNK, SSD_GROUPS, r, SSD_HEAD_DIM)
    adt = (dt * a).reshape(b, nc, SSD_CHUNK, SSD_GROUPS, r).transpose(0, 3, 4, 1, 2)
    bc = bm.reshape(b, nc, SSD_CHUNK, SSD_GROUPS, SSD_STATE)
    cc = cm.reshape(b, nc, SSD_CHUNK, SSD_GROUPS, SSD_STATE)
    a_cs = jnp.cumsum(adt, axis=-1)
    decay = segsum_exp(adt)
    cb = jnp.einsum('bclgn,bcsgn->bgcls', cc, bc)
    y_diag = jnp.einsum('bgrcls,bcsgrp->bclgrp', cb[:, :, None] * decay, xc)
    decay_to_end = jnp.exp(a_cs[..., -1:] - a_cs)
    states = jnp.einsum('bclgn,bgrcl,bclgrp->bcgrpn', bc, decay_to_end, xc)
    chunk_decay = jnp.exp(a_cs[..., -1])

    def step(h, inp):
        s_c, d_c = inp
        return d_c[..., None, None] * h + s_c, h

    h0 = jnp.zeros((b, SSD_GROUPS, r, SSD_HEAD_DIM, SSD_STATE), states.dtype)
    _, prev = lax.scan(step, h0, (states.transpose(1, 0, 2, 3, 4, 5),
                                  chunk_decay.transpose(3, 0, 1, 2)))
    prev = prev.transpose(1, 0, 2, 3, 4, 5)
    y_off = jnp.einsum('bclgn,bcgrpn,bgrcl->bclgrp', cc, prev, jnp.exp(a_cs))
    return (y_diag + y_off).reshape(b, s, SSD_HEADS, SSD_HEAD_DIM)


def mla_attention(cq, ckv, k_rope, q_norm_w, w_uq, kv_norm_w, w_ukv, cos, sin):
    b, s, _ = cq.shape
    q = (rms_norm(cq, q_norm_w) @ w_uq).reshape(b, s, MLA_HEADS, MLA_NOPE + MLA_ROPE)
    q_nope, q_pe = q[..., :MLA_NOPE], q[..., MLA_NOPE:]
    q_pe = apply_rope(q_pe, cos[:, None, :], sin[:, None, :])
    kv = (rms_norm(ckv, kv_norm_w) @ w_ukv).reshape(b, s, MLA_HEADS, MLA_NOPE + MLA_V)
    k_nope, v = kv[..., :MLA_NOPE], kv[..., MLA_NOPE:]
    k_pe = apply_rope(k_rope, cos, sin)
    k = jnp.concatenate([k_nope, jnp.broadcast_to(k_pe[:, :, None, :], (b, s, MLA_HEADS, MLA_ROPE))], -1)
    qh = jnp.concatenate([q_nope, q_pe], -1)
    scale = (MLA_NOPE + MLA_ROPE) ** -0.5
    outs = []
    for i in range(s // ATTN_BLOCK):
        start, end = i * ATTN_BLOCK, (i + 1) * ATTN_BLOCK
        sc = jnp.einsum('bqhd,bkhd->bhqk', qh[:, start:end], k[:, :end]).astype(jnp.float32) * scale
        mask = (start + jnp.arange(ATTN_BLOCK))[:, None] >= jnp.arange(end)[None, :]
        p = jax.nn.softmax(jnp.where(mask, sc, -jnp.inf), axis=-1).astype(v.dtype)
        outs.append(jnp.einsum('bhqk,bkhd->bqhd', p, v[:, :end]))
    return jnp.concatenate(outs, axis=1).reshape(b, s, MLA_HEADS * MLA_V)


def even_mixer(h, w_in, conv_w, conv_b, dt_bias, a_log, d_skip, ssd_norm_w,
               q_norm_w, w_uq, kv_norm_w, w_ukv, w_out, cos, sin):
    b, s, _ = h.shape
    z, xbc, dt_raw, cq, ckv, k_rope = jnp.split(h @ w_in, EVEN_SPLITS, axis=-1)
    xbc = jax.nn.silu(causal_depthwise_conv(xbc, conv_w, conv_b))
    xs, bm, cm = jnp.split(xbc, [SSD_D_INNER, SSD_D_INNER + SSD_GROUPS * SSD_STATE], axis=-1)
    xs4 = xs.reshape(b, s, SSD_HEADS, SSD_HEAD_DIM).astype(jnp.float32)
    dt = jax.nn.softplus(dt_raw.astype(jnp.float32) + dt_bias)
    a = -jnp.exp(a_log.astype(jnp.float32))
    y = ssd_chunked_scan(xs4, dt, a,
                         bm.reshape(b, s, SSD_GROUPS, SSD_STATE).astype(jnp.float32),
                         cm.reshape(b, s, SSD_GROUPS, SSD_STATE).astype(jnp.float32))
    y = (y + d_skip[:, None] * xs4).reshape(b, s, SSD_D_INNER).astype(h.dtype)
    yg = (y * jax.nn.silu(z)).reshape(b, s, SSD_GROUPS, SSD_D_INNER // SSD_GROUPS)
    y_ssd = rms_norm(yg, ssd_norm_w.reshape(SSD_GROUPS, -1)).reshape(b, s, SSD_D_INNER)
    y_att = mla_attention(cq, ckv, k_rope, q_norm_w, w_uq, kv_norm_w, w_ukv, cos, sin)
    return jnp.concatenate([y_ssd, y_att], axis=-1) @ w_out


def odd_mixer(h, w_uv, b_uv, sgu_ln_g, sgu_ln_b, w_s, b_s, w_out):
    b, s, _ = h.shape
    u, v = jnp.split(jax.nn.gelu(h @ w_uv + b_uv, approximate=False), 2, axis=-1)
    v = layer_norm(v, sgu_ln_g, sgu_ln_b)
    vc = v.reshape(b, s // SGU_CHUNK, SGU_CHUNK, SGU_GROUPS, SGU_GROUP_DIM)
    w_causal = w_s * jnp.tril(jnp.ones((SGU_CHUNK, SGU_CHUNK), w_s.dtype))
    sp = jnp.einsum('gts,bcsgd->bctgd', w_causal, vc) + b_s.T[None, None, :, :, None]
    return (u * sp.reshape(b, s, SGU_WIDTH)) @ w_out


def hybrid_layer(x, c, ada_w, ada_b, ln_g, ln_b, ffa_w_in, ffa_w_out, ffb_w_in, ffb_w_out, mixer):
    mod = jax.nn.silu(c) @ ada_w + ada_b
    sh1, sc1, g1, sh2, sc2, g2, sh3, sc3, g3 = [m[:, None, :] for m in jnp.split(mod, N_MOD, axis=-1)]
    f1 = 0.5 * swiglu(x * (1.0 + sc1) + sh1, ffa_w_in, ffa_w_out)
    x = layer_norm(DEEPNORM_ALPHA * x + (1.0 + g1) * f1, ln_g[0], ln_b[0])
    m = mixer(x * (1.0 + sc2) + sh2)
    x = layer_norm(DEEPNORM_ALPHA * x + (1.0 + g2) * m, ln_g[1], ln_b[1])
    f2 = 0.5 * swiglu(x * (1.0 + sc3) + sh3, ffb_w_in, ffb_w_out)
    return layer_norm(DEEPNORM_ALPHA * x + (1.0 + g3) * f2, ln_g[2], ln_b[2])


def setup_inputs(seed: int = 0) -> dict:
    key = jax.random.key(seed)
    keys = iter(jax.random.split(key, 64))
    f32 = jnp.float32

    def nrm(shape, scale):
        return jax.random.normal(next(keys), shape, f32) * scale

    def gain(shape):
        return 1.0 + nrm(shape, 0.02)

    def common(prefix):
        return {
            prefix + 'ada_w': nrm((D_MODEL, N_MOD * D_MODEL), 0.2 * D_MODEL ** -0.5),
            prefix + 'ada_b': nrm((N_MOD * D_MODEL,), 0.01),
            prefix + 'ln_g': gain((3, D_MODEL)),
            prefix + 'ln_b': nrm((3, D_MODEL), 0.01),
            prefix + 'ffa_w_in': nrm((D_MODEL, 2 * D_FF), D_MODEL ** -0.5),
            prefix + 'ffa_w_out': nrm((D_FF, D_MODEL), D_FF ** -0.5 * DEEPNORM_BETA),
            prefix + 'ffb_w_in': nrm((D_MODEL, 2 * D_FF), D_MODEL ** -0.5),
            prefix + 'ffb_w_out': nrm((D_FF, D_MODEL), D_FF ** -0.5 * DEEPNORM_BETA),
        }

    out = {'x': nrm((BATCH, SEQ, D_MODEL), 1.0), 'c': nrm((BATCH, D_MODEL), 1.0)}
    out.update(common('l0_'))
    dt0 = jnp.exp(jax.random.uniform(next(keys), (SSD_HEADS,), f32)
                  * (math.log(0.1) - math.log(0.001)) + math.log(0.001))
    out.update({
        'l0_w_in': nrm((D_MODEL, EVEN_IN), D_MODEL ** -0.5),
        'l0_conv_w': nrm((SSD_CONV, SSD_CONV_DIM), SSD_CONV ** -0.5),
        'l0_conv_b': nrm((SSD_CONV_DIM,), 0.01),
        'l0_dt_bias': dt0 + jnp.log(-jnp.expm1(-dt0)),
        'l0_a_log': jnp.log(jax.random.uniform(next(keys), (SSD_HEADS,), f32, 1.0, 16.0)),
        'l0_d_skip': 1.0 + nrm((SSD_HEADS,), 0.1),
        'l0_ssd_norm_w': gain((SSD_D_INNER,)),
        'l0_q_norm_w': gain((MLA_Q_RANK,)),
        'l0_w_uq': nrm((MLA_Q_RANK, MLA_HEADS * (MLA_NOPE + MLA_ROPE)), MLA_Q_RANK ** -0.5),
        'l0_kv_norm_w': gain((MLA_KV_RANK,)),
        'l0_w_ukv': nrm((MLA_KV_RANK, MLA_HEADS * (MLA_NOPE + MLA_V)), MLA_KV_RANK ** -0.5),
        'l0_w_out': nrm((EVEN_MIX, D_MODEL), EVEN_MIX ** -0.5 * DEEPNORM_BETA),
    })
    out.update(common('l1_'))
    out.update({
        'l1_w_uv': nrm((D_MODEL, 2 * SGU_WIDTH), D_MODEL ** -0.5),
        'l1_b_uv': nrm((2 * SGU_WIDTH,), 0.01),
        'l1_sgu_ln_g': gain((SGU_WIDTH,)),
        'l1_sgu_ln_b': nrm((SGU_WIDTH,), 0.01),
        'l1_w_s': nrm((SGU_GROUPS, SGU_CHUNK, SGU_CHUNK), SGU_CHUNK ** -0.5),
        'l1_b_s': 1.0 + nrm((SGU_GROUPS, SGU_CHUNK), 0.1),
        'l1_w_out': nrm((SGU_WIDTH, D_MODEL), SGU_WIDTH ** -0.5 * DEEPNORM_BETA),
    })
    return out


def reference(x, c, l0_ada_w, l0_ada_b, l0_ln_g, l0_ln_b, l0_ffa_w_in, l0_ffa_w_out,
              l0_ffb_w_in, l0_ffb_w_out, l0_w_in, l0_conv_w, l0_conv_b, l0_dt_bias,
              l0_a_log, l0_d_skip, l0_ssd_norm_w, l0_q_norm_w, l0_w_uq, l0_kv_norm_w,
              l0_w_ukv, l0_w_out, l1_ada_w, l1_ada_b, l1_ln_g, l1_ln_b, l1_ffa_w_in,
              l1_ffa_w_out, l1_ffb_w_in, l1_ffb_w_out, l1_w_uv, l1_b_uv, l1_sgu_ln_g,
              l1_sgu_ln_b, l1_w_s, l1_b_s, l1_w_out):
    cos, sin = rope_tables(x.shape[1])
    commons = [
        (l0_ada_w, l0_ada_b, l0_ln_g, l0_ln_b, l0_ffa_w_in, l0_ffa_w_out, l0_ffb_w_in, l0_ffb_w_out),
        (l1_ada_w, l1_ada_b, l1_ln_g, l1_ln_b, l1_ffa_w_in, l1_ffa_w_out, l1_ffb_w_in, l1_ffb_w_out),
    ]
    mixers = [
        functools.partial(even_mixer, w_in=l0_w_in, conv_w=l0_conv_w, conv_b=l0_conv_b,
                          dt_bias=l0_dt_bias, a_log=l0_a_log, d_skip=l0_d_skip,
                          ssd_norm_w=l0_ssd_norm_w, q_norm_w=l0_q_norm_w, w_uq=l0_w_uq,
                          kv_norm_w=l0_kv_norm_w, w_ukv=l0_w_ukv, w_out=l0_w_out,
                          cos=cos, sin=sin),
        functools.partial(odd_mixer, w_uv=l1_w_uv, b_uv=l1_b_uv, sgu_ln_g=l1_sgu_ln_g,
                          sgu_ln_b=l1_sgu_ln_b, w_s=l1_w_s, b_s=l1_b_s, w_out=l1_w_out),
    ]
    for layer in range(DEPTH):
        x = hybrid_layer(x, c, *commons[layer], mixers[layer])
    return x
```

```python
import math
import numpy as np
from contextlib import ExitStack, contextmanager
import concourse.bass as bass
import concourse.mybir as mybir
from concourse.bass_utils import run_bass_kernel_spmd

F32 = mybir.dt.float32
BF16 = mybir.dt.bfloat16
AF = mybir.ActivationFunctionType
ALU = mybir.AluOpType
AX = mybir.AxisListType

ENGS = ("tensor", "vector", "scalar", "gpsimd", "sync")
N_DMA_SEMS = 24
import os
SSD_CUT = int(os.environ['SSD_CUT']) if 'SSD_CUT' in os.environ else None
SSD_SUB = int(os.environ['SSD_SUB']) if 'SSD_SUB' in os.environ else None

D = 1024
S = 2048
DFF = 2816
NFF = 22
ALPHA = 4.0 ** 0.25
EPS = 1e-5
EPS_LN = EPS / (ALPHA * ALPHA)


class _Op:
    __slots__ = ("eng", "fn", "deps", "dma", "idx", "needed", "semval", "dsem", "epoch")


class Builder:
    def __init__(self):
        self.nc = bass.Bass("TRN2", target_bir_lowering=False)
        self.es = ExitStack()
        self.ops = []
        self.st = {}
        self.dma_last = [None] * N_DMA_SEMS
        self.dma_cnt = [0] * N_DMA_SEMS
        self.dma_rr = [0, 0]
        self.out_dmas = []
        self.last_on = {e: None for e in ENGS}
        self.bar = {}
        self.epoch = 0
        self.pes = None
        self.exclusive = set()

    def sbuf(self, name, shape, dtype, persistent=False):
        es = self.es if (persistent or self.pes is None) else self.pes
        if es is not self.es:
            name = "%s_p%d" % (name, self.epoch)
        return es.enter_context(self.nc.sbuf_tensor(name, list(shape), dtype))

    def psum(self, name, shape, dtype=F32, persistent=False):
        es = self.es if (persistent or self.pes is None) else self.pes
        return es.enter_context(self.nc.psum_tensor(name, list(shape), dtype))

    def dram(self, name, shape, dtype, kind="Internal"):
        return self.nc.dram_tensor(name, list(shape), dtype, kind=kind)

    @contextmanager
    def phase(self):
        self.pes = ExitStack()
        try:
            yield
        finally:
            self.pes.close()
            self.pes = None
            self.barrier()

    def barrier(self):
        snap = set()
        for e in ENGS:
            if self.last_on[e] is not None:
                snap.add(self.last_on[e])
        for s in range(N_DMA_SEMS):
            if self.dma_last[s] is not None:
                snap.add(self.dma_last[s])
        self.bar = {e: set(snap) for e in ENGS}
        self.st = {}
        self.epoch += 1

    @staticmethod
    def _norm(a):
        if isinstance(a, tuple):
            b, k = a
        else:
            b, k = a, None
        if not isinstance(b, str):
            b = b.name
        return b, k

    def _entries(self, b, k):
        d = self.st.setdefault(b, {})
        if k is None:
            return list(d.keys())
        out = []
        if k in d:
            out.append(k)
        if None in d:
            out.append(None)
        return out

    def add(self, eng, fn, reads=(), writes=(), dma=False, out=False):
        op = _Op()
        op.eng, op.fn, op.dma = eng, fn, dma
        op.idx = len(self.ops)
        op.needed = False
        op.semval = None
        op.dsem = None
        op.epoch = self.epoch
        deps = set()
        if self.bar.get(eng):
            deps |= self.bar.pop(eng)
        reads = [self._norm(a) for a in reads]
        writes = [self._norm(a) for a in writes]
        for (bb, kk) in list(reads):
            if bb in self.exclusive and (bb, kk) not in writes:
                writes.append((bb, kk))
        for b, k in reads:
            d = self.st.setdefault(b, {})
            for kk in self._entries(b, k):
                w = d[kk][0]
                if w is not None:
                    deps.add(w)
        for b, k in writes:
            d = self.st.setdefault(b, {})
            for kk in self._entries(b, k):
                w, rs = d[kk]
                if w is not None:
                    deps.add(w)
                deps.update(rs)
        for b, k in reads:
            d = self.st[b]
            if k not in d:
                d[k] = [None, []]
                if k is not None and None in d:
                    d[k][0] = d[None][0]
            d[k][1].append(op.idx)
        for b, k in writes:
            d = self.st[b]
            if k is None:
                d.clear()
                d[None] = [op.idx, []]
            else:
                d[k] = [op.idx, []]
        if dma:
            half = N_DMA_SEMS // 2
            pool = 1 if eng == "gpsimd" else 0
            s = pool * half + self.dma_rr[pool]
            self.dma_rr[pool] = (self.dma_rr[pool] + 1) % half
            if self.dma_last[s] is not None:
                deps.add(self.dma_last[s])
            self.dma_last[s] = op.idx
            self.dma_cnt[s] += 1
            op.dsem = s
            op.semval = 16 * self.dma_cnt[s]
            if out:
                self.out_dmas.append(op.idx)
        deps.discard(op.idx)
        op.deps = deps
        self.ops.append(op)
        self.last_on[eng] = op.idx
        return op.idx

    def finish(self):
        nc = self.nc
        last = _Op()
        last.eng, last.fn, last.dma = "sync", (lambda e: e.nop()), False
        last.idx = len(self.ops)
        last.needed = False
        last.semval = None
        last.dsem = None
        last.epoch = self.epoch
        last.deps = set(self.out_dmas)
        self.ops.append(last)
        ops = self.ops
        for op in ops:
            nd = set()
            for j in op.deps:
                pj = ops[j]
                if (not pj.dma) and pj.eng == op.eng and (not op.dma) and op.eng == "tensor":
                    continue
                nd.add(j)
            op.deps = nd
            for j in nd:
                ops[j].needed = True
        cnt = {}
        for op in ops:
            if op.dma:
                continue
            if op.needed:
                key = (op.eng, op.epoch)
                cnt[key] = cnt.get(key, 0) + 1
                op.semval = cnt[key]
        esem = {}
        for key in cnt:
            esem[key] = self.es.enter_context(nc.semaphore("s_%s_%d" % key))
        dsem = [self.es.enter_context(nc.semaphore("d_%d" % i)) for i in range(N_DMA_SEMS)]
        self.n_sems = len(esem) + N_DMA_SEMS

        def emit_engine(ename):
            def body(eng):
                waited = {}
                for op in ops:
                    if op.eng != ename:
                        continue
                    need = {}
                    for j in op.deps:
                        pj = ops[j]
                        if pj.dma:
                            key = ("d", pj.dsem)
                            sem = dsem[pj.dsem]
                        else:
                            key = (pj.eng, pj.epoch)
                            sem = esem[key]
                        v = pj.semval
                        if waited.get(key, 0) >= v:
                            continue
                        if key not in need or need[key][1] < v:
                            need[key] = (sem, v)
                    for key, (sem, v) in need.items():
                        eng.wait_ge(sem, v)
                        waited[key] = v
                    ins = op.fn(eng)
                    if op.dma:
                        ins.then_inc(dsem[op.dsem], 16)
                    elif op.needed:
                        ins.then_inc(esem[(op.eng, op.epoch)], 1)
            return body

        with nc.Block() as block:
            block.tensor(emit_engine("tensor"))
            block.vector(emit_engine("vector"))
            block.scalar(emit_engine("scalar"))
            block.gpsimd(emit_engine("gpsimd"))
            block.sync(emit_engine("sync"))
        self.es.close()
        return nc


class Prog:
    pass


def tiles_of(k, n0, n1):
    return [(k, n) for n in range(n0, n1)]


def ada_phase(b, P, L):
    adaw = P.adaw[L]
    vec = P.vec[L]
    with b.phase():
        scb = b.sbuf("ada_sc", [128, 8], BF16)
        b.add("scalar", lambda e: e.activation(out=scb[:], in_=P.cT[:], func=AF.Silu),
              reads=[P.cT], writes=[scb])
        wb = [b.sbuf("ada_w%d" % i, [128, 9216], BF16) for i in range(3)]
        ps = P.PSA

        def load(k):
            t = wb[k % 3]
            b.add("gpsimd", lambda e: e.dma_start(out=t[:], in_=adaw[k]), reads=[], writes=[t], dma=True)
        for k in range(3):
            load(k)
        zt = b.sbuf("ada_zero", [128, 128], BF16)
        b.add("vector", lambda e: e.memset(zt[:], 0.0), reads=[], writes=[zt])
        b.add("tensor", lambda e: e.matmul(ps[:, 0, 0:72], lhsT=zt[:], rhs=zt[:, 0:72], start=True, stop=False),
              reads=[zt], writes=[(ps, 0)])
        for k in range(8):
            t = wb[k % 3]
            for j in range(72):
                b.add("tensor",
                      lambda e, t=t, j=j, k=k: e.matmul(ps[:, 0, j:j + 1], lhsT=t[:, j * 128:(j + 1) * 128],
                                                        rhs=scb[:, k:k + 1], start=False, stop=(k == 7 and j == 71)),
                      reads=[t, scb], writes=[(ps, 0)])
            if k + 3 < 8:
                load(k + 3)
        b.add("vector", lambda e: e.tensor_tensor(out=P.modv[:], in0=ps[:, 0, 0:72], in1=vec[:, 0:72], op=ALU.add),
              reads=[(ps, 0), vec], writes=[P.modv])
        derive_vecs(b, P)


def derive_vecs(b, P):
    if True:
        for i in range(3):
            coef = (1.0 / ALPHA) if i == 1 else (0.5 / ALPHA)
            b.add("vector", lambda e, i=i: e.tensor_scalar(out=P.dvA[:, i * 8:(i + 1) * 8],
                                                           in0=P.modv[:, (3 * i + 1) * 8:(3 * i + 2) * 8],
                                                           scalar1=1.0, scalar2=None, op0=ALU.add),
                  reads=[P.modv], writes=[(P.dvA, i)])
            b.add("vector", lambda e, i=i, coef=coef: e.tensor_scalar(out=P.dvG[:, i * 8:(i + 1) * 8],
                                                                      in0=P.modv[:, (3 * i + 2) * 8:(3 * i + 3) * 8],
                                                                      scalar1=1.0, scalar2=coef, op0=ALU.add, op1=ALU.mult),
                  reads=[P.modv], writes=[(P.dvG, i)])


def make_h(b, P, i, hT, t0, nt, eng="scalar"):
    for k in range(8):
        rd = [(P.xT, (k, n)) for n in range(t0 // 512, (t0 + nt + 511) // 512)]
        b.add("scalar",
              lambda e, k=k: e.activation(out=hT[:, k, 0:nt], in_=P.xT[:, k, t0:t0 + nt], func=AF.Identity,
                                          scale=P.dvA[:, i * 8 + k:i * 8 + k + 1],
                                          bias=P.modv[:, (3 * i) * 8 + k:(3 * i) * 8 + k + 1]),
              reads=rd + [(P.dvA, i), P.modv], writes=[(hT, k)])


class LNState:
    def __init__(self, b, P, nbuf=3, nt=2, nset=1):
        self.nbuf, self.nt = nbuf, nt
        self.sets = []
        self.ybf = [b.sbuf("ln_ybf%d" % i, [128, 512], BF16) for i in range(nbuf)]
        self.ysq = [b.sbuf("ln_ysq%d" % i, [128, 512], BF16) for i in range(nbuf)]
        for q in range(nset):
            self.sets.append((b.sbuf("ln_mean%d" % q, [128, 512], F32), b.sbuf("ln_msq%d" % q, [128, 512], F32),
                              b.sbuf("ln_rstd%d" % q, [128, 512], F32)))
        self.t1 = [b.sbuf("ln_t1_%d" % i, [128, 512], F32) for i in range(nt)]
        self.t2 = [b.sbuf("ln_t2_%d" % i, [128, 512], F32) for i in range(nt)]
        self.cnt = 0


def residual_evac(b, P, ln, i, pso, psk, j, n, s1, s2):
    c = ln.cnt
    ln.cnt += 1
    ybf = ln.ybf[c % ln.nbuf]
    ysq = ln.ysq[c % ln.nbuf]
    sl = slice(n * 512, (n + 1) * 512)
    b.add("vector",
          lambda e: e.scalar_tensor_tensor(out=P.xT[:, j, sl], in0=pso[:, psk, :], scalar=P.dvG[:, i * 8 + j:i * 8 + j + 1],
                                           in1=P.xT[:, j, sl], op0=ALU.mult, op1=ALU.add),
          reads=[(pso, psk), (P.dvG, i), (P.xT, (j, n))], writes=[(P.xT, (j, n))])
    b.add("scalar", lambda e: e.activation(out=ybf[:], in_=P.xT[:, j, sl], func=AF.Identity),
          reads=[(P.xT, (j, n))], writes=[ybf])
    b.add("scalar", lambda e: e.activation(out=ysq[:], in_=P.xT[:, j, sl], func=AF.Square),
          reads=[(P.xT, (j, n))], writes=[ysq])
    b.add("tensor", lambda e: e.matmul(s1[0][:, s1[1], :], lhsT=P.onesb[:], rhs=ybf[:], start=(j == 0), stop=(j == 7)),
          reads=[ybf, P.onesb], writes=[s1])
    b.add("tensor", lambda e: e.matmul(s2[0][:, s2[1], :], lhsT=P.onesb[:], rhs=ysq[:], start=(j == 0), stop=(j == 7)),
          reads=[ysq, P.onesb], writes=[s2])


def ln_finish(b, P, ln, L, i, n, s1, s2, sidx=0, defer=False):
    sl = slice(n * 512, (n + 1) * 512)
    vec = P.vec[L]
    gcol = 72 + i * 8
    bcol = 72 + 24 + i * 8
    mean, msq, rstd = ln.sets[sidx]
    b.add("scalar", lambda e: e.activation(out=mean[:], in_=s1[0][:, s1[1], :], func=AF.Identity),
          reads=[s1], writes=[mean])
    b.add("scalar", lambda e: e.activation(out=msq[:], in_=s1[0][:, s1[1], :], func=AF.Square),
          reads=[s1], writes=[msq])
    b.add("vector", lambda e: e.tensor_tensor(out=msq[:], in0=s2[0][:, s2[1], :], in1=msq[:], op=ALU.subtract),
          reads=[s2, msq], writes=[msq])
    specs = []
    specs.append(("scalar", lambda e: e.activation(out=rstd[:], in_=msq[:], func=AF.Sqrt, bias=P.epsln[:, 0:1]),
                  [msq, P.epsln], [rstd]))
    specs.append(("vector", lambda e: e.reciprocal(out=rstd[:], in_=rstd[:]), [rstd], [rstd]))
    for k in range(8):
        t1 = ln.t1[k % ln.nt]
        t2 = ln.t2[k % ln.nt]
        specs.append(("gpsimd", lambda e, k=k, t1=t1: e.tensor_tensor(out=t1[:], in0=P.xT[:, k, sl], in1=mean[:], op=ALU.subtract),
                      [(P.xT, (k, n)), mean], [t1]))
        specs.append(("vector", lambda e, k=k, t1=t1, t2=t2: e.scalar_tensor_tensor(out=t2[:], in0=t1[:], scalar=vec[:, gcol + k:gcol + k + 1],
                                                                                 in1=rstd[:], op0=ALU.mult, op1=ALU.mult),
                      [t1, rstd, vec], [t2]))
        specs.append(("scalar", lambda e, k=k, t2=t2: e.activation(out=P.xT[:, k, sl], in_=t2[:], func=AF.Identity,
                                                               bias=vec[:, bcol + k:bcol + k + 1]),
                      [t2, vec], [(P.xT, (k, n))]))
    if defer:
        return specs
    emit_specs(b, specs)
    return []


def emit_specs(b, specs, n=None):
    k = len(specs) if n is None else min(n, len(specs))
    for _ in range(k):
        eng, fn, rd, wr = specs.pop(0)
        b.add(eng, fn, reads=rd, writes=wr)


def ffn_phase(b, P, L, i, win, wout):
    with b.phase():
        hT = b.sbuf("ffn_hT", [128, 8, 1024], BF16)
        gT = b.sbuf("ffn_gT", [128, NFF, 1024], BF16)
        wib = [b.sbuf("ffn_wi%d" % r, [128, 2, 8, 128], BF16) for r in range(3)]
        wob = [b.sbuf("ffn_wo%d" % r, [128, NFF, 128], BF16) for r in range(3)]
        sa = [b.sbuf("ffn_sa%d" % r, [128, 512], F32) for r in range(2)]
        ln = LNState(b, P, nset=2)
        PSA, PSB = P.PSA, P.PSB
        pend = []

        def load_wi(m):
            t = wib[m % 3]
            b.add("gpsimd", lambda e: e.dma_start(out=t[:].rearrange("p a k c -> p (a k c)"), in_=win[m]),
                  reads=[], writes=[t], dma=True)

        def load_wo(j):
            t = wob[j % 3]
            b.add("gpsimd", lambda e: e.dma_start(out=t[:].rearrange("p m c -> p (m c)"), in_=wout[j]),
                  reads=[], writes=[t], dma=True)

        for hf in range(2):
            T0 = hf * 1024
            for m in range(3):
                load_wi(m)
            make_h(b, P, i, hT, T0, 1024)
            cnt = 0
            for m in range(NFF):
                w = wib[m % 3]
                for n in range(2):
                    ka = cnt % 2
                    cnt += 1
                    for ab in range(2):
                        bank = 2 * ka + ab
                        for k in range(8):
                            b.add("tensor",
                                  lambda e, w=w, ab=ab, k=k, bank=bank, n=n: e.matmul(
                                      PSB[:, bank, :], lhsT=w[:, ab, k, :], rhs=hT[:, k, n * 512:(n + 1) * 512],
                                      start=(k == 0), stop=(k == 7)),
                                  reads=[w, (hT, k)], writes=[(PSB, bank)])
                    s = sa[ka]
                    b.add("scalar", lambda e, s=s, ka=ka: e.activation(out=s[:], in_=PSB[:, 2 * ka, :], func=AF.Silu),
                          reads=[(PSB, 2 * ka)], writes=[s])
                    b.add("vector",
                          lambda e, s=s, ka=ka, m=m, n=n: e.tensor_tensor(out=gT[:, m, n * 512:(n + 1) * 512], in0=PSB[:, 2 * ka + 1, :],
                                                                            in1=s[:], op=ALU.mult),
                          reads=[(PSB, 2 * ka + 1), s], writes=[(gT, (m, n))])
                    emit_specs(b, pend, 2)
                if m + 3 < NFF:
                    load_wi(m + 3)
            emit_specs(b, pend)
            for j in range(3):
                load_wo(j)
            cnt = 0
            for j in range(8):
                w = wob[j % 3]
                for n in range(2):
                    ko = cnt % 2
                    cnt += 1
                    for m in range(NFF):
                        b.add("tensor",
                              lambda e, w=w, m=m, n=n, ko=ko: e.matmul(PSA[:, ko, :], lhsT=w[:, m, :],
                                                                       rhs=gT[:, m, n * 512:(n + 1) * 512],
                                                                       start=(m == 0), stop=(m == NFF - 1)),
                              reads=[w, (gT, (m, n))], writes=[(PSA, ko)])
                    st = (PSA, PSB)[n]
                    residual_evac(b, P, ln, i, PSA, ko, j, hf * 2 + n, (st, 2), (st, 3))
                if j + 3 < 8:
                    load_wo(j + 3)
            for n in range(2):
                st = (PSA, PSB)[n]
                pend += ln_finish(b, P, ln, L, i, hf * 2 + n, (st, 2), (st, 3), sidx=n, defer=(hf == 0))


def sgu_phase(b, P, L, W):
    vec = P.vec[L]
    BU, LNG2, LNB2 = 120, 136, 152
    with b.phase():
        hT = b.sbuf("sgs_hT", [128, 8, 512], BF16)
        Wv = b.sbuf("sgs_Wv", [128, 8, 2048], BF16)
        tri = b.sbuf("sgs_tri", [128, 128], F32)
        WcT = b.sbuf("sgs_WcT", [128, 2048], BF16)
        Bt = b.sbuf("sgs_Bt", [128, 2048], F32)
        grep = b.sbuf("sgs_grep", [128, 2048], F32)
        ones1 = b.sbuf("sgs_ones1", [128, 128], BF16)
        orow = b.sbuf("sgs_orow", [1, 128], BF16)
        bvrow = b.sbuf("sgs_bvrow", [1, 2048], BF16)
        uT = b.sbuf("sgs_uT", [128, 16, 512], BF16)
        PT = b.sbuf("sgs_PT", [128, 16, 512], BF16)
        vg = b.sbuf("sgs_vg", [128, 2048], F32)
        vn = b.sbuf("sgs_vn", [128, 2048], BF16)
        vsq = vn
        wsf = vg
        tt = b.sbuf("sgs_tt", [128, 1024], F32)
        st = b.sbuf("sgs_st", [128, 8], F32)
        wub = [b.sbuf("sgs_wu%d" % r, [128, 8, 128], BF16) for r in range(2)]
        wob = [b.sbuf("sgs_wo%d" % r, [128, 16, 128], BF16) for r in range(2)]
        ln = LNState(b, P, nbuf=2, nt=1)
        PSA, PSB = P.PSA, P.PSB
        pend = []
        for k in range(8):
            b.add("gpsimd", lambda e, k=k: e.dma_start(out=Wv[:, k, :], in_=W["wv"][k]), reads=[], writes=[(Wv, k)], dma=True)
        b.add("sync", lambda e: e.dma_start(out=wsf[:], in_=W["wsT"][:]), reads=[], writes=[wsf], dma=True)
        b.add("sync", lambda e: e.dma_start(out=tri[:], in_=P.cst_d[:, 128:256]), reads=[], writes=[tri], dma=True)
        b.add("sync", lambda e: e.dma_start(out=grep[:], in_=W["grep"][:]), reads=[], writes=[grep], dma=True)
        b.add("sync", lambda e: e.dma_start(out=Bt[:], in_=W["bsb"][:]), reads=[], writes=[Bt], dma=True)
        b.add("gpsimd", lambda e: e.dma_start(out=bvrow[:], in_=W["bvrow"][:]), reads=[], writes=[bvrow], dma=True)
        b.add("vector", lambda e: e.memset(ones1[:], 1.0), reads=[], writes=[ones1])
        b.add("vector", lambda e: e.memset(orow[:], 1.0), reads=[], writes=[orow])
        b.add("vector", lambda e: e.tensor_tensor(out=WcT[:].rearrange("p (g t) -> p g t", g=16),
                                                  in0=wsf[:].rearrange("p (g t) -> p g t", g=16),
                                                  in1=tri[:].unsqueeze(1).to_broadcast([128, 16, 128]), op=ALU.mult),
              reads=[wsf, tri], writes=[WcT])
        for q in range(4):
            b.add("tensor", lambda e, q=q: e.matmul(PSA[:, q, :], lhsT=ones1[:], rhs=WcT[:, q * 512:(q + 1) * 512], start=True, stop=True),
                  reads=[ones1, WcT], writes=[(PSA, q)])
        b.add("vector", lambda e: e.tensor_tensor(out=vg[:].rearrange("p (g t) -> p g t", g=16),
                                                  in0=PSA[:].rearrange("p q (g t) -> p (q g) t", g=4),
                                                  in1=vec[:, LNB2:LNB2 + 16].unsqueeze(2).to_broadcast([128, 16, 128]), op=ALU.mult),
              reads=[PSA, vec], writes=[vg])
        b.add("vector", lambda e: e.tensor_tensor(out=Bt[:], in0=Bt[:], in1=vg[:], op=ALU.add), reads=[Bt, vg], writes=[Bt])

        def load_wu(ii):
            t = wub[ii % 2]
            b.add("gpsimd", lambda e: e.dma_start(out=t[:].rearrange("p k c -> p (k c)"), in_=W["wu"][ii % 16]),
                  reads=[], writes=[t], dma=True)

        def load_wo(jj):
            t = wob[jj % 2]
            b.add("gpsimd", lambda e: e.dma_start(out=t[:].rearrange("p i c -> p (i c)"), in_=W["wo"][jj % 8]),
                  reads=[], writes=[t], dma=True)

        for G in range(4):
            make_h(b, P, 1, hT, G * 512, 512)
            for ii in range(2):
                load_wu(G * 16 + ii)
            for i in range(16):
                w = wub[(G * 16 + i) % 2]
                bk = i % 2
                for k in range(8):
                    b.add("tensor", lambda e, w=w, k=k, bk=bk: e.matmul(PSB[:, bk, :], lhsT=w[:, k, :], rhs=hT[:, k, :],
                                                                       start=(k == 0), stop=(k == 7)),
                          reads=[w, (hT, k)], writes=[(PSB, bk)])
                b.add("scalar", lambda e, i=i, bk=bk: e.activation(out=uT[:, i, :], in_=PSB[:, bk, :], func=AF.Gelu,
                                                                   bias=vec[:, BU + i:BU + i + 1]),
                      reads=[(PSB, bk), vec], writes=[(uT, i)])
                emit_specs(b, pend, 2)
                if i + 2 < 16:
                    load_wu(G * 16 + i + 2)
            emit_specs(b, pend)
            for j in range(2):
                load_wo(G * 8 + j)
            for cc in range(4):
                tk = slice(cc * 128, (cc + 1) * 128)
                for nb in range(4):
                    for k in range(8):
                        b.add("tensor", lambda e, nb=nb, k=k, tk=tk: e.matmul(PSA[:, nb, :], lhsT=hT[:, k, tk], rhs=Wv[:, k, nb * 512:(nb + 1) * 512],
                                                                             start=(k == 0), stop=False),
                              reads=[(hT, k), (Wv, k)], writes=[(PSA, nb)])
                    b.add("tensor", lambda e, nb=nb: e.matmul(PSA[:, nb, :], lhsT=orow[0:1, :], rhs=bvrow[0:1, nb * 512:(nb + 1) * 512],
                                                             start=False, stop=True),
                          reads=[orow, bvrow], writes=[(PSA, nb)])
                b.add("scalar", lambda e: e.activation(out=vg[:], in_=PSA[:].rearrange("p q n -> p (q n)"), func=AF.Gelu),
                      reads=[PSA], writes=[vg])
                b.add("vector", lambda e: e.reduce_sum(out=st[:, 0:1], in_=vg[:], axis=AX.X), reads=[vg], writes=[(st, 0)])
                b.add("scalar", lambda e: e.activation(out=vsq[:], in_=vg[:], func=AF.Square), reads=[vg], writes=[vsq])
                b.add("vector", lambda e: e.reduce_sum(out=st[:, 1:2], in_=vsq[:], axis=AX.X), reads=[vsq], writes=[(st, 1)])
                b.add("vector", lambda e: e.tensor_scalar(out=st[:, 2:3], in0=st[:, 0:1], scalar1=1.0 / 2048.0, scalar2=None, op0=ALU.mult),
                      reads=[(st, 0)], writes=[(st, 2)])
                b.add("vector", lambda e: e.tensor_tensor(out=st[:, 3:4], in0=st[:, 2:3], in1=st[:, 2:3], op=ALU.mult),
                      reads=[(st, 2)], writes=[(st, 3)])
                b.add("vector", lambda e: e.scalar_tensor_tensor(out=st[:, 4:5], in0=st[:, 1:2], scalar=1.0 / 2048.0, in1=st[:, 3:4],
                                                                 op0=ALU.mult, op1=ALU.subtract),
                      reads=[(st, 1), (st, 3)], writes=[(st, 4)])
                b.add("scalar", lambda e: e.activation(out=st[:, 5:6], in_=st[:, 4:5], func=AF.Sqrt, bias=P.eps5[:, 0:1]),
                      reads=[(st, 4), P.eps5], writes=[(st, 5)])
                b.add("vector", lambda e: e.reciprocal(out=st[:, 6:7], in_=st[:, 5:6]), reads=[(st, 5)], writes=[(st, 6)])
                b.add("vector", lambda e: e.scalar_tensor_tensor(out=st[:, 7:8], in0=st[:, 2:3], scalar=-1.0, in1=st[:, 6:7],
                                                                 op0=ALU.mult, op1=ALU.mult),
                      reads=[(st, 2), (st, 6)], writes=[(st, 7)])
                b.add("scalar", lambda e: e.activation(out=vn[:], in_=vg[:], func=AF.Identity, scale=st[:, 6:7], bias=st[:, 7:8]),
                      reads=[vg, (st, 6), (st, 7)], writes=[vn])
                for hh in range(2):
                    for g8 in range(8):
                        g = hh * 8 + g8
                        col = g8 * 128
                        b.add("tensor", lambda e, g=g, hh=hh, col=col: e.matmul(
                            PSB[:, 2 * hh + col // 512, (col % 512):(col % 512) + 128], lhsT=vn[:, g * 128:(g + 1) * 128],
                            rhs=WcT[:, g * 128:(g + 1) * 128], start=True, stop=True),
                              reads=[vn, WcT], writes=[(PSB, 2 * hh + col // 512)])
                    hs = slice(hh * 1024, (hh + 1) * 1024)
                    b.add("vector", lambda e, hh=hh, hs=hs: e.tensor_tensor(out=tt[:], in0=PSB[:, 2 * hh:2 * hh + 2, :].rearrange("p q n -> p (q n)"),
                                                                          in1=grep[:, hs], op=ALU.mult),
                          reads=[(PSB, 2 * hh), (PSB, 2 * hh + 1), grep], writes=[tt])
                    b.add("gpsimd", lambda e, hs=hs: e.tensor_tensor(out=tt[:], in0=tt[:], in1=Bt[:, hs], op=ALU.add),
                          reads=[tt, Bt], writes=[tt])
                    b.add("vector", lambda e, hh=hh, tk=tk: e.tensor_tensor(out=PT[:, hh * 8:(hh + 1) * 8, tk],
                                                                          in0=tt[:].rearrange("p (g t) -> p g t", g=8),
                                                                          in1=uT[:, hh * 8:(hh + 1) * 8, tk], op=ALU.mult),
                          reads=[tt] + [(uT, hh * 8 + q) for q in range(8)], writes=[(PT, (hh, cc))])
            for j in range(8):
                w = wob[(G * 8 + j) % 2]
                ko = j % 2
                for i in range(16):
                    b.add("tensor", lambda e, w=w, i=i, ko=ko: e.matmul(PSB[:, ko, :], lhsT=w[:, i, :], rhs=PT[:, i, :],
                                                                       start=(i == 0), stop=(i == 15)),
                          reads=[w] + [(PT, (i // 8, q)) for q in range(4)], writes=[(PSB, ko)])
                residual_evac(b, P, ln, 1, PSB, ko, j, G, (PSB, 2), (PSB, 3))
                if j + 2 < 8:
                    load_wo(G * 8 + j + 2)
            ln_finish(b, P, ln, L, 1, G, (PSB, 2), (PSB, 3))


def ssd_phase(b, P, W):
    vec = P.vec[0]
    CW, CB, NW = 120, 168, 180
    with b.phase():
        hT = b.sbuf("sd_hT", [128, 8, 512], BF16)
        Wx = b.sbuf("sd_Wx", [128, 8, 1536], BF16)
        Wz = b.sbuf("sd_Wz", [128, 8, 1024], BF16)
        Wd = b.sbuf("sd_Wd", [128, 8, 16], BF16)
        ident = b.sbuf("sd_ident", [128, 128], BF16)
        trib = b.sbuf("sd_trib", [128, 128], BF16)
        maskb = b.sbuf("sd_maskb", [128, 512], BF16)
        onesb1 = b.sbuf("sd_onesb1", [128, 128], BF16)
        adth = b.sbuf("sd_adth", [128, 16], BF16)
        adtl = b.sbuf("sd_adtl", [128, 16], BF16)
        RH = b.sbuf("sd_RH", [128, 2048], BF16)
        RL = b.sbuf("sd_RL", [128, 2048], BF16)
        one1 = b.sbuf("sd_one1", [128, 1], F32)
        tokc = b.sbuf("sd_tokc", [128, 1056], F32)
        halo = b.sbuf("sd_halo", [128, 12, 3], F32)
        xr = [b.sbuf("sd_xr%d" % i, [128, 515], F32) for i in range(2)]
        acc = [b.sbuf("sd_acc%d" % i, [128, 512], F32) for i in range(2)]
        xbcs = b.sbuf("sd_xbcs", [128, 12, 512], BF16)
        small = b.sbuf("sd_small", [128, 16 * 13], F32)
        Dm = b.sbuf("sd_Dm", [128, 2048], F32)
        MT = b.sbuf("sd_MT", [128, 2048], BF16)
        cbT = b.sbuf("sd_cbT", [128, 256], F32)
        zs = b.sbuf("sd_zs", [128, 1024], F32)
        xst = b.sbuf("sd_xst", [128, 1024], BF16)
        Btok = b.sbuf("sd_Btok", [128, 256], BF16)
        y1 = b.sbuf("sd_y1", [128, 1024], F32)
        tmp = b.sbuf("sd_tmp", [128, 1024], F32)
        xcd = b.sbuf("sd_xcd", [128, 1024], BF16)
        state = b.sbuf("sd_state", [128, 1024], F32)
        statebf = b.sbuf("sd_statebf", [128, 1024], BF16)
        yn = b.sbuf("sd_yn", [128, 1024], BF16)
        ysT = b.sbuf("sd_ysT", [128, 8, 512], BF16)
        PSA, PSB = P.PSA, P.PSB
        R3 = PSA[:].rearrange("p q (h l) -> p (q h) l", l=128)

        def sm(i, n=16):
            return small[:, i * 16:i * 16 + n]

        lim = [None]

        def A(*a, **k):
            if lim[0] is None:
                return b.add(*a, **k)
            if lim[0] > 0:
                lim[0] -= 1
                return b.add(*a, **k)
            return None

        def smk(i):
            return (small, i)

        for k in range(8):
            b.add("gpsimd", lambda e, k=k: e.dma_start(out=Wx[:, k, :], in_=W["wxbc"][k]), reads=[], writes=[(Wx, k)], dma=True)
            b.add("gpsimd", lambda e, k=k: e.dma_start(out=Wz[:, k, :], in_=W["wz"][k]), reads=[], writes=[(Wz, k)], dma=True)
            b.add("gpsimd", lambda e, k=k: e.dma_start(out=Wd[:, k, :], in_=W["wdt"][k]), reads=[], writes=[(Wd, k)], dma=True)
        b.add("gpsimd", lambda e: e.dma_start(out=ident[:], in_=P.cst_d[:, 0:128]), reads=[], writes=[ident], dma=True)
        b.add("gpsimd", lambda e: e.dma_start(out=trib[:], in_=P.cst_d[:, 128:256]), reads=[], writes=[trib], dma=True)
        for q in range(4):
            b.add("gpsimd", lambda e, q=q: e.dma_start(out=maskb[:, q * 128:(q + 1) * 128], in_=P.cst_d[:, 256:384]),
                  reads=[], writes=[(maskb, q)], dma=True)
        b.add("sync", lambda e: e.dma_start(out=tokc[:], in_=W["tokc"][:]), reads=[], writes=[tokc], dma=True)
        b.add("vector", lambda e: e.memset(onesb1[:], 1.0), reads=[], writes=[onesb1])
        if SSD_CUT is not None:
            b.add("vector", lambda e: e.memset(ysT[:], 0.0), reads=[], writes=[ysT])
        b.add("vector", lambda e: e.memset(one1[:], 1.0), reads=[], writes=[one1])
        b.add("vector", lambda e: e.memset(state[:], 0.0), reads=[], writes=[state])
        b.add("vector", lambda e: e.memset(statebf[:], 0.0), reads=[], writes=[statebf])
        b.add("scalar", lambda e: e.activation(out=sm(11), in_=tokc[:, 1040:1056], func=AF.Exp), reads=[tokc], writes=[smk(11)])
        b.add("vector", lambda e: e.tensor_scalar(out=sm(11), in0=sm(11), scalar1=-1.0, scalar2=None, op0=ALU.mult),
              reads=[smk(11)], writes=[smk(11)])

        for G in range(4):
            make_h(b, P, 1, hT, G * 512, 512)
            for q in range(12):
                bk = q % 2
                for k in range(8):
                    b.add("tensor", lambda e, q=q, k=k, bk=bk: e.matmul(PSB[:, bk, :], lhsT=Wx[:, k, q * 128:(q + 1) * 128], rhs=hT[:, k, :],
                                                                       start=(k == 0), stop=(k == 7)),
                          reads=[(Wx, k), (hT, k)], writes=[(PSB, bk)])
                x_r = xr[q % 2]
                a = acc[q % 2]
                if G == 0:
                    b.add("vector", lambda e, x_r=x_r: e.memset(x_r[:, 0:3], 0.0), reads=[], writes=[(x_r, "h")])
                else:
                    b.add("vector", lambda e, x_r=x_r, q=q: e.tensor_copy(out=x_r[:, 0:3], in_=halo[:, q, :]),
                          reads=[(halo, q)], writes=[(x_r, "h")])
                b.add("scalar", lambda e, x_r=x_r, bk=bk: e.activation(out=x_r[:, 3:515], in_=PSB[:, bk, :], func=AF.Identity),
                      reads=[(PSB, bk)], writes=[(x_r, "b")])
                if G < 3:
                    b.add("vector", lambda e, x_r=x_r, q=q: e.tensor_copy(out=halo[:, q, :], in_=x_r[:, 512:515]),
                          reads=[(x_r, "b")], writes=[(halo, q)])
                b.add("vector", lambda e, x_r=x_r, a=a, q=q: e.tensor_scalar(out=a[:], in0=x_r[:, 0:512], scalar1=vec[:, CW + q:CW + q + 1],
                                                                             scalar2=None, op0=ALU.mult),
                      reads=[(x_r, "h"), (x_r, "b"), vec], writes=[a])
                for kk in range(1, 4):
                    b.add("vector", lambda e, x_r=x_r, a=a, q=q, kk=kk: e.scalar_tensor_tensor(
                        out=a[:], in0=x_r[:, kk:kk + 512], scalar=vec[:, CW + kk * 12 + q:CW + kk * 12 + q + 1], in1=a[:],
                        op0=ALU.mult, op1=ALU.add),
                          reads=[(x_r, "h"), (x_r, "b"), vec, a], writes=[a])
                b.add("scalar", lambda e, a=a, q=q: e.activation(out=xbcs[:, q, :], in_=a[:], func=AF.Silu, bias=vec[:, CB + q:CB + q + 1]),
                      reads=[a, vec], writes=[(xbcs, q)])
            for cc in range(4):
                tk = slice(cc * 128, (cc + 1) * 128)
                if SSD_CUT is not None and SSD_CUT < 1:
                    continue
                for nb in range(2):
                    for k in range(8):
                        b.add("tensor", lambda e, nb=nb, k=k, tk=tk: e.matmul(PSB[:, 1 + nb, :], lhsT=hT[:, k, tk], rhs=Wz[:, k, nb * 512:(nb + 1) * 512],
                                                                             start=(k == 0), stop=(k == 7)),
                              reads=[(hT, k), (Wz, k)], writes=[(PSB, 1 + nb)])
                b.add("scalar", lambda e: e.activation(out=zs[:], in_=PSB[:, 1:3, :].rearrange("p q n -> p (q n)"), func=AF.Silu),
                      reads=[(PSB, 1), (PSB, 2)], writes=[zs])
                if SSD_CUT is not None and SSD_CUT < 2:
                    continue
                for k in range(8):
                    b.add("tensor", lambda e, k=k, tk=tk: e.matmul(PSB[:, 0, 0:16], lhsT=hT[:, k, tk], rhs=Wd[:, k, :], start=(k == 0), stop=(k == 7)),
                          reads=[(hT, k), (Wd, k)], writes=[(PSB, 0)])
                b.add("vector", lambda e: e.tensor_tensor(out=sm(0), in0=PSB[:, 0, 0:16], in1=tokc[:, 1024:1040], op=ALU.add),
                      reads=[(PSB, 0), tokc], writes=[smk(0)])
                b.add("scalar", lambda e: e.activation(out=sm(1), in_=sm(0), func=AF.Abs), reads=[smk(0)], writes=[smk(1)])
                b.add("scalar", lambda e: e.activation(out=sm(2), in_=sm(1), func=AF.Exp, scale=-1.0), reads=[smk(1)], writes=[smk(2)])
                b.add("scalar", lambda e: e.activation(out=sm(3), in_=sm(2), func=AF.Ln, bias=one1[:, 0:1]), reads=[smk(2), one1], writes=[smk(3)])
                b.add("vector", lambda e: e.scalar_tensor_tensor(out=sm(4), in0=sm(0), scalar=0.0, in1=sm(3), op0=ALU.max, op1=ALU.add),
                      reads=[smk(0), smk(3)], writes=[smk(4)])
                b.add("scalar", lambda e: e.activation(out=sm(5), in_=sm(4), func=AF.Ln), reads=[smk(4)], writes=[smk(5)])
                b.add("vector", lambda e: e.tensor_tensor(out=sm(6), in0=sm(4), in1=sm(11), op=ALU.mult), reads=[smk(4), smk(11)], writes=[smk(6)])
                if SSD_CUT is not None and SSD_CUT < 3:
                    continue
                lim[0] = SSD_SUB
                A("scalar", lambda e: e.activation(out=adth[:], in_=sm(6), func=AF.Identity), reads=[smk(6)], writes=[adth])
                A("vector", lambda e: e.tensor_tensor(out=adtl[:], in0=sm(6), in1=adth[:], op=ALU.subtract), reads=[smk(6), adth], writes=[adtl])
                for (src, dst) in ((adth, RH), (adtl, RL)):
                    A("vector", lambda e, src=src, dst=dst: e.tensor_tensor(out=dst[:].rearrange("p (h l) -> p h l", l=128),
                                                                              in0=src[:].unsqueeze(2).to_broadcast([128, 16, 128]),
                                                                              in1=trib[:].unsqueeze(1).to_broadcast([128, 16, 128]), op=ALU.mult),
                          reads=[src, trib], writes=[dst])
                for q4 in range(4):
                    A("tensor", lambda e, q4=q4: e.matmul(PSA[:, q4, :], lhsT=onesb1[:], rhs=RH[:, q4 * 512:(q4 + 1) * 512], start=True, stop=False),
                          reads=[onesb1, RH], writes=[(PSA, q4)])
                    A("tensor", lambda e, q4=q4: e.matmul(PSA[:, q4, :], lhsT=onesb1[:], rhs=RL[:, q4 * 512:(q4 + 1) * 512], start=False, stop=False),
                          reads=[onesb1, RL], writes=[(PSA, q4)])
                    A("tensor", lambda e, q4=q4: e.matmul(PSA[:, q4, :], lhsT=ident[:], rhs=maskb[:], start=False, stop=True),
                          reads=[ident, maskb], writes=[(PSA, q4)])
                A("tensor", lambda e: e.matmul(PSB[:, 0, 16:32], lhsT=trib[:], rhs=adth[:], start=True, stop=False),
                      reads=[trib, adth], writes=[(PSB, 0)])
                A("tensor", lambda e: e.matmul(PSB[:, 0, 16:32], lhsT=trib[:], rhs=adtl[:], start=False, stop=True),
                      reads=[trib, adtl], writes=[(PSB, 0)])
                A("vector", lambda e: e.tensor_tensor(out=sm(7), in0=PSB[:, 0, 16:32], in1=sm(5), op=ALU.subtract),
                      reads=[(PSB, 0), smk(5)], writes=[smk(7)])
                A("scalar", lambda e: e.activation(out=sm(8), in_=PSB[:, 0, 16:32], func=AF.Exp), reads=[(PSB, 0)], writes=[smk(8)])
                A("vector", lambda e: e.tensor_tensor(out=sm(9), in0=R3[:, :, 127], in1=sm(7), op=ALU.subtract),
                      reads=[PSA, smk(7)], writes=[smk(9)])
                A("scalar", lambda e: e.activation(out=sm(9), in_=sm(9), func=AF.Exp), reads=[smk(9)], writes=[smk(9)])
                A("scalar", lambda e: e.activation(out=sm(10), in_=R3[:, :, 127], func=AF.Exp), reads=[PSA], writes=[smk(10)])
                A("vector", lambda e: e.tensor_tensor(out=Dm[:].rearrange("p (h l) -> p h l", l=128), in0=R3,
                                                          in1=sm(7).unsqueeze(2).to_broadcast([128, 16, 128]), op=ALU.subtract),
                      reads=[PSA, smk(7)], writes=[Dm])
                A("scalar", lambda e: e.activation(out=Dm[:], in_=Dm[:], func=AF.Exp), reads=[Dm], writes=[Dm])
                if SSD_CUT is not None and SSD_CUT < 5:
                    continue
                for g in range(2):
                    b.add("tensor", lambda e, g=g, tk=tk: e.matmul(PSB[:, 0, 128 + g * 128:256 + g * 128], lhsT=xbcs[:, 8 + g, tk], rhs=xbcs[:, 10 + g, tk],
                                                                 start=True, stop=True),
                          reads=[(xbcs, 8 + g), (xbcs, 10 + g)], writes=[(PSB, 0)])
                b.add("scalar", lambda e: e.activation(out=cbT[:], in_=PSB[:, 0, 128:384], func=AF.Identity), reads=[(PSB, 0)], writes=[cbT])
                b.add("vector", lambda e: e.tensor_tensor(out=MT[:].rearrange("p (g r l) -> p g r l", g=2, r=8),
                                                          in0=Dm[:].rearrange("p (g r l) -> p g r l", g=2, r=8),
                                                          in1=cbT[:].rearrange("p (g l) -> p g l", g=2).unsqueeze(2).to_broadcast([128, 2, 8, 128]),
                                                          op=ALU.mult),
                      reads=[Dm, cbT], writes=[MT])
                if SSD_CUT is not None and SSD_CUT < 6:
                    continue
                for q in range(8):
                    b.add("tensor", lambda e, q=q, tk=tk: e.matmul(PSB[:, 1 + q // 4, (q % 4) * 128:(q % 4) * 128 + 128], lhsT=xbcs[:, q, tk], rhs=ident[:],
                                                                 start=True, stop=True),
                          reads=[(xbcs, q), ident], writes=[(PSB, 1 + q // 4)])
                b.add("scalar", lambda e: e.activation(out=xst[:], in_=PSB[:, 1:3, :].rearrange("p q n -> p (q n)"), func=AF.Identity),
                      reads=[(PSB, 1), (PSB, 2)], writes=[xst])
                for g in range(2):
                    b.add("tensor", lambda e, g=g, tk=tk: e.matmul(PSB[:, 3, g * 128:(g + 1) * 128], lhsT=xbcs[:, 8 + g, tk], rhs=ident[:], start=True, stop=True),
                          reads=[(xbcs, 8 + g), ident], writes=[(PSB, 3)])
                b.add("scalar", lambda e: e.activation(out=Btok[:], in_=PSB[:, 3, 0:256], func=AF.Identity), reads=[(PSB, 3)], writes=[Btok])
                if SSD_CUT is not None and SSD_CUT < 7:
                    continue
                for h in range(16):
                    b.add("tensor", lambda e, h=h: e.matmul(PSA[:, h // 8, (h % 8) * 64:(h % 8) * 64 + 64], lhsT=MT[:, h * 128:(h + 1) * 128],
                                                           rhs=xst[:, h * 64:(h + 1) * 64], start=True, stop=True),
                          reads=[MT, xst], writes=[(PSA, h // 8)])
                for g in range(2):
                    b.add("tensor", lambda e, g=g, tk=tk: e.matmul(PSA[:, 2 + g, :], lhsT=xbcs[:, 10 + g, tk], rhs=statebf[:, g * 512:(g + 1) * 512],
                                                                 start=True, stop=True),
                          reads=[(xbcs, 10 + g), statebf], writes=[(PSA, 2 + g)])
                b.add("vector", lambda e: e.tensor_tensor(out=y1[:].rearrange("p (h d) -> p h d", d=64),
                                                          in0=PSA[:, 2:4, :].rearrange("p q (h d) -> p (q h) d", d=64),
                                                          in1=sm(8).unsqueeze(2).to_broadcast([128, 16, 64]), op=ALU.mult),
                      reads=[(PSA, 2), (PSA, 3), smk(8)], writes=[y1])
                b.add("vector", lambda e: e.tensor_tensor(out=y1[:], in0=y1[:], in1=PSA[:, 0:2, :].rearrange("p q n -> p (q n)"), op=ALU.add),
                      reads=[y1, (PSA, 0), (PSA, 1)], writes=[y1])
                b.add("vector", lambda e: e.tensor_tensor(out=tmp[:], in0=xst[:], in1=tokc[:, 0:1024], op=ALU.mult), reads=[xst, tokc], writes=[tmp])
                b.add("vector", lambda e: e.tensor_tensor(out=y1[:], in0=y1[:], in1=tmp[:], op=ALU.add), reads=[y1, tmp], writes=[y1])
                b.add("vector", lambda e: e.tensor_tensor(out=y1[:], in0=y1[:], in1=zs[:], op=ALU.mult), reads=[y1, zs], writes=[y1])
                if SSD_CUT is not None and SSD_CUT < 8:
                    continue
                b.add("vector", lambda e: e.tensor_tensor(out=xcd[:].rearrange("p (h d) -> p h d", d=64), in0=xst[:].rearrange("p (h d) -> p h d", d=64),
                                                          in1=sm(9).unsqueeze(2).to_broadcast([128, 16, 64]), op=ALU.mult),
                      reads=[xst, smk(9)], writes=[xcd])
                for g in range(2):
                    b.add("tensor", lambda e, g=g: e.matmul(PSB[:, 1 + g, :], lhsT=Btok[:, g * 128:(g + 1) * 128], rhs=xcd[:, g * 512:(g + 1) * 512],
                                                           start=True, stop=True),
                          reads=[Btok, xcd], writes=[(PSB, 1 + g)])
                b.add("vector", lambda e: e.tensor_tensor(out=state[:].rearrange("p (h d) -> p h d", d=64), in0=state[:].rearrange("p (h d) -> p h d", d=64),
                                                          in1=sm(10).unsqueeze(2).to_broadcast([128, 16, 64]), op=ALU.mult),
                      reads=[state, smk(10)], writes=[state])
                b.add("vector", lambda e: e.tensor_tensor(out=state[:], in0=state[:], in1=PSB[:, 1:3, :].rearrange("p q n -> p (q n)"), op=ALU.add),
                      reads=[state, (PSB, 1), (PSB, 2)], writes=[state])
                b.add("scalar", lambda e: e.activation(out=statebf[:], in_=state[:], func=AF.Identity), reads=[state], writes=[statebf])
                if SSD_CUT is not None and SSD_CUT < 9:
                    continue
                b.add("scalar", lambda e: e.activation(out=tmp[:], in_=y1[:], func=AF.Square), reads=[y1], writes=[tmp])
                for g in range(2):
                    b.add("vector", lambda e, g=g: e.reduce_sum(out=small[:, 192 + g:193 + g], in_=tmp[:, g * 512:(g + 1) * 512], axis=AX.X),
                          reads=[tmp], writes=[(small, 12)])
                b.add("scalar", lambda e: e.activation(out=small[:, 194:196], in_=small[:, 192:194], func=AF.Sqrt, scale=1.0 / 512.0, bias=P.eps5[:, 0:1]),
                      reads=[(small, 12), P.eps5], writes=[(small, 12)])
                b.add("vector", lambda e: e.reciprocal(out=small[:, 196:198], in_=small[:, 194:196]), reads=[(small, 12)], writes=[(small, 12)])
                for g in range(2):
                    b.add("scalar", lambda e, g=g: e.activation(out=yn[:, g * 512:(g + 1) * 512], in_=y1[:, g * 512:(g + 1) * 512], func=AF.Identity,
                                                                scale=small[:, 196 + g:197 + g]),
                          reads=[y1, (small, 12)], writes=[(yn, g)])
                for q in range(8):
                    b.add("tensor", lambda e, q=q: e.matmul(PSB[:, 1 + q // 4, (q % 4) * 128:(q % 4) * 128 + 128], lhsT=yn[:, q * 128:(q + 1) * 128], rhs=ident[:],
                                                           start=True, stop=True),
                          reads=[(yn, q // 4), ident], writes=[(PSB, 1 + q // 4)])
                b.add("vector", lambda e, tk=tk: e.tensor_tensor(out=ysT[:, :, tk], in0=PSB[:, 1:3, :].rearrange("p q (c t) -> p (q c) t", t=128),
                                                               in1=vec[:, NW:NW + 8].unsqueeze(2).to_broadcast([128, 8, 128]), op=ALU.mult),
                      reads=[(PSB, 1), (PSB, 2), vec], writes=[(ysT, cc)])
            for q in range(8):
                b.add("sync", lambda e, q=q, G=G: e.dma_start(out=P.ymix[q][:, G * 512:(G + 1) * 512], in_=ysT[:, q, :]),
                      reads=[(ysT, c4) for c4 in range(4)], writes=[("ymix", (q, G))], dma=True)


QSCALE = 192.0 ** -0.5


def mla1_phase(b, P, W):
    vec = P.vec[0]
    QW, KW = 188, 191
    with b.phase():
        hT = b.sbuf("m1_hT", [128, 8, 512], BF16)
        Wl = b.sbuf("m1_Wl", [128, 8, 704], BF16)
        rope = b.sbuf("m1_rope", [64, 2, S], F32)
        Rm = b.sbuf("m1_Rm", [64, 64], BF16)
        onq = b.sbuf("m1_onq", [128, 128], BF16)
        onk = b.sbuf("m1_onk", [128, 128], BF16)
        lat = b.sbuf("m1_lat", [128, 5, 512], F32)
        sq = b.sbuf("m1_sq", [128, 5, 512], BF16)
        krf = b.sbuf("m1_krf", [64, 512], F32)
        krh = b.sbuf("m1_krh", [64, 512], BF16)
        krl = b.sbuf("m1_krl", [64, 512], BF16)
        t1 = b.sbuf("m1_t1", [64, 512], F32)
        t2 = b.sbuf("m1_t2", [64, 512], F32)
        rs = b.sbuf("m1_rs", [128, 2, 512], F32)
        outn = b.sbuf("m1_outn", [128, 5, 512], BF16)
        kpo = b.sbuf("m1_kpo", [64, 512], BF16)
        PSA, PSB = P.PSA, P.PSB
        for k in range(8):
            b.add("gpsimd", lambda e, k=k: e.dma_start(out=Wl[:, k, :], in_=W["wlat"][k]), reads=[], writes=[(Wl, k)], dma=True)
        b.add("sync", lambda e: e.dma_start(out=rope[:], in_=W["rope"][:]), reads=[], writes=[rope], dma=True)
        b.add("gpsimd", lambda e: e.dma_start(out=Rm[:], in_=P.cst_d[0:64, 384:448]), reads=[], writes=[Rm], dma=True)
        b.add("vector", lambda e: e.memset(onq[:], 1.0 / 384.0), reads=[], writes=[onq])
        b.add("vector", lambda e: e.memset(onk[:], 1.0 / 256.0), reads=[], writes=[onk])
        for G in range(4):
            ts = slice(G * 512, (G + 1) * 512)
            make_h(b, P, 1, hT, G * 512, 512)
            for c in range(5):
                bk = c % 2
                for k in range(8):
                    b.add("tensor", lambda e, c=c, k=k, bk=bk: e.matmul(PSB[:, bk, :], lhsT=Wl[:, k, c * 128:(c + 1) * 128], rhs=hT[:, k, :],
                                                                       start=(k == 0), stop=(k == 7)),
                          reads=[(Wl, k), (hT, k)], writes=[(PSB, bk)])
                b.add("scalar", lambda e, c=c, bk=bk: e.activation(out=lat[:, c, :], in_=PSB[:, bk, :], func=AF.Identity),
                      reads=[(PSB, bk)], writes=[(lat, c)])
                b.add("scalar", lambda e, c=c: e.activation(out=sq[:, c, :], in_=lat[:, c, :], func=AF.Square),
                      reads=[(lat, c)], writes=[(sq, c)])
            for k in range(8):
                b.add("tensor", lambda e, k=k: e.matmul(PSB[0:64, 2, :], lhsT=Wl[:, k, 640:704], rhs=hT[:, k, :], start=(k == 0), stop=(k == 7)),
                      reads=[(Wl, k), (hT, k)], writes=[(PSB, 2)])
            b.add("scalar", lambda e: e.activation(out=krf[:], in_=PSB[0:64, 2, :], func=AF.Identity), reads=[(PSB, 2)], writes=[krf])
            for (c0, nchunk, on, bank, wcol, r) in ((0, 3, onq, 0, QW, 0), (3, 2, onk, 1, KW, 1)):
                for c in range(nchunk):
                    b.add("tensor", lambda e, c=c, c0=c0, on=on, bank=bank, nchunk=nchunk: e.matmul(PSA[:, bank, :], lhsT=on[:], rhs=sq[:, c0 + c, :],
                                                                                                 start=(c == 0), stop=(c == nchunk - 1)),
                          reads=[on, (sq, c0 + c)], writes=[(PSA, bank)])
                b.add("scalar", lambda e, bank=bank, r=r: e.activation(out=rs[:, r, :], in_=PSA[:, bank, :], func=AF.Sqrt, bias=P.eps5[:, 0:1]),
                      reads=[(PSA, bank), P.eps5], writes=[(rs, r)])
                b.add("vector", lambda e, r=r: e.reciprocal(out=rs[:, r, :], in_=rs[:, r, :]), reads=[(rs, r)], writes=[(rs, r)])
                for c in range(nchunk):
                    b.add("vector", lambda e, c=c, c0=c0, wcol=wcol, r=r: e.scalar_tensor_tensor(
                        out=outn[:, c0 + c, :], in0=lat[:, c0 + c, :], scalar=vec[:, wcol + c:wcol + c + 1], in1=rs[:, r, :],
                        op0=ALU.mult, op1=ALU.mult),
                          reads=[(lat, c0 + c), vec, (rs, r)], writes=[(outn, c0 + c)])
            b.add("scalar", lambda e: e.activation(out=krh[:], in_=krf[:], func=AF.Identity), reads=[krf], writes=[krh])
            b.add("vector", lambda e: e.tensor_tensor(out=krl[:], in0=krf[:], in1=krh[:], op=ALU.subtract), reads=[krf, krh], writes=[krl])
            b.add("tensor", lambda e: e.matmul(PSA[0:64, 2, :], lhsT=Rm[:], rhs=krh[:], start=True, stop=False), reads=[Rm, krh], writes=[(PSA, 2)])
            b.add("tensor", lambda e: e.matmul(PSA[0:64, 2, :], lhsT=Rm[:], rhs=krl[:], start=False, stop=True), reads=[Rm, krl], writes=[(PSA, 2)])
            b.add("vector", lambda e, ts=ts: e.tensor_tensor(out=t1[:], in0=krf[:], in1=rope[:, 0, ts], op=ALU.mult), reads=[krf, rope], writes=[t1])
            b.add("vector", lambda e, ts=ts: e.tensor_tensor(out=t2[:], in0=PSA[0:64, 2, :], in1=rope[:, 1, ts], op=ALU.mult), reads=[(PSA, 2), rope], writes=[t2])
            b.add("vector", lambda e: e.tensor_tensor(out=kpo[:], in0=t1[:], in1=t2[:], op=ALU.add), reads=[t1, t2], writes=[kpo])
            for c in range(5):
                b.add("sync", lambda e, c=c, ts=ts: e.dma_start(out=P.mlat[c][:, ts], in_=outn[:, c, :]), reads=[(outn, c)], writes=[("mlat", (c, G))], dma=True)
            b.add("sync", lambda e, ts=ts: e.dma_start(out=P.mlat[5][0:64, ts], in_=kpo[:]), reads=[kpo], writes=[("mlat", (5, G))], dma=True)


def mla2_phase(b, P, W):
    with b.phase():
        Wq = b.sbuf("m2_Wq", [128, 3, 1536], BF16)
        Wkv = b.sbuf("m2_Wkv", [128, 2, 2048], BF16)
        rope = b.sbuf("m2_rope", [64, 2, S], F32)
        Rm = b.sbuf("m2_Rm", [64, 64], BF16)
        trib = b.sbuf("m2_trib", [128, 128], BF16)
        ones1 = b.sbuf("m2_ones1", [128, 128], BF16)
        cqn = b.sbuf("m2_cqn", [128, 3, S], BF16)
        ckvn = b.sbuf("m2_ckvn", [128, 2, S], BF16)
        kpe = b.sbuf("m2_kpe", [65, S], BF16)
        sqkpe = b.sbuf("m2_sqkpe", [64, S], BF16)
        QnT = b.sbuf("m2_QnT", [128, S], BF16)
        QpT = b.sbuf("m2_QpT", [65, S], BF16)
        KnT = b.sbuf("m2_KnT", [128, S], BF16)
        V = b.sbuf("m2_V", [128, 16, 128], BF16)
        qn2s = b.sbuf("m2_qn2s", [65, S], F32)
        sqa = b.sbuf("m2_sqa", [128, 512], BF16)
        sqb = b.sbuf("m2_sqb", [64, 512], BF16)
        qpf = b.sbuf("m2_qpf", [64, 512], F32)
        qph = b.sbuf("m2_qph", [64, 512], BF16)
        qpl = b.sbuf("m2_qpl", [64, 512], BF16)
        t1 = b.sbuf("m2_t1", [64, 512], F32)
        t2 = b.sbuf("m2_t2", [64, 512], F32)
        kmx = b.sbuf("m2_kmx", [128, 8], F32)
        crow = b.sbuf("m2_crow", [65, 512], F32)
        PT = [b.sbuf("m2_PT%d" % r, [128, 512], BF16) for r in range(2)]
        rr = b.sbuf("m2_rr", [128, 128], F32)
        yatt = b.sbuf("m2_yatt", [128, S], BF16)
        PSA, PSB = P.PSA, P.PSB
        for c in range(3):
            b.add("gpsimd", lambda e, c=c: e.dma_start(out=Wq[:, c, :], in_=W["wuq"][c]), reads=[], writes=[(Wq, c)], dma=True)
            b.add("sync", lambda e, c=c: e.dma_start(out=cqn[:, c, :], in_=P.mlat[c][:, :]), reads=[("mlat", None)], writes=[(cqn, c)], dma=True)
        for c in range(2):
            b.add("gpsimd", lambda e, c=c: e.dma_start(out=Wkv[:, c, :], in_=W["wukv"][c]), reads=[], writes=[(Wkv, c)], dma=True)
            b.add("sync", lambda e, c=c: e.dma_start(out=ckvn[:, c, :], in_=P.mlat[3 + c][:, :]), reads=[("mlat", None)], writes=[(ckvn, c)], dma=True)
        b.add("sync", lambda e: e.dma_start(out=kpe[0:64, :], in_=P.mlat[5][0:64, :]), reads=[("mlat", None)], writes=[(kpe, 0)], dma=True)
        b.add("sync", lambda e: e.dma_start(out=rope[:], in_=W["rope"][:]), reads=[], writes=[rope], dma=True)
        b.add("gpsimd", lambda e: e.dma_start(out=Rm[:], in_=P.cst_d[0:64, 384:448]), reads=[], writes=[Rm], dma=True)
        b.add("gpsimd", lambda e: e.dma_start(out=trib[:], in_=P.cst_d[:, 128:256]), reads=[], writes=[trib], dma=True)
        b.add("vector", lambda e: e.memset(ones1[:], 1.0), reads=[], writes=[ones1])
        b.add("vector", lambda e: e.memset(kpe[64:65, :], 1.0), reads=[], writes=[(kpe, 1)])
        b.add("scalar", lambda e: e.activation(out=sqkpe[:], in_=kpe[0:64, :], func=AF.Square), reads=[(kpe, 0)], writes=[sqkpe])

        for h in range(8):
            for n in range(4):
                ts = slice(n * 512, (n + 1) * 512)
                for c in range(2):
                    b.add("tensor", lambda e, c=c, h=h, ts=ts: e.matmul(PSB[:, 0, :], lhsT=Wkv[:, c, h * 256:h * 256 + 128], rhs=ckvn[:, c, ts],
                                                                       start=(c == 0), stop=(c == 1)),
                          reads=[(Wkv, c), (ckvn, c)], writes=[(PSB, 0)])
                b.add("scalar", lambda e, ts=ts: e.activation(out=KnT[:, ts], in_=PSB[:, 0, :], func=AF.Identity), reads=[(PSB, 0)], writes=[(KnT, n)])
                b.add("scalar", lambda e, ts=ts: e.activation(out=sqa[:], in_=KnT[:, ts], func=AF.Square), reads=[(KnT, n)], writes=[sqa])
                b.add("tensor", lambda e: e.matmul(PSA[:, 3, :], lhsT=ones1[:], rhs=sqa[:], start=True, stop=False), reads=[ones1, sqa], writes=[(PSA, 3)])
                b.add("tensor", lambda e, ts=ts: e.matmul(PSA[:, 3, :], lhsT=ones1[0:64, :], rhs=sqkpe[:, ts], start=False, stop=True),
                      reads=[ones1, sqkpe], writes=[(PSA, 3)])
                b.add("vector", lambda e, n=n: e.reduce_max(out=kmx[:, n:n + 1], in_=PSA[:, 3, :], axis=AX.X), reads=[(PSA, 3)], writes=[(kmx, n)])
                for blk in range(4):
                    tb = slice(n * 512 + blk * 128, n * 512 + (blk + 1) * 128)
                    for c in range(2):
                        b.add("tensor", lambda e, c=c, h=h, tb=tb, blk=blk: e.matmul(PSA[:, 0, blk * 128:(blk + 1) * 128], lhsT=ckvn[:, c, tb],
                                                                                  rhs=Wkv[:, c, h * 256 + 128:h * 256 + 256], start=(c == 0), stop=(c == 1)),
                              reads=[(ckvn, c), (Wkv, c)], writes=[(PSA, 0)])
                b.add("scalar", lambda e, n=n: e.activation(out=V[:, n * 4:(n + 1) * 4, :], in_=PSA[:, 0, :].rearrange("p (b d) -> p b d", d=128), func=AF.Identity),
                      reads=[(PSA, 0)], writes=[(V, n)])
                for c in range(3):
                    b.add("tensor", lambda e, c=c, h=h, ts=ts: e.matmul(PSB[:, 1, :], lhsT=Wq[:, c, h * 192:h * 192 + 128], rhs=cqn[:, c, ts],
                                                                       start=(c == 0), stop=(c == 2)),
                          reads=[(Wq, c), (cqn, c)], writes=[(PSB, 1)])
                b.add("scalar", lambda e, ts=ts: e.activation(out=QnT[:, ts], in_=PSB[:, 1, :], func=AF.Identity, scale=QSCALE), reads=[(PSB, 1)], writes=[(QnT, n)])
                b.add("scalar", lambda e, ts=ts: e.activation(out=sqa[:], in_=QnT[:, ts], func=AF.Square), reads=[(QnT, n)], writes=[sqa])
                for c in range(3):
                    b.add("tensor", lambda e, c=c, h=h, ts=ts: e.matmul(PSB[0:64, 2, :], lhsT=Wq[:, c, h * 192 + 128:h * 192 + 192], rhs=cqn[:, c, ts],
                                                                       start=(c == 0), stop=(c == 2)),
                          reads=[(Wq, c), (cqn, c)], writes=[(PSB, 2)])
                b.add("scalar", lambda e: e.activation(out=qpf[:], in_=PSB[0:64, 2, :], func=AF.Identity, scale=QSCALE), reads=[(PSB, 2)], writes=[qpf])
                b.add("scalar", lambda e: e.activation(out=qph[:], in_=qpf[:], func=AF.Identity), reads=[qpf], writes=[qph])
                b.add("vector", lambda e: e.tensor_tensor(out=qpl[:], in0=qpf[:], in1=qph[:], op=ALU.subtract), reads=[qpf, qph], writes=[qpl])
                b.add("tensor", lambda e: e.matmul(PSA[0:64, 2, :], lhsT=Rm[:], rhs=qph[:], start=True, stop=False), reads=[Rm, qph], writes=[(PSA, 2)])
                b.add("tensor", lambda e: e.matmul(PSA[0:64, 2, :], lhsT=Rm[:], rhs=qpl[:], start=False, stop=True), reads=[Rm, qpl], writes=[(PSA, 2)])
                b.add("vector", lambda e, ts=ts: e.tensor_tensor(out=t1[:], in0=qpf[:], in1=rope[:, 0, ts], op=ALU.mult), reads=[qpf, rope], writes=[t1])
                b.add("vector", lambda e, ts=ts: e.tensor_tensor(out=t2[:], in0=PSA[0:64, 2, :], in1=rope[:, 1, ts], op=ALU.mult), reads=[(PSA, 2), rope], writes=[t2])
                b.add("vector", lambda e, ts=ts: e.tensor_tensor(out=QpT[0:64, ts], in0=t1[:], in1=t2[:], op=ALU.add), reads=[t1, t2], writes=[(QpT, n)])
                b.add("scalar", lambda e, ts=ts: e.activation(out=sqb[:], in_=QpT[0:64, ts], func=AF.Square), reads=[(QpT, n)], writes=[sqb])
                b.add("tensor", lambda e: e.matmul(PSA[:, 3, :], lhsT=ones1[:], rhs=sqa[:], start=True, stop=False), reads=[ones1, sqa], writes=[(PSA, 3)])
                b.add("tensor", lambda e: e.matmul(PSA[:, 3, :], lhsT=ones1[0:64, :], rhs=sqb[:], start=False, stop=True), reads=[ones1, sqb], writes=[(PSA, 3)])
                b.add("scalar", lambda e, ts=ts: e.activation(out=qn2s[64:65, ts], in_=PSA[64:65, 3, :], func=AF.Identity), reads=[(PSA, 3)], writes=[(qn2s, n)])
            b.add("vector", lambda e: e.reduce_max(out=kmx[:, 4:5], in_=kmx[:, 0:4], axis=AX.X), reads=[(kmx, n) for n in range(4)], writes=[(kmx, 4)])
            for n in range(4):
                ts = slice(n * 512, (n + 1) * 512)
                b.add("scalar", lambda e, ts=ts: e.activation(out=crow[64:65, :], in_=qn2s[64:65, ts], func=AF.Sqrt, scale=kmx[64:65, 4:5]),
                      reads=[(qn2s, n), (kmx, 4)], writes=[crow])
                b.add("vector", lambda e, ts=ts: e.tensor_scalar(out=QpT[64:65, ts], in0=crow[64:65, :], scalar1=-1.0, scalar2=None, op0=ALU.mult),
                      reads=[crow], writes=[(QpT, (n, "c"))])
            cnt = 0
            for i in range(16):
                qs = slice(i * 128, (i + 1) * 128)
                ab = 2 * (i % 2)
                for jg in range(0, i + 1, 4):
                    js = list(range(jg, min(jg + 4, i + 1)))
                    bk = cnt % 2
                    pt = PT[cnt % 2]
                    cnt += 1
                    for jj, j in enumerate(js):
                        ks = slice(j * 128, (j + 1) * 128)
                        b.add("tensor", lambda e, jj=jj, ks=ks, qs=qs, bk=bk: e.matmul(PSB[:, bk, jj * 128:(jj + 1) * 128], lhsT=KnT[:, ks], rhs=QnT[:, qs],
                                                                                    start=True, stop=False),
                              reads=[(KnT, j // 4), (QnT, i // 4)], writes=[(PSB, bk)])
                        b.add("tensor", lambda e, jj=jj, ks=ks, qs=qs, bk=bk: e.matmul(PSB[:, bk, jj * 128:(jj + 1) * 128], lhsT=kpe[0:65, ks], rhs=QpT[0:65, qs],
                                                                                    start=False, stop=True),
                              reads=[(kpe, 0), (kpe, 1), (QpT, i // 4), (QpT, (i // 4, "c"))], writes=[(PSB, bk)])
                    nb = len(js)
                    b.add("scalar", lambda e, pt=pt, bk=bk, nb=nb: e.activation(out=pt[:, 0:nb * 128], in_=PSB[:, bk, 0:nb * 128], func=AF.Exp),
                          reads=[(PSB, bk)], writes=[pt])
                    if js[-1] == i:
                        jj = len(js) - 1
                        b.add("vector", lambda e, pt=pt, jj=jj: e.tensor_tensor(out=pt[:, jj * 128:(jj + 1) * 128], in0=pt[:, jj * 128:(jj + 1) * 128],
                                                                              in1=trib[:], op=ALU.mult),
                              reads=[pt, trib], writes=[pt])
                    for jj, j in enumerate(js):
                        b.add("tensor", lambda e, pt=pt, jj=jj, j=j, i=i, ab=ab: e.matmul(PSA[:, ab, 0:128], lhsT=V[:, j, :], rhs=pt[:, jj * 128:(jj + 1) * 128],
                                                                                       start=(j == 0), stop=(j == i)),
                              reads=[(V, j // 4), pt], writes=[(PSA, ab)])
                        b.add("tensor", lambda e, pt=pt, jj=jj, j=j, i=i, ab=ab: e.matmul(PSA[:, ab + 1, 0:128], lhsT=ones1[:], rhs=pt[:, jj * 128:(jj + 1) * 128],
                                                                                       start=(j == 0), stop=(j == i)),
                              reads=[ones1, pt], writes=[(PSA, ab + 1)])
                b.add("vector", lambda e, ab=ab: e.reciprocal(out=rr[:], in_=PSA[:, ab + 1, 0:128]), reads=[(PSA, ab + 1)], writes=[rr])
                b.add("vector", lambda e, ab=ab, qs=qs: e.tensor_tensor(out=yatt[:, qs], in0=PSA[:, ab, 0:128], in1=rr[:], op=ALU.mult),
                      reads=[(PSA, ab), rr], writes=[(yatt, i)])
            b.add("sync", lambda e, h=h: e.dma_start(out=P.ymix[8 + h][:, :], in_=yatt[:]), reads=[yatt], writes=[("ymix", (8 + h, 0))], dma=True)


def mixout_phase(b, P, L, wo):
    with b.phase():
        ybs = [b.sbuf("mo_yb%d" % r, [128, 16, 512], BF16) for r in range(2)]
        wob = [b.sbuf("mo_wo%d" % r, [128, 16, 128], BF16) for r in range(3)]
        ln = LNState(b, P, nset=2)
        PSB = P.PSB
        pend = []

        def load_wo(jj):
            t = wob[jj % 3]
            b.add("gpsimd", lambda e: e.dma_start(out=t[:].rearrange("p i c -> p (i c)"), in_=wo[jj % 8]), reads=[], writes=[t], dma=True)

        for n in range(4):
            ts = slice(n * 512, (n + 1) * 512)
            yb = ybs[n % 2]
            for c in range(16):
                b.add("sync", lambda e, c=c, ts=ts, yb=yb: e.dma_start(out=yb[:, c, :], in_=P.ymix[c][:, ts]), reads=[("ymix", None)], writes=[(yb, c)], dma=True)
            for j in range(3):
                load_wo(n * 8 + j)
            for j in range(8):
                w = wob[(n * 8 + j) % 3]
                ko = j % 2
                for c in range(16):
                    b.add("tensor", lambda e, w=w, c=c, ko=ko, yb=yb: e.matmul(PSB[:, ko, :], lhsT=w[:, c, :], rhs=yb[:, c, :], start=(c == 0), stop=(c == 15)),
                          reads=[w, (yb, c)], writes=[(PSB, ko)])
                emit_specs(b, pend, 4)
                residual_evac(b, P, ln, 1, PSB, ko, j, n, (PSB, 2), (PSB, 3))
                if j + 3 < 8:
                    load_wo(n * 8 + j + 3)
            emit_specs(b, pend)
            pend += ln_finish(b, P, ln, L, 1, n, (PSB, 2), (PSB, 3), sidx=n % 2, defer=(n < 3))

def dump_ymix(b, P, q0):
    with b.phase():
        t = b.sbuf("dbg_y", [128, S], BF16)
        for q in range(8):
            b.add("sync", lambda e, q=q: e.dma_start(out=t[:], in_=P.ymix[q0 + q][:, :]), reads=[("ymix", None)], writes=[t], dma=True)
            b.add("vector", lambda e, q=q: e.tensor_copy(out=P.xT[:, q, :], in_=t[:]), reads=[t], writes=[(P.xT, (q, n)) for n in range(4)])


def build_program(stages=None, dbg_modv=False, dump_modv=False):
    if stages is None:
        stages = []
        for L in range(2):
            stages += [("ada", L), ("ffa", L), ("mix", L), ("ffb", L)]
    b = Builder()
    nc = b.nc
    P = Prog()
    xT_d = b.dram("xT", [128, 8, S], F32, kind="ExternalInput")
    cT_d = b.dram("cT", [128, 8], F32, kind="ExternalInput")
    P.cst_d = b.dram("cst", [128, 512], F32, kind="ExternalInput")
    P.adaw, P.vec_d = {}, {}
    ffw = {}
    mixw = {}
    for L in range(2):
        P.vec_d[L] = b.dram("vec%d" % L, [128, 256], F32, kind="ExternalInput")
        if ("ada", L) in stages:
            P.adaw[L] = b.dram("adaw%d" % L, [8, 128, 9216], F32, kind="ExternalInput")
        for nm in ("ffa", "ffb"):
            if (nm, L) in stages:
                ffw[(nm, L)] = (b.dram("%s_in%d" % (nm, L), [NFF, 128, 2048], F32, kind="ExternalInput"),
                                b.dram("%s_out%d" % (nm, L), [8, 128, DFF], F32, kind="ExternalInput"))
    if ("mix", 1) in stages:
        mixw[1] = {
            "wu": b.dram("sg_wu", [16, 128, 1024], F32, kind="ExternalInput"),
            "wv": b.dram("sg_wv", [8, 128, 2048], F32, kind="ExternalInput"),
            "wo": b.dram("sg_wo", [8, 128, 2048], F32, kind="ExternalInput"),
            "wsT": b.dram("sg_wsT", [128, 2048], F32, kind="ExternalInput"),
            "grep": b.dram("sg_grep", [128, 2048], F32, kind="ExternalInput"),
            "bsb": b.dram("sg_bsb", [128, 2048], F32, kind="ExternalInput"),
            "bvrow": b.dram("sg_bvrow", [1, 2048], F32, kind="ExternalInput"),
        }
    kinds0 = [k for (k, L) in stages if L == 0]
    if any(k in ("mix", "ssd", "mla") for k in kinds0):
        mixw[0] = {
            "wz": b.dram("ev_wz", [8, 128, 1024], F32, kind="ExternalInput"),
            "wxbc": b.dram("ev_wxbc", [8, 128, 1536], F32, kind="ExternalInput"),
            "wdt": b.dram("ev_wdt", [8, 128, 16], F32, kind="ExternalInput"),
            "tokc": b.dram("ev_tokc", [128, 1056], F32, kind="ExternalInput"),
            "wlat": b.dram("ev_wlat", [8, 128, 704], F32, kind="ExternalInput"),
            "wuq": b.dram("ev_wuq", [3, 128, 1536], F32, kind="ExternalInput"),
            "wukv": b.dram("ev_wukv", [2, 128, 2048], F32, kind="ExternalInput"),
            "rope": b.dram("ev_rope", [64, 2, S], F32, kind="ExternalInput"),
            "wo": b.dram("ev_wo", [8, 128, 2048], F32, kind="ExternalInput"),
        }
        P.ymix = b.dram("ymix", [16, 128, S], BF16, kind="Internal")
        P.mlat = b.dram("mlat", [6, 128, S], BF16, kind="Internal")
    if dbg_modv:
        modv_d = b.dram("modv_dbg", [128, 72], F32, kind="ExternalInput")
    y_d = b.dram("yT", [128, 8, S], F32, kind="ExternalOutput")

    P.xT = b.sbuf("xT_sb", [128, 8, S], F32, persistent=True)
    P.cT = b.sbuf("cT_sb", [128, 8], F32, persistent=True)
    P.vec = [b.sbuf("vec_sb%d" % L, [128, 256], F32, persistent=True) for L in range(2)]
    P.modv = b.sbuf("modv", [128, 72], F32, persistent=True)
    P.dvA = b.sbuf("dvA", [128, 24], F32, persistent=True)
    P.dvG = b.sbuf("dvG", [128, 24], F32, persistent=True)
    P.onesb = b.sbuf("onesb", [128, 128], BF16, persistent=True)
    P.epsln = b.sbuf("epsln", [128, 1], F32, persistent=True)
    P.eps5 = b.sbuf("eps5", [128, 1], F32, persistent=True)
    P.PSA = b.psum("PSA", [128, 4, 512], F32, persistent=True)
    P.PSB = b.psum("PSB", [128, 4, 512], F32, persistent=True)
    b.exclusive = {"PSA", "PSB"}

    for k in range(8):
        b.add("sync", lambda e, k=k: e.dma_start(out=P.xT[:, k, :], in_=xT_d[:, k, :]),
              reads=[], writes=[(P.xT, (k, n)) for n in range(4)], dma=True)
    b.add("sync", lambda e: e.dma_start(out=P.cT[:], in_=cT_d[:]), reads=[], writes=[P.cT], dma=True)
    for L in range(2):
        b.add("sync", lambda e, L=L: e.dma_start(out=P.vec[L][:], in_=P.vec_d[L][:]), reads=[], writes=[P.vec[L]], dma=True)
    b.add("vector", lambda e: e.memset(P.onesb[:], 1.0 / 1024.0), reads=[], writes=[P.onesb])
    b.add("vector", lambda e: e.memset(P.epsln[:], EPS_LN), reads=[], writes=[P.epsln])
    b.add("vector", lambda e: e.memset(P.eps5[:], EPS), reads=[], writes=[P.eps5])
    if dbg_modv:
        b.add("sync", lambda e: e.dma_start(out=P.modv[:], in_=modv_d[:]), reads=[], writes=[P.modv], dma=True)
        derive_vecs(b, P)
    b.barrier()

    for (kind, L) in stages:
        if kind == "ada":
            ada_phase(b, P, L)
        elif kind == "ffa":
            ffn_phase(b, P, L, 0, *ffw[("ffa", L)])
        elif kind == "ffb":
            ffn_phase(b, P, L, 2, *ffw[("ffb", L)])
        elif kind == "mix" and L == 1:
            sgu_phase(b, P, L, mixw[1])
        elif kind == "ssd":
            ssd_phase(b, P, mixw[0])
            dump_ymix(b, P, 0)
        elif kind == "mla":
            mla1_phase(b, P, mixw[0])
            mla2_phase(b, P, mixw[0])
            dump_ymix(b, P, 8)
        elif kind == "mix" and L == 0:
            ssd_phase(b, P, mixw[0])
            mla1_phase(b, P, mixw[0])
            mla2_phase(b, P, mixw[0])
            mixout_phase(b, P, 0, mixw[0]["wo"])


    if dump_modv:
        b.add("vector", lambda e: e.tensor_copy(out=P.xT[:, 0, 0:72], in_=P.modv[:]), reads=[P.modv], writes=[(P.xT, (0, 0))])
    for k in range(8):
        b.add("sync", lambda e, k=k: e.dma_start(out=y_d[:, k, :], in_=P.xT[:, k, :]),
              reads=[(P.xT, (k, n)) for n in range(4)], writes=[], dma=True, out=True)
    nc = b.finish()
    return nc, b


def _fm(v):
    v = np.asarray(v, dtype=np.float32)
    return np.ascontiguousarray(v.reshape(-1, 128).T)


def prep_shared(inp):
    sh = {}
    cst = np.zeros((128, 512), np.float32)
    cst[:, 0:128] = np.eye(128, dtype=np.float32)
    cst[:, 128:256] = np.triu(np.ones((128, 128), np.float32))
    cst[:, 256:384] = (1.0 - np.triu(np.ones((128, 128), np.float32))) * -30000.0
    for mm in range(32):
        cst[mm + 32, 384 + mm] = -1.0
        cst[mm, 384 + mm + 32] = 1.0
    sh["cst"] = cst
    for L in range(2):
        p = "l%d_" % L
        sh["adaw%d" % L] = np.ascontiguousarray(inp[p + "ada_w"].reshape(8, 128, 9216))
        vec = np.zeros((128, 256), np.float32)
        vec[:, 0:72] = _fm(inp[p + "ada_b"])
        vec[:, 72:96] = _fm(inp[p + "ln_g"].reshape(-1))
        vec[:, 96:120] = _fm(inp[p + "ln_b"].reshape(-1))
        sh["vec%d" % L] = vec
        for nm in ("ffa", "ffb"):
            w_in = inp[p + nm + "_w_in"]
            t = w_in.reshape(8, 128, 2, NFF, 128)
            sh[nm + "_in%d" % L] = np.ascontiguousarray(t.transpose(3, 1, 2, 0, 4).reshape(NFF, 128, 2048))
            w_out = inp[p + nm + "_w_out"]
            t = w_out.reshape(NFF, 128, 8, 128)
            sh[nm + "_out%d" % L] = np.ascontiguousarray(t.transpose(2, 1, 0, 3).reshape(8, 128, DFF))
    w_in = inp["l0_w_in"]
    sh["ev_wz"] = np.ascontiguousarray(w_in[:, 0:1024].reshape(8, 128, 1024))
    sh["ev_wxbc"] = np.ascontiguousarray(w_in[:, 1024:2560].reshape(8, 128, 1536))
    sh["ev_wdt"] = np.ascontiguousarray(w_in[:, 2560:2576].reshape(8, 128, 16))
    sh["ev_wlat"] = np.ascontiguousarray(w_in[:, 2576:3280].reshape(8, 128, 704))
    sh["ev_wuq"] = np.ascontiguousarray(inp["l0_w_uq"].reshape(3, 128, 1536))
    sh["ev_wukv"] = np.ascontiguousarray(inp["l0_w_ukv"].reshape(2, 128, 2048))
    sh["ev_wo"] = np.ascontiguousarray(inp["l0_w_out"].reshape(16, 128, 8, 128).transpose(2, 1, 0, 3).reshape(8, 128, 2048))
    inv = (1.0 / (np.float32(10000.0) ** (np.arange(0, 64, 2, dtype=np.float32) / np.float32(64.0)))).astype(np.float32)
    ang = np.arange(S, dtype=np.float32)[:, None] * inv[None, :]
    cosT = np.cos(ang).astype(np.float32).T
    sinT = np.sin(ang).astype(np.float32).T
    rope = np.zeros((64, 2, S), np.float32)
    rope[0:32, 0] = cosT
    rope[32:64, 0] = cosT
    rope[0:32, 1] = sinT
    rope[32:64, 1] = sinT
    sh["ev_rope"] = rope
    tokc = np.zeros((128, 1056), np.float32)
    tokc[:, 0:1024] = np.repeat(inp["l0_d_skip"], 64)[None, :]
    tokc[:, 1024:1040] = inp["l0_dt_bias"][None, :]
    tokc[:, 1040:1056] = inp["l0_a_log"][None, :]
    sh["ev_tokc"] = tokc
    v0 = sh["vec0"]
    v0[:, 120:168] = np.concatenate([_fm(inp["l0_conv_w"][k]) for k in range(4)], axis=1)
    v0[:, 168:180] = _fm(inp["l0_conv_b"])
    v0[:, 180:188] = _fm(inp["l0_ssd_norm_w"])
    v0[:, 188:191] = _fm(inp["l0_q_norm_w"])
    v0[:, 191:193] = _fm(inp["l0_kv_norm_w"])
    w_uv = inp["l1_w_uv"]
    sh["sg_wu"] = np.ascontiguousarray(w_uv[:, :2048].reshape(8, 128, 16, 128).transpose(2, 1, 0, 3).reshape(16, 128, 1024))
    sh["sg_wv"] = np.ascontiguousarray(w_uv[:, 2048:].reshape(8, 128, 2048))
    sh["sg_wo"] = np.ascontiguousarray(inp["l1_w_out"].reshape(16, 128, 8, 128).transpose(2, 1, 0, 3).reshape(8, 128, 2048))
    sh["sg_wsT"] = np.ascontiguousarray(inp["l1_w_s"].transpose(2, 0, 1).reshape(128, 2048))
    sh["sg_grep"] = np.ascontiguousarray(np.repeat(_fm(inp["l1_sgu_ln_g"]), 128, axis=1))
    sh["sg_bsb"] = np.ascontiguousarray(np.broadcast_to(inp["l1_b_s"].reshape(1, 2048), (128, 2048)))
    sh["sg_bvrow"] = np.ascontiguousarray(inp["l1_b_uv"][2048:].reshape(1, 2048))
    v1 = sh["vec1"]
    v1[:, 120:136] = _fm(inp["l1_b_uv"][:2048])
    v1[:, 152:168] = _fm(inp["l1_sgu_ln_b"])
    return sh


def prep_core(inp, bi):
    x = inp["x"][bi]
    xT = np.ascontiguousarray(x.reshape(S, 8, 128).transpose(2, 1, 0))
    cT = _fm(inp["c"][bi])
    return {"xT": xT, "cT": cT}


def kernel(**inputs):
    inp = {k: np.asarray(v) for k, v in inputs.items()}
    nc, _ = build_program()
    sh = prep_shared(inp)
    decl = set()
    for nm in list(sh.keys()) + ["xT", "cT"]:
        try:
            nc.lookup_mloc(nm)
            decl.add(nm)
        except Exception:
            pass
    in_maps = []
    for bi in range(8):
        m = dict(sh)
        m.update(prep_core(inp, bi))
        in_maps.append({k: v for k, v in m.items() if k in decl})
    res = run_bass_kernel_spmd(nc, in_maps, core_ids=list(range(8)))
    out = np.empty((8, S, D), np.float32)
    for bi in range(8):
        yT = res.results[bi]["yT"]
        out[bi] = yT.transpose(2, 1, 0).reshape(S, D)
    return out
```

```python
import math
import numpy as np
from contextlib import ExitStack, contextmanager
import concourse.bass as bass
import concourse.mybir as mybir
from concourse.bass_utils import run_bass_kernel_spmd

F32 = mybir.dt.float32
BF16 = mybir.dt.bfloat16
AF = mybir.ActivationFunctionType
ALU = mybir.AluOpType
AX = mybir.AxisListType

ENGS = ("tensor", "vector", "scalar", "gpsimd", "sync")
N_DMA_SEMS = 24
import os
SSD_CUT = int(os.environ['SSD_CUT']) if 'SSD_CUT' in os.environ else None
SSD_SUB = int(os.environ['SSD_SUB']) if 'SSD_SUB' in os.environ else None

D = 1024
S = 2048
DFF = 2816
NFF = 22
ALPHA = 4.0 ** 0.25
EPS = 1e-5
EPS_LN = EPS / (ALPHA * ALPHA)


class _Op:
    __slots__ = ("eng", "fn", "deps", "dma", "idx", "needed", "semval", "dsem", "epoch")


class Builder:
    def __init__(self):
        self.nc = bass.Bass("TRN2", target_bir_lowering=False)
        self.es = ExitStack()
        self.ops = []
        self.st = {}
        self.dma_last = [None] * N_DMA_SEMS
        self.dma_cnt = [0] * N_DMA_SEMS
        self.dma_rr = [0, 0]
        self.out_dmas = []
        self.last_on = {e: None for e in ENGS}
        self.bar = {}
        self.epoch = 0
        self.pes = None
        self.exclusive = set()

    def sbuf(self, name, shape, dtype, persistent=False):
        es = self.es if (persistent or self.pes is None) else self.pes
        if es is not self.es:
            name = "%s_p%d" % (name, self.epoch)
        return es.enter_context(self.nc.sbuf_tensor(name, list(shape), dtype))

    def psum(self, name, shape, dtype=F32, persistent=False):
        es = self.es if (persistent or self.pes is None) else self.pes
        return es.enter_context(self.nc.psum_tensor(name, list(shape), dtype))

    def dram(self, name, shape, dtype, kind="Internal"):
        return self.nc.dram_tensor(name, list(shape), dtype, kind=kind)

    @contextmanager
    def phase(self):
        self.pes = ExitStack()
        try:
            yield
        finally:
            self.pes.close()
            self.pes = None
            self.barrier()

    def barrier(self):
        snap = set()
        for e in ENGS:
            if self.last_on[e] is not None:
                snap.add(self.last_on[e])
        for s in range(N_DMA_SEMS):
            if self.dma_last[s] is not None:
                snap.add(self.dma_last[s])
        self.bar = {e: set(snap) for e in ENGS}
        self.st = {}
        self.epoch += 1

    @staticmethod
    def _norm(a):
        if isinstance(a, tuple):
            b, k = a
        else:
            b, k = a, None
        if not isinstance(b, str):
            b = b.name
        return b, k

    def _entries(self, b, k):
        d = self.st.setdefault(b, {})
        if k is None:
            return list(d.keys())
        out = []
        if k in d:
            out.append(k)
        if None in d:
            out.append(None)
        return out

    def add(self, eng, fn, reads=(), writes=(), dma=False, out=False):
        op = _Op()
        op.eng, op.fn, op.dma = eng, fn, dma
        op.idx = len(self.ops)
        op.needed = False
        op.semval = None
        op.dsem = None
        op.epoch = self.epoch
        deps = set()
        if self.bar.get(eng):
            deps |= self.bar.pop(eng)
        reads = [self._norm(a) for a in reads]
        writes = [self._norm(a) for a in writes]
        for (bb, kk) in list(reads):
            if bb in self.exclusive and (bb, kk) not in writes:
                writes.append((bb, kk))
        for b, k in reads:
            d = self.st.setdefault(b, {})
            for kk in self._entries(b, k):
                w = d[kk][0]
                if w is not None:
                    deps.add(w)
        for b, k in writes:
            d = self.st.setdefault(b, {})
            for kk in self._entries(b, k):
                w, rs = d[kk]
                if w is not None:
                    deps.add(w)
                deps.update(rs)
        for b, k in reads:
            d = self.st[b]
            if k not in d:
                d[k] = [None, []]
                if k is not None and None in d:
                    d[k][0] = d[None][0]
            d[k][1].append(op.idx)
        for b, k in writes:
            d = self.st[b]
            if k is None:
                d.clear()
                d[None] = [op.idx, []]
            else:
                d[k] = [op.idx, []]
        if dma:
            half = N_DMA_SEMS // 2
            pool = 1 if eng == "gpsimd" else 0
            s = pool * half + self.dma_rr[pool]
            self.dma_rr[pool] = (self.dma_rr[pool] + 1) % half
            if self.dma_last[s] is not None:
                deps.add(self.dma_last[s])
            self.dma_last[s] = op.idx
            self.dma_cnt[s] += 1
            op.dsem = s
            op.semval = 16 * self.dma_cnt[s]
            if out:
                self.out_dmas.append(op.idx)
        deps.discard(op.idx)
        op.deps = deps
        self.ops.append(op)
        self.last_on[eng] = op.idx
        return op.idx

    def finish(self):
        nc = self.nc
        last = _Op()
        last.eng, last.fn, last.dma = "sync", (lambda e: e.nop()), False
        last.idx = len(self.ops)
        last.needed = False
        last.semval = None
        last.dsem = None
        last.epoch = self.epoch
        last.deps = set(self.out_dmas)
        self.ops.append(last)
        ops = self.ops
        for op in ops:
            nd = set()
            for j in op.deps:
                pj = ops[j]
                if (not pj.dma) and pj.eng == op.eng and (not op.dma) and op.eng == "tensor":
                    continue
                nd.add(j)
            op.deps = nd
            for j in nd:
                ops[j].needed = True
        cnt = {}
        for op in ops:
            if op.dma:
                continue
            if op.needed:
                key = (op.eng, op.epoch)
                cnt[key] = cnt.get(key, 0) + 1
                op.semval = cnt[key]
        esem = {}
        for key in cnt:
            esem[key] = self.es.enter_context(nc.semaphore("s_%s_%d" % key))
        dsem = [self.es.enter_context(nc.semaphore("d_%d" % i)) for i in range(N_DMA_SEMS)]
        self.n_sems = len(esem) + N_DMA_SEMS

        def emit_engine(ename):
            def body(eng):
                waited = {}
                for op in ops:
                    if op.eng != ename:
                        continue
                    need = {}
                    for j in op.deps:
                        pj = ops[j]
                        if pj.dma:
                            key = ("d", pj.dsem)
                            sem = dsem[pj.dsem]
                        else:
                            key = (pj.eng, pj.epoch)
                            sem = esem[key]
                        v = pj.semval
                        if waited.get(key, 0) >= v:
                            continue
                        if key not in need or need[key][1] < v:
                            need[key] = (sem, v)
                    for key, (sem, v) in need.items():
                        eng.wait_ge(sem, v)
                        waited[key] = v
                    ins = op.fn(eng)
                    if op.dma:
                        ins.then_inc(dsem[op.dsem], 16)
                    elif op.needed:
                        ins.then_inc(esem[(op.eng, op.epoch)], 1)
            return body

        with nc.Block() as block:
            block.tensor(emit_engine("tensor"))
            block.vector(emit_engine("vector"))
            block.scalar(emit_engine("scalar"))
            block.gpsimd(emit_engine("gpsimd"))
            block.sync(emit_engine("sync"))
        self.es.close()
        return nc


class Prog:
    pass


def tiles_of(k, n0, n1):
    return [(k, n) for n in range(n0, n1)]


def ada_phase(b, P, L):
    adaw = P.adaw[L]
    vec = P.vec[L]
    with b.phase():
        scb = b.sbuf("ada_sc", [128, 8], BF16)
        b.add("scalar", lambda e: e.activation(out=scb[:], in_=P.cT[:], func=AF.Silu),
              reads=[P.cT], writes=[scb])
        wb = [b.sbuf("ada_w%d" % i, [128, 9216], BF16) for i in range(3)]
        ps = P.PSA

        def load(k):
            t = wb[k % 3]
            b.add("gpsimd", lambda e: e.dma_start(out=t[:], in_=adaw[k]), reads=[], writes=[t], dma=True)
        for k in range(3):
            load(k)
        zt = b.sbuf("ada_zero", [128, 128], BF16)
        b.add("vector", lambda e: e.memset(zt[:], 0.0), reads=[], writes=[zt])
        b.add("tensor", lambda e: e.matmul(ps[:, 0, 0:72], lhsT=zt[:], rhs=zt[:, 0:72], start=True, stop=False),
              reads=[zt], writes=[(ps, 0)])
        for k in range(8):
            t = wb[k % 3]
            for j in range(72):
                b.add("tensor",
                      lambda e, t=t, j=j, k=k: e.matmul(ps[:, 0, j:j + 1], lhsT=t[:, j * 128:(j + 1) * 128],
                                                        rhs=scb[:, k:k + 1], start=False, stop=(k == 7 and j == 71)),
                      reads=[t, scb], writes=[(ps, 0)])
            if k + 3 < 8:
                load(k + 3)
        b.add("vector", lambda e: e.tensor_tensor(out=P.modv[:], in0=ps[:, 0, 0:72], in1=vec[:, 0:72], op=ALU.add),
              reads=[(ps, 0), vec], writes=[P.modv])
        derive_vecs(b, P)


def derive_vecs(b, P):
    if True:
        for i in range(3):
            coef = (1.0 / ALPHA) if i == 1 else (0.5 / ALPHA)
            b.add("vector", lambda e, i=i: e.tensor_scalar(out=P.dvA[:, i * 8:(i + 1) * 8],
                                                           in0=P.modv[:, (3 * i + 1) * 8:(3 * i + 2) * 8],
                                                           scalar1=1.0, scalar2=None, op0=ALU.add),
                  reads=[P.modv], writes=[(P.dvA, i)])
            b.add("vector", lambda e, i=i, coef=coef: e.tensor_scalar(out=P.dvG[:, i * 8:(i + 1) * 8],
                                                                      in0=P.modv[:, (3 * i + 2) * 8:(3 * i + 3) * 8],
                                                                      scalar1=1.0, scalar2=coef, op0=ALU.add, op1=ALU.mult),
                  reads=[P.modv], writes=[(P.dvG, i)])


def make_h(b, P, i, hT, t0, nt, eng="scalar"):
    for k in range(8):
        rd = [(P.xT, (k, n)) for n in range(t0 // 512, (t0 + nt + 511) // 512)]
        b.add("scalar",
              lambda e, k=k: e.activation(out=hT[:, k, 0:nt], in_=P.xT[:, k, t0:t0 + nt], func=AF.Identity,
                                          scale=P.dvA[:, i * 8 + k:i * 8 + k + 1],
                                          bias=P.modv[:, (3 * i) * 8 + k:(3 * i) * 8 + k + 1]),
              reads=rd + [(P.dvA, i), P.modv], writes=[(hT, k)])


class LNState:
    def __init__(self, b, P, nbuf=3, nt=2, nset=1):
        self.nbuf, self.nt = nbuf, nt
        self.sets = []
        self.ybf = [b.sbuf("ln_ybf%d" % i, [128, 512], BF16) for i in range(nbuf)]
        self.ysq = [b.sbuf("ln_ysq%d" % i, [128, 512], BF16) for i in range(nbuf)]
        for q in range(nset):
            self.sets.append((b.sbuf("ln_mean%d" % q, [128, 512], F32), b.sbuf("ln_msq%d" % q, [128, 512], F32),
                              b.sbuf("ln_rstd%d" % q, [128, 512], F32)))
        self.t1 = [b.sbuf("ln_t1_%d" % i, [128, 512], F32) for i in range(nt)]
        self.t2 = [b.sbuf("ln_t2_%d" % i, [128, 512], F32) for i in range(nt)]
        self.cnt = 0


def residual_evac(b, P, ln, i, pso, psk, j, n, s1, s2):
    c = ln.cnt
    ln.cnt += 1
    ybf = ln.ybf[c % ln.nbuf]
    ysq = ln.ysq[c % ln.nbuf]
    sl = slice(n * 512, (n + 1) * 512)
    b.add("vector",
          lambda e: e.scalar_tensor_tensor(out=P.xT[:, j, sl], in0=pso[:, psk, :], scalar=P.dvG[:, i * 8 + j:i * 8 + j + 1],
                                           in1=P.xT[:, j, sl], op0=ALU.mult, op1=ALU.add),
          reads=[(pso, psk), (P.dvG, i), (P.xT, (j, n))], writes=[(P.xT, (j, n))])
    b.add("scalar", lambda e: e.activation(out=ybf[:], in_=P.xT[:, j, sl], func=AF.Identity),
          reads=[(P.xT, (j, n))], writes=[ybf])
    b.add("scalar", lambda e: e.activation(out=ysq[:], in_=P.xT[:, j, sl], func=AF.Square),
          reads=[(P.xT, (j, n))], writes=[ysq])
    b.add("tensor", lambda e: e.matmul(s1[0][:, s1[1], :], lhsT=P.onesb[:], rhs=ybf[:], start=(j == 0), stop=(j == 7)),
          reads=[ybf, P.onesb], writes=[s1])
    b.add("tensor", lambda e: e.matmul(s2[0][:, s2[1], :], lhsT=P.onesb[:], rhs=ysq[:], start=(j == 0), stop=(j == 7)),
          reads=[ysq, P.onesb], writes=[s2])


def ln_finish(b, P, ln, L, i, n, s1, s2, sidx=0, defer=False):
    sl = slice(n * 512, (n + 1) * 512)
    vec = P.vec[L]
    gcol = 72 + i * 8
    bcol = 72 + 24 + i * 8
    mean, msq, rstd = ln.sets[sidx]
    b.add("scalar", lambda e: e.activation(out=mean[:], in_=s1[0][:, s1[1], :], func=AF.Identity),
          reads=[s1], writes=[mean])
    b.add("scalar", lambda e: e.activation(out=msq[:], in_=s1[0][:, s1[1], :], func=AF.Square),
          reads=[s1], writes=[msq])
    b.add("vector", lambda e: e.tensor_tensor(out=msq[:], in0=s2[0][:, s2[1], :], in1=msq[:], op=ALU.subtract),
          reads=[s2, msq], writes=[msq])
    specs = []
    specs.append(("scalar", lambda e: e.activation(out=rstd[:], in_=msq[:], func=AF.Sqrt, bias=P.epsln[:, 0:1]),
                  [msq, P.epsln], [rstd]))
    specs.append(("vector", lambda e: e.reciprocal(out=rstd[:], in_=rstd[:]), [rstd], [rstd]))
    for k in range(8):
        t1 = ln.t1[k % ln.nt]
        t2 = ln.t2[k % ln.nt]
        specs.append(("gpsimd", lambda e, k=k, t1=t1: e.tensor_tensor(out=t1[:], in0=P.xT[:, k, sl], in1=mean[:], op=ALU.subtract),
                      [(P.xT, (k, n)), mean], [t1]))
        specs.append(("vector", lambda e, k=k, t1=t1, t2=t2: e.scalar_tensor_tensor(out=t2[:], in0=t1[:], scalar=vec[:, gcol + k:gcol + k + 1],
                                                                                 in1=rstd[:], op0=ALU.mult, op1=ALU.mult),
                      [t1, rstd, vec], [t2]))
        specs.append(("scalar", lambda e, k=k, t2=t2: e.activation(out=P.xT[:, k, sl], in_=t2[:], func=AF.Identity,
                                                               bias=vec[:, bcol + k:bcol + k + 1]),
                      [t2, vec], [(P.xT, (k, n))]))
    if defer:
        return specs
    emit_specs(b, specs)
    return []


def emit_specs(b, specs, n=None):
    k = len(specs) if n is None else min(n, len(specs))
    for _ in range(k):
        eng, fn, rd, wr = specs.pop(0)
        b.add(eng, fn, reads=rd, writes=wr)


def ffn_phase(b, P, L, i, win, wout):
    with b.phase():
        hT = b.sbuf("ffn_hT", [128, 8, 1024], BF16)
        gT = b.sbuf("ffn_gT", [128, NFF, 1024], BF16)
        wib = [b.sbuf("ffn_wi%d" % r, [128, 2, 8, 128], BF16) for r in range(3)]
        wob = [b.sbuf("ffn_wo%d" % r, [128, NFF, 128], BF16) for r in range(3)]
        sa = [b.sbuf("ffn_sa%d" % r, [128, 512], F32) for r in range(2)]
        ln = LNState(b, P, nset=2)
        PSA, PSB = P.PSA, P.PSB
        pend = []

        def load_wi(m):
            t = wib[m % 3]
            b.add("gpsimd", lambda e: e.dma_start(out=t[:].rearrange("p a k c -> p (a k c)"), in_=win[m]),
                  reads=[], writes=[t], dma=True)

        def load_wo(j):
            t = wob[j % 3]
            b.add("gpsimd", lambda e: e.dma_start(out=t[:].rearrange("p m c -> p (m c)"), in_=wout[j]),
                  reads=[], writes=[t], dma=True)

        for hf in range(2):
            T0 = hf * 1024
            for m in range(3):
                load_wi(m)
            make_h(b, P, i, hT, T0, 1024)
            cnt = 0
            for m in range(NFF):
                w = wib[m % 3]
                for n in range(2):
                    ka = cnt % 2
                    cnt += 1
                    for ab in range(2):
                        bank = 2 * ka + ab
                        for k in range(8):
                            b.add("tensor",
                                  lambda e, w=w, ab=ab, k=k, bank=bank, n=n: e.matmul(
                                      PSB[:, bank, :], lhsT=w[:, ab, k, :], rhs=hT[:, k, n * 512:(n + 1) * 512],
                                      start=(k == 0), stop=(k == 7)),
                                  reads=[w, (hT, k)], writes=[(PSB, bank)])
                    s = sa[ka]
                    b.add("scalar", lambda e, s=s, ka=ka: e.activation(out=s[:], in_=PSB[:, 2 * ka, :], func=AF.Silu),
                          reads=[(PSB, 2 * ka)], writes=[s])
                    b.add("vector",
                          lambda e, s=s, ka=ka, m=m, n=n: e.tensor_tensor(out=gT[:, m, n * 512:(n + 1) * 512], in0=PSB[:, 2 * ka + 1, :],
                                                                            in1=s[:], op=ALU.mult),
                          reads=[(PSB, 2 * ka + 1), s], writes=[(gT, (m, n))])
                    emit_specs(b, pend, 2)
                if m + 3 < NFF:
                    load_wi(m + 3)
            emit_specs(b, pend)
            for j in range(3):
                load_wo(j)
            cnt = 0
            for j in range(8):
                w = wob[j % 3]
                for n in range(2):
                    ko = cnt % 2
                    cnt += 1
                    for m in range(NFF):
                        b.add("tensor",
                              lambda e, w=w, m=m, n=n, ko=ko: e.matmul(PSA[:, ko, :], lhsT=w[:, m, :],
                                                                       rhs=gT[:, m, n * 512:(n + 1) * 512],
                                                                       start=(m == 0), stop=(m == NFF - 1)),
                              reads=[w, (gT, (m, n))], writes=[(PSA, ko)])
                    st = (PSA, PSB)[n]
                    residual_evac(b, P, ln, i, PSA, ko, j, hf * 2 + n, (st, 2), (st, 3))
                if j + 3 < 8:
                    load_wo(j + 3)
            for n in range(2):
                st = (PSA, PSB)[n]
                pend += ln_finish(b, P, ln, L, i, hf * 2 + n, (st, 2), (st, 3), sidx=n, defer=(hf == 0))


def sgu_phase(b, P, L, W):
    vec = P.vec[L]
    BU, LNG2, LNB2 = 120, 136, 152
    with b.phase():
        hT = b.sbuf("sgs_hT", [128, 8, 512], BF16)
        Wv = b.sbuf("sgs_Wv", [128, 8, 2048], BF16)
        tri = b.sbuf("sgs_tri", [128, 128], F32)
        WcT = b.sbuf("sgs_WcT", [128, 2048], BF16)
        Bt = b.sbuf("sgs_Bt", [128, 2048], F32)
        grep = b.sbuf("sgs_grep", [128, 2048], F32)
        ones1 = b.sbuf("sgs_ones1", [128, 128], BF16)
        orow = b.sbuf("sgs_orow", [1, 128], BF16)
        bvrow = b.sbuf("sgs_bvrow", [1, 2048], BF16)
        uT = b.sbuf("sgs_uT", [128, 16, 512], BF16)
        PT = b.sbuf("sgs_PT", [128, 16, 512], BF16)
        vg = b.sbuf("sgs_vg", [128, 2048], F32)
        vn = b.sbuf("sgs_vn", [128, 2048], BF16)
        vsq = vn
        wsf = vg
        tt = b.sbuf("sgs_tt", [128, 1024], F32)
        st = b.sbuf("sgs_st", [128, 8], F32)
        wub = [b.sbuf("sgs_wu%d" % r, [128, 8, 128], BF16) for r in range(2)]
        wob = [b.sbuf("sgs_wo%d" % r, [128, 16, 128], BF16) for r in range(2)]
        ln = LNState(b, P, nbuf=2, nt=1)
        PSA, PSB = P.PSA, P.PSB
        pend = []
        for k in range(8):
            b.add("gpsimd", lambda e, k=k: e.dma_start(out=Wv[:, k, :], in_=W["wv"][k]), reads=[], writes=[(Wv, k)], dma=True)
        b.add("sync", lambda e: e.dma_start(out=wsf[:], in_=W["wsT"][:]), reads=[], writes=[wsf], dma=True)
        b.add("sync", lambda e: e.dma_start(out=tri[:], in_=P.cst_d[:, 128:256]), reads=[], writes=[tri], dma=True)
        b.add("sync", lambda e: e.dma_start(out=grep[:], in_=W["grep"][:]), reads=[], writes=[grep], dma=True)
        b.add("sync", lambda e: e.dma_start(out=Bt[:], in_=W["bsb"][:]), reads=[], writes=[Bt], dma=True)
        b.add("gpsimd", lambda e: e.dma_start(out=bvrow[:], in_=W["bvrow"][:]), reads=[], writes=[bvrow], dma=True)
        b.add("vector", lambda e: e.memset(ones1[:], 1.0), reads=[], writes=[ones1])
        b.add("vector", lambda e: e.memset(orow[:], 1.0), reads=[], writes=[orow])
        b.add("vector", lambda e: e.tensor_tensor(out=WcT[:].rearrange("p (g t) -> p g t", g=16),
                                                  in0=wsf[:].rearrange("p (g t) -> p g t", g=16),
                                                  in1=tri[:].unsqueeze(1).to_broadcast([128, 16, 128]), op=ALU.mult),
              reads=[wsf, tri], writes=[WcT])
        for q in range(4):
            b.add("tensor", lambda e, q=q: e.matmul(PSA[:, q, :], lhsT=ones1[:], rhs=WcT[:, q * 512:(q + 1) * 512], start=True, stop=True),
                  reads=[ones1, WcT], writes=[(PSA, q)])
        b.add("vector", lambda e: e.tensor_tensor(out=vg[:].rearrange("p (g t) -> p g t", g=16),
                                                  in0=PSA[:].rearrange("p q (g t) -> p (q g) t", g=4),
                                                  in1=vec[:, LNB2:LNB2 + 16].unsqueeze(2).to_broadcast([128, 16, 128]), op=ALU.mult),
              reads=[PSA, vec], writes=[vg])
        b.add("vector", lambda e: e.tensor_tensor(out=Bt[:], in0=Bt[:], in1=vg[:], op=ALU.add), reads=[Bt, vg], writes=[Bt])

        def load_wu(ii):
            t = wub[ii % 2]
            b.add("gpsimd", lambda e: e.dma_start(out=t[:].rearrange("p k c -> p (k c)"), in_=W["wu"][ii % 16]),
                  reads=[], writes=[t], dma=True)

        def load_wo(jj):
            t = wob[jj % 2]
            b.add("gpsimd", lambda e: e.dma_start(out=t[:].rearrange("p i c -> p (i c)"), in_=W["wo"][jj % 8]),
                  reads=[], writes=[t], dma=True)

        for G in range(4):
            make_h(b, P, 1, hT, G * 512, 512)
            for ii in range(2):
                load_wu(G * 16 + ii)
            for i in range(16):
                w = wub[(G * 16 + i) % 2]
                bk = i % 2
                for k in range(8):
                    b.add("tensor", lambda e, w=w, k=k, bk=bk: e.matmul(PSB[:, bk, :], lhsT=w[:, k, :], rhs=hT[:, k, :],
                                                                       start=(k == 0), stop=(k == 7)),
                          reads=[w, (hT, k)], writes=[(PSB, bk)])
                b.add("scalar", lambda e, i=i, bk=bk: e.activation(out=uT[:, i, :], in_=PSB[:, bk, :], func=AF.Gelu,
                                                                   bias=vec[:, BU + i:BU + i + 1]),
                      reads=[(PSB, bk), vec], writes=[(uT, i)])
                emit_specs(b, pend, 2)
                if i + 2 < 16:
                    load_wu(G * 16 + i + 2)
            emit_specs(b, pend)
            for j in range(2):
                load_wo(G * 8 + j)
            def v_proj(cc):
                tk = slice(cc * 128, (cc + 1) * 128)
                for nb in range(4):
                    for k in range(8):
                        b.add("tensor", lambda e, nb=nb, k=k, tk=tk: e.matmul(PSA[:, nb, :], lhsT=hT[:, k, tk], rhs=Wv[:, k, nb * 512:(nb + 1) * 512],
                                                                             start=(k == 0), stop=False),
                              reads=[(hT, k), (Wv, k)], writes=[(PSA, nb)])
                    b.add("tensor", lambda e, nb=nb: e.matmul(PSA[:, nb, :], lhsT=orow[0:1, :], rhs=bvrow[0:1, nb * 512:(nb + 1) * 512],
                                                             start=False, stop=True),
                          reads=[orow, bvrow], writes=[(PSA, nb)])

            v_proj(0)
            for cc in range(4):
                tk = slice(cc * 128, (cc + 1) * 128)
                b.add("scalar", lambda e: e.activation(out=vg[:], in_=PSA[:].rearrange("p q n -> p (q n)"), func=AF.Gelu),
                      reads=[PSA], writes=[vg])
                b.add("vector", lambda e: e.reduce_sum(out=st[:, 0:1], in_=vg[:], axis=AX.X), reads=[vg], writes=[(st, 0)])
                b.add("scalar", lambda e: e.activation(out=vsq[:], in_=vg[:], func=AF.Square), reads=[vg], writes=[vsq])
                b.add("vector", lambda e: e.reduce_sum(out=st[:, 1:2], in_=vsq[:], axis=AX.X), reads=[vsq], writes=[(st, 1)])
                b.add("vector", lambda e: e.tensor_scalar(out=st[:, 2:3], in0=st[:, 0:1], scalar1=1.0 / 2048.0, scalar2=None, op0=ALU.mult),
                      reads=[(st, 0)], writes=[(st, 2)])
                b.add("vector", lambda e: e.tensor_tensor(out=st[:, 3:4], in0=st[:, 2:3], in1=st[:, 2:3], op=ALU.mult),
                      reads=[(st, 2)], writes=[(st, 3)])
                b.add("vector", lambda e: e.scalar_tensor_tensor(out=st[:, 4:5], in0=st[:, 1:2], scalar=1.0 / 2048.0, in1=st[:, 3:4],
                                                                 op0=ALU.mult, op1=ALU.subtract),
                      reads=[(st, 1), (st, 3)], writes=[(st, 4)])
                b.add("scalar", lambda e: e.activation(out=st[:, 5:6], in_=st[:, 4:5], func=AF.Sqrt, bias=P.eps5[:, 0:1]),
                      reads=[(st, 4), P.eps5], writes=[(st, 5)])
                b.add("vector", lambda e: e.reciprocal(out=st[:, 6:7], in_=st[:, 5:6]), reads=[(st, 5)], writes=[(st, 6)])
                b.add("vector", lambda e: e.scalar_tensor_tensor(out=st[:, 7:8], in0=st[:, 2:3], scalar=-1.0, in1=st[:, 6:7],
                                                                 op0=ALU.mult, op1=ALU.mult),
                      reads=[(st, 2), (st, 6)], writes=[(st, 7)])
                b.add("scalar", lambda e: e.activation(out=vn[:], in_=vg[:], func=AF.Identity, scale=st[:, 6:7], bias=st[:, 7:8]),
                      reads=[vg, (st, 6), (st, 7)], writes=[vn])
                if cc + 1 < 4:
                    v_proj(cc + 1)
                for hh in range(2):
                    for g8 in range(8):
                        g = hh * 8 + g8
                        col = g8 * 128
                        b.add("tensor", lambda e, g=g, hh=hh, col=col: e.matmul(
                            PSB[:, 2 * hh + col // 512, (col % 512):(col % 512) + 128], lhsT=vn[:, g * 128:(g + 1) * 128],
                            rhs=WcT[:, g * 128:(g + 1) * 128], start=True, stop=True),
                              reads=[vn, WcT], writes=[(PSB, 2 * hh + col // 512)])
                    hs = slice(hh * 1024, (hh + 1) * 1024)
                    b.add("vector", lambda e, hh=hh, hs=hs: e.tensor_tensor(out=tt[:], in0=PSB[:, 2 * hh:2 * hh + 2, :].rearrange("p q n -> p (q n)"),
                                                                          in1=grep[:, hs], op=ALU.mult),
                          reads=[(PSB, 2 * hh), (PSB, 2 * hh + 1), grep], writes=[tt])
                    b.add("gpsimd", lambda e, hs=hs: e.tensor_tensor(out=tt[:], in0=tt[:], in1=Bt[:, hs], op=ALU.add),
                          reads=[tt, Bt], writes=[tt])
                    b.add("vector", lambda e, hh=hh, tk=tk: e.tensor_tensor(out=PT[:, hh * 8:(hh + 1) * 8, tk],
                                                                          in0=tt[:].rearrange("p (g t) -> p g t", g=8),
                                                                          in1=uT[:, hh * 8:(hh + 1) * 8, tk], op=ALU.mult),
                          reads=[tt] + [(uT, hh * 8 + q) for q in range(8)], writes=[(PT, (hh, cc))])
            for j in range(8):
                w = wob[(G * 8 + j) % 2]
                ko = j % 2
                for i in range(16):
                    b.add("tensor", lambda e, w=w, i=i, ko=ko: e.matmul(PSB[:, ko, :], lhsT=w[:, i, :], rhs=PT[:, i, :],
                                                                       start=(i == 0), stop=(i == 15)),
                          reads=[w] + [(PT, (i // 8, q)) for q in range(4)], writes=[(PSB, ko)])
                residual_evac(b, P, ln, 1, PSB, ko, j, G, (PSB, 2), (PSB, 3))
                if j + 2 < 8:
                    load_wo(G * 8 + j + 2)
            ln_finish(b, P, ln, L, 1, G, (PSB, 2), (PSB, 3))


def ssd_phase(b, P, W):
    vec = P.vec[0]
    CW, CB, NW = 120, 168, 180
    with b.phase():
        hT = b.sbuf("sd_hT", [128, 8, 512], BF16)
        Wx = b.sbuf("sd_Wx", [128, 8, 1536], BF16)
        Wz = b.sbuf("sd_Wz", [128, 8, 1024], BF16)
        Wd = b.sbuf("sd_Wd", [128, 8, 16], BF16)
        ident = b.sbuf("sd_ident", [128, 128], BF16)
        trib = b.sbuf("sd_trib", [128, 128], BF16)
        maskb = b.sbuf("sd_maskb", [128, 512], BF16)
        onesb1 = b.sbuf("sd_onesb1", [128, 128], BF16)
        adth = b.sbuf("sd_adth", [128, 16], BF16)
        adtl = b.sbuf("sd_adtl", [128, 16], BF16)
        RH = b.sbuf("sd_RH", [128, 2048], BF16)
        RL = b.sbuf("sd_RL", [128, 2048], BF16)
        one1 = b.sbuf("sd_one1", [128, 1], F32)
        tokc = b.sbuf("sd_tokc", [128, 1056], F32)
        halo = b.sbuf("sd_halo", [128, 12, 3], F32)
        xr = [b.sbuf("sd_xr%d" % i, [128, 515], F32) for i in range(2)]
        acc = [b.sbuf("sd_acc%d" % i, [128, 512], F32) for i in range(2)]
        xbcs = b.sbuf("sd_xbcs", [128, 12, 512], BF16)
        small = b.sbuf("sd_small", [128, 16 * 13], F32)
        Dm = b.sbuf("sd_Dm", [128, 2048], F32)
        MT = b.sbuf("sd_MT", [128, 2048], BF16)
        cbT = b.sbuf("sd_cbT", [128, 256], F32)
        zs = b.sbuf("sd_zs", [128, 1024], F32)
        xst = b.sbuf("sd_xst", [128, 1024], BF16)
        Btok = b.sbuf("sd_Btok", [128, 256], BF16)
        y1 = b.sbuf("sd_y1", [128, 1024], F32)
        tmp = b.sbuf("sd_tmp", [128, 1024], F32)
        xcd = b.sbuf("sd_xcd", [128, 1024], BF16)
        state = b.sbuf("sd_state", [128, 1024], F32)
        statebf = b.sbuf("sd_statebf", [128, 1024], BF16)
        yn = b.sbuf("sd_yn", [128, 1024], BF16)
        ysT = b.sbuf("sd_ysT", [128, 8, 512], BF16)
        PSA, PSB = P.PSA, P.PSB
        R3 = PSA[:].rearrange("p q (h l) -> p (q h) l", l=128)

        def sm(i, n=16):
            return small[:, i * 16:i * 16 + n]

        lim = [None]

        def A(*a, **k):
            if lim[0] is None:
                return b.add(*a, **k)
            if lim[0] > 0:
                lim[0] -= 1
                return b.add(*a, **k)
            return None

        def smk(i):
            return (small, i)

        for k in range(8):
            b.add("gpsimd", lambda e, k=k: e.dma_start(out=Wx[:, k, :], in_=W["wxbc"][k]), reads=[], writes=[(Wx, k)], dma=True)
            b.add("gpsimd", lambda e, k=k: e.dma_start(out=Wz[:, k, :], in_=W["wz"][k]), reads=[], writes=[(Wz, k)], dma=True)
            b.add("gpsimd", lambda e, k=k: e.dma_start(out=Wd[:, k, :], in_=W["wdt"][k]), reads=[], writes=[(Wd, k)], dma=True)
        b.add("gpsimd", lambda e: e.dma_start(out=ident[:], in_=P.cst_d[:, 0:128]), reads=[], writes=[ident], dma=True)
        b.add("gpsimd", lambda e: e.dma_start(out=trib[:], in_=P.cst_d[:, 128:256]), reads=[], writes=[trib], dma=True)
        for q in range(4):
            b.add("gpsimd", lambda e, q=q: e.dma_start(out=maskb[:, q * 128:(q + 1) * 128], in_=P.cst_d[:, 256:384]),
                  reads=[], writes=[(maskb, q)], dma=True)
        b.add("sync", lambda e: e.dma_start(out=tokc[:], in_=W["tokc"][:]), reads=[], writes=[tokc], dma=True)
        b.add("vector", lambda e: e.memset(onesb1[:], 1.0), reads=[], writes=[onesb1])
        if SSD_CUT is not None:
            b.add("vector", lambda e: e.memset(ysT[:], 0.0), reads=[], writes=[ysT])
        b.add("vector", lambda e: e.memset(one1[:], 1.0), reads=[], writes=[one1])
        b.add("vector", lambda e: e.memset(state[:], 0.0), reads=[], writes=[state])
        b.add("vector", lambda e: e.memset(statebf[:], 0.0), reads=[], writes=[statebf])
        b.add("scalar", lambda e: e.activation(out=sm(11), in_=tokc[:, 1040:1056], func=AF.Exp), reads=[tokc], writes=[smk(11)])
        b.add("vector", lambda e: e.tensor_scalar(out=sm(11), in0=sm(11), scalar1=-1.0, scalar2=None, op0=ALU.mult),
              reads=[smk(11)], writes=[smk(11)])

        for G in range(4):
            make_h(b, P, 1, hT, G * 512, 512)
            for q in range(12):
                bk = q % 2
                for k in range(8):
                    b.add("tensor", lambda e, q=q, k=k, bk=bk: e.matmul(PSB[:, bk, :], lhsT=Wx[:, k, q * 128:(q + 1) * 128], rhs=hT[:, k, :],
                                                                       start=(k == 0), stop=(k == 7)),
                          reads=[(Wx, k), (hT, k)], writes=[(PSB, bk)])
                x_r = xr[q % 2]
                a = acc[q % 2]
                if G == 0:
                    b.add("vector", lambda e, x_r=x_r: e.memset(x_r[:, 0:3], 0.0), reads=[], writes=[(x_r, "h")])
                else:
                    b.add("vector", lambda e, x_r=x_r, q=q: e.tensor_copy(out=x_r[:, 0:3], in_=halo[:, q, :]),
                          reads=[(halo, q)], writes=[(x_r, "h")])
                b.add("scalar", lambda e, x_r=x_r, bk=bk: e.activation(out=x_r[:, 3:515], in_=PSB[:, bk, :], func=AF.Identity),
                      reads=[(PSB, bk)], writes=[(x_r, "b")])
                if G < 3:
                    b.add("vector", lambda e, x_r=x_r, q=q: e.tensor_copy(out=halo[:, q, :], in_=x_r[:, 512:515]),
                          reads=[(x_r, "b")], writes=[(halo, q)])
                b.add("vector", lambda e, x_r=x_r, a=a, q=q: e.tensor_scalar(out=a[:], in0=x_r[:, 0:512], scalar1=vec[:, CW + q:CW + q + 1],
                                                                             scalar2=None, op0=ALU.mult),
                      reads=[(x_r, "h"), (x_r, "b"), vec], writes=[a])
                for kk in range(1, 4):
                    b.add("vector", lambda e, x_r=x_r, a=a, q=q, kk=kk: e.scalar_tensor_tensor(
                        out=a[:], in0=x_r[:, kk:kk + 512], scalar=vec[:, CW + kk * 12 + q:CW + kk * 12 + q + 1], in1=a[:],
                        op0=ALU.mult, op1=ALU.add),
                          reads=[(x_r, "h"), (x_r, "b"), vec, a], writes=[a])
                b.add("scalar", lambda e, a=a, q=q: e.activation(out=xbcs[:, q, :], in_=a[:], func=AF.Silu, bias=vec[:, CB + q:CB + q + 1]),
                      reads=[a, vec], writes=[(xbcs, q)])
            for cc in range(4):
                tk = slice(cc * 128, (cc + 1) * 128)
                if SSD_CUT is not None and SSD_CUT < 1:
                    continue
                for nb in range(2):
                    for k in range(8):
                        b.add("tensor", lambda e, nb=nb, k=k, tk=tk: e.matmul(PSB[:, 1 + nb, :], lhsT=hT[:, k, tk], rhs=Wz[:, k, nb * 512:(nb + 1) * 512],
                                                                             start=(k == 0), stop=(k == 7)),
                              reads=[(hT, k), (Wz, k)], writes=[(PSB, 1 + nb)])
                b.add("scalar", lambda e: e.activation(out=zs[:], in_=PSB[:, 1:3, :].rearrange("p q n -> p (q n)"), func=AF.Silu),
                      reads=[(PSB, 1), (PSB, 2)], writes=[zs])
                if SSD_CUT is not None and SSD_CUT < 2:
                    continue
                for k in range(8):
                    b.add("tensor", lambda e, k=k, tk=tk: e.matmul(PSB[:, 0, 0:16], lhsT=hT[:, k, tk], rhs=Wd[:, k, :], start=(k == 0), stop=(k == 7)),
                          reads=[(hT, k), (Wd, k)], writes=[(PSB, 0)])
                b.add("vector", lambda e: e.tensor_tensor(out=sm(0), in0=PSB[:, 0, 0:16], in1=tokc[:, 1024:1040], op=ALU.add),
                      reads=[(PSB, 0), tokc], writes=[smk(0)])
                b.add("scalar", lambda e: e.activation(out=sm(1), in_=sm(0), func=AF.Abs), reads=[smk(0)], writes=[smk(1)])
                b.add("scalar", lambda e: e.activation(out=sm(2), in_=sm(1), func=AF.Exp, scale=-1.0), reads=[smk(1)], writes=[smk(2)])
                b.add("scalar", lambda e: e.activation(out=sm(3), in_=sm(2), func=AF.Ln, bias=one1[:, 0:1]), reads=[smk(2), one1], writes=[smk(3)])
                b.add("vector", lambda e: e.scalar_tensor_tensor(out=sm(4), in0=sm(0), scalar=0.0, in1=sm(3), op0=ALU.max, op1=ALU.add),
                      reads=[smk(0), smk(3)], writes=[smk(4)])
                b.add("scalar", lambda e: e.activation(out=sm(5), in_=sm(4), func=AF.Ln), reads=[smk(4)], writes=[smk(5)])
                b.add("vector", lambda e: e.tensor_tensor(out=sm(6), in0=sm(4), in1=sm(11), op=ALU.mult), reads=[smk(4), smk(11)], writes=[smk(6)])
                if SSD_CUT is not None and SSD_CUT < 3:
                    continue
                lim[0] = SSD_SUB
                A("scalar", lambda e: e.activation(out=adth[:], in_=sm(6), func=AF.Identity), reads=[smk(6)], writes=[adth])
                A("vector", lambda e: e.tensor_tensor(out=adtl[:], in0=sm(6), in1=adth[:], op=ALU.subtract), reads=[smk(6), adth], writes=[adtl])
                for (src, dst) in ((adth, RH), (adtl, RL)):
                    A("vector", lambda e, src=src, dst=dst: e.tensor_tensor(out=dst[:].rearrange("p (h l) -> p h l", l=128),
                                                                              in0=src[:].unsqueeze(2).to_broadcast([128, 16, 128]),
                                                                              in1=trib[:].unsqueeze(1).to_broadcast([128, 16, 128]), op=ALU.mult),
                          reads=[src, trib], writes=[dst])
                for q4 in range(4):
                    A("tensor", lambda e, q4=q4: e.matmul(PSA[:, q4, :], lhsT=onesb1[:], rhs=RH[:, q4 * 512:(q4 + 1) * 512], start=True, stop=False),
                          reads=[onesb1, RH], writes=[(PSA, q4)])
                    A("tensor", lambda e, q4=q4: e.matmul(PSA[:, q4, :], lhsT=onesb1[:], rhs=RL[:, q4 * 512:(q4 + 1) * 512], start=False, stop=False),
                          reads=[onesb1, RL], writes=[(PSA, q4)])
                    A("tensor", lambda e, q4=q4: e.matmul(PSA[:, q4, :], lhsT=ident[:], rhs=maskb[:], start=False, stop=True),
                          reads=[ident, maskb], writes=[(PSA, q4)])
                A("tensor", lambda e: e.matmul(PSB[:, 0, 16:32], lhsT=trib[:], rhs=adth[:], start=True, stop=False),
                      reads=[trib, adth], writes=[(PSB, 0)])
                A("tensor", lambda e: e.matmul(PSB[:, 0, 16:32], lhsT=trib[:], rhs=adtl[:], start=False, stop=True),
                      reads=[trib, adtl], writes=[(PSB, 0)])
                A("vector", lambda e: e.tensor_tensor(out=sm(7), in0=PSB[:, 0, 16:32], in1=sm(5), op=ALU.subtract),
                      reads=[(PSB, 0), smk(5)], writes=[smk(7)])
                A("scalar", lambda e: e.activation(out=sm(8), in_=PSB[:, 0, 16:32], func=AF.Exp), reads=[(PSB, 0)], writes=[smk(8)])
                A("vector", lambda e: e.tensor_tensor(out=sm(9), in0=R3[:, :, 127], in1=sm(7), op=ALU.subtract),
                      reads=[PSA, smk(7)], writes=[smk(9)])
                A("scalar", lambda e: e.activation(out=sm(9), in_=sm(9), func=AF.Exp), reads=[smk(9)], writes=[smk(9)])
                A("scalar", lambda e: e.activation(out=sm(10), in_=R3[:, :, 127], func=AF.Exp), reads=[PSA], writes=[smk(10)])
                A("vector", lambda e: e.tensor_tensor(out=Dm[:].rearrange("p (h l) -> p h l", l=128), in0=R3,
                                                          in1=sm(7).unsqueeze(2).to_broadcast([128, 16, 128]), op=ALU.subtract),
                      reads=[PSA, smk(7)], writes=[Dm])
                A("scalar", lambda e: e.activation(out=Dm[:], in_=Dm[:], func=AF.Exp), reads=[Dm], writes=[Dm])
                if SSD_CUT is not None and SSD_CUT < 5:
                    continue
                for g in range(2):
                    b.add("tensor", lambda e, g=g, tk=tk: e.matmul(PSB[:, 0, 128 + g * 128:256 + g * 128], lhsT=xbcs[:, 8 + g, tk], rhs=xbcs[:, 10 + g, tk],
                                                                 start=True, stop=True),
                          reads=[(xbcs, 8 + g), (xbcs, 10 + g)], writes=[(PSB, 0)])
                b.add("scalar", lambda e: e.activation(out=cbT[:], in_=PSB[:, 0, 128:384], func=AF.Identity), reads=[(PSB, 0)], writes=[cbT])
                b.add("vector", lambda e: e.tensor_tensor(out=MT[:].rearrange("p (g r l) -> p g r l", g=2, r=8),
                                                          in0=Dm[:].rearrange("p (g r l) -> p g r l", g=2, r=8),
                                                          in1=cbT[:].rearrange("p (g l) -> p g l", g=2).unsqueeze(2).to_broadcast([128, 2, 8, 128]),
                                                          op=ALU.mult),
                      reads=[Dm, cbT], writes=[MT])
                if SSD_CUT is not None and SSD_CUT < 6:
                    continue
                for q in range(8):
                    b.add("tensor", lambda e, q=q, tk=tk: e.matmul(PSB[:, 1 + q // 4, (q % 4) * 128:(q % 4) * 128 + 128], lhsT=xbcs[:, q, tk], rhs=ident[:],
                                                                 start=True, stop=True),
                          reads=[(xbcs, q), ident], writes=[(PSB, 1 + q // 4)])
                b.add("scalar", lambda e: e.activation(out=xst[:], in_=PSB[:, 1:3, :].rearrange("p q n -> p (q n)"), func=AF.Identity),
                      reads=[(PSB, 1), (PSB, 2)], writes=[xst])
                for g in range(2):
                    b.add("tensor", lambda e, g=g, tk=tk: e.matmul(PSB[:, 3, g * 128:(g + 1) * 128], lhsT=xbcs[:, 8 + g, tk], rhs=ident[:], start=True, stop=True),
                          reads=[(xbcs, 8 + g), ident], writes=[(PSB, 3)])
                b.add("scalar", lambda e: e.activation(out=Btok[:], in_=PSB[:, 3, 0:256], func=AF.Identity), reads=[(PSB, 3)], writes=[Btok])
                if SSD_CUT is not None and SSD_CUT < 7:
                    continue
                for h in range(16):
                    b.add("tensor", lambda e, h=h: e.matmul(PSA[:, h // 8, (h % 8) * 64:(h % 8) * 64 + 64], lhsT=MT[:, h * 128:(h + 1) * 128],
                                                           rhs=xst[:, h * 64:(h + 1) * 64], start=True, stop=True),
                          reads=[MT, xst], writes=[(PSA, h // 8)])
                for g in range(2):
                    b.add("tensor", lambda e, g=g, tk=tk: e.matmul(PSA[:, 2 + g, :], lhsT=xbcs[:, 10 + g, tk], rhs=statebf[:, g * 512:(g + 1) * 512],
                                                                 start=True, stop=True),
                          reads=[(xbcs, 10 + g), statebf], writes=[(PSA, 2 + g)])
                b.add("vector", lambda e: e.tensor_tensor(out=y1[:].rearrange("p (h d) -> p h d", d=64),
                                                          in0=PSA[:, 2:4, :].rearrange("p q (h d) -> p (q h) d", d=64),
                                                          in1=sm(8).unsqueeze(2).to_broadcast([128, 16, 64]), op=ALU.mult),
                      reads=[(PSA, 2), (PSA, 3), smk(8)], writes=[y1])
                b.add("vector", lambda e: e.tensor_tensor(out=y1[:], in0=y1[:], in1=PSA[:, 0:2, :].rearrange("p q n -> p (q n)"), op=ALU.add),
                      reads=[y1, (PSA, 0), (PSA, 1)], writes=[y1])
                b.add("vector", lambda e: e.tensor_tensor(out=tmp[:], in0=xst[:], in1=tokc[:, 0:1024], op=ALU.mult), reads=[xst, tokc], writes=[tmp])
                b.add("vector", lambda e: e.tensor_tensor(out=y1[:], in0=y1[:], in1=tmp[:], op=ALU.add), reads=[y1, tmp], writes=[y1])
                b.add("vector", lambda e: e.tensor_tensor(out=y1[:], in0=y1[:], in1=zs[:], op=ALU.mult), reads=[y1, zs], writes=[y1])
                if SSD_CUT is not None and SSD_CUT < 8:
                    continue
                b.add("vector", lambda e: e.tensor_tensor(out=xcd[:].rearrange("p (h d) -> p h d", d=64), in0=xst[:].rearrange("p (h d) -> p h d", d=64),
                                                          in1=sm(9).unsqueeze(2).to_broadcast([128, 16, 64]), op=ALU.mult),
                      reads=[xst, smk(9)], writes=[xcd])
                for g in range(2):
                    b.add("tensor", lambda e, g=g: e.matmul(PSB[:, 1 + g, :], lhsT=Btok[:, g * 128:(g + 1) * 128], rhs=xcd[:, g * 512:(g + 1) * 512],
                                                           start=True, stop=True),
                          reads=[Btok, xcd], writes=[(PSB, 1 + g)])
                b.add("vector", lambda e: e.tensor_tensor(out=state[:].rearrange("p (h d) -> p h d", d=64), in0=state[:].rearrange("p (h d) -> p h d", d=64),
                                                          in1=sm(10).unsqueeze(2).to_broadcast([128, 16, 64]), op=ALU.mult),
                      reads=[state, smk(10)], writes=[state])
                b.add("vector", lambda e: e.tensor_tensor(out=state[:], in0=state[:], in1=PSB[:, 1:3, :].rearrange("p q n -> p (q n)"), op=ALU.add),
                      reads=[state, (PSB, 1), (PSB, 2)], writes=[state])
                b.add("scalar", lambda e: e.activation(out=statebf[:], in_=state[:], func=AF.Identity), reads=[state], writes=[statebf])
                if SSD_CUT is not None and SSD_CUT < 9:
                    continue
                b.add("scalar", lambda e: e.activation(out=tmp[:], in_=y1[:], func=AF.Square), reads=[y1], writes=[tmp])
                for g in range(2):
                    b.add("vector", lambda e, g=g: e.reduce_sum(out=small[:, 192 + g:193 + g], in_=tmp[:, g * 512:(g + 1) * 512], axis=AX.X),
                          reads=[tmp], writes=[(small, 12)])
                b.add("scalar", lambda e: e.activation(out=small[:, 194:196], in_=small[:, 192:194], func=AF.Sqrt, scale=1.0 / 512.0, bias=P.eps5[:, 0:1]),
                      reads=[(small, 12), P.eps5], writes=[(small, 12)])
                b.add("vector", lambda e: e.reciprocal(out=small[:, 196:198], in_=small[:, 194:196]), reads=[(small, 12)], writes=[(small, 12)])
                for g in range(2):
                    b.add("scalar", lambda e, g=g: e.activation(out=yn[:, g * 512:(g + 1) * 512], in_=y1[:, g * 512:(g + 1) * 512], func=AF.Identity,
                                                                scale=small[:, 196 + g:197 + g]),
                          reads=[y1, (small, 12)], writes=[(yn, g)])
                for q in range(8):
                    b.add("tensor", lambda e, q=q: e.matmul(PSB[:, 1 + q // 4, (q % 4) * 128:(q % 4) * 128 + 128], lhsT=yn[:, q * 128:(q + 1) * 128], rhs=ident[:],
                                                           start=True, stop=True),
                          reads=[(yn, q // 4), ident], writes=[(PSB, 1 + q // 4)])
                b.add("vector", lambda e, tk=tk: e.tensor_tensor(out=ysT[:, :, tk], in0=PSB[:, 1:3, :].rearrange("p q (c t) -> p (q c) t", t=128),
                                                               in1=vec[:, NW:NW + 8].unsqueeze(2).to_broadcast([128, 8, 128]), op=ALU.mult),
                      reads=[(PSB, 1), (PSB, 2), vec], writes=[(ysT, cc)])
            for q in range(8):
                b.add("sync", lambda e, q=q, G=G: e.dma_start(out=P.ymix[q][:, G * 512:(G + 1) * 512], in_=ysT[:, q, :]),
                      reads=[(ysT, c4) for c4 in range(4)], writes=[("ymix", (q, G))], dma=True)


QSCALE = 192.0 ** -0.5


def mla1_phase(b, P, W):
    vec = P.vec[0]
    QW, KW = 188, 191
    with b.phase():
        hT = b.sbuf("m1_hT", [128, 8, 512], BF16)
        Wl = b.sbuf("m1_Wl", [128, 8, 704], BF16)
        rope = b.sbuf("m1_rope", [64, 2, S], F32)
        Rm = b.sbuf("m1_Rm", [64, 64], BF16)
        onq = b.sbuf("m1_onq", [128, 128], BF16)
        onk = b.sbuf("m1_onk", [128, 128], BF16)
        lat = b.sbuf("m1_lat", [128, 5, 512], F32)
        sq = b.sbuf("m1_sq", [128, 5, 512], BF16)
        krf = b.sbuf("m1_krf", [64, 512], F32)
        krh = b.sbuf("m1_krh", [64, 512], BF16)
        krl = b.sbuf("m1_krl", [64, 512], BF16)
        t1 = b.sbuf("m1_t1", [64, 512], F32)
        t2 = b.sbuf("m1_t2", [64, 512], F32)
        rs = b.sbuf("m1_rs", [128, 2, 512], F32)
        outn = b.sbuf("m1_outn", [128, 5, 512], BF16)
        kpo = b.sbuf("m1_kpo", [64, 512], BF16)
        PSA, PSB = P.PSA, P.PSB
        for k in range(8):
            b.add("gpsimd", lambda e, k=k: e.dma_start(out=Wl[:, k, :], in_=W["wlat"][k]), reads=[], writes=[(Wl, k)], dma=True)
        b.add("sync", lambda e: e.dma_start(out=rope[:], in_=W["rope"][:]), reads=[], writes=[rope], dma=True)
        b.add("gpsimd", lambda e: e.dma_start(out=Rm[:], in_=P.cst_d[0:64, 384:448]), reads=[], writes=[Rm], dma=True)
        b.add("vector", lambda e: e.memset(onq[:], 1.0 / 384.0), reads=[], writes=[onq])
        b.add("vector", lambda e: e.memset(onk[:], 1.0 / 256.0), reads=[], writes=[onk])
        for G in range(4):
            ts = slice(G * 512, (G + 1) * 512)
            make_h(b, P, 1, hT, G * 512, 512)
            for c in range(5):
                bk = c % 2
                for k in range(8):
                    b.add("tensor", lambda e, c=c, k=k, bk=bk: e.matmul(PSB[:, bk, :], lhsT=Wl[:, k, c * 128:(c + 1) * 128], rhs=hT[:, k, :],
                                                                       start=(k == 0), stop=(k == 7)),
                          reads=[(Wl, k), (hT, k)], writes=[(PSB, bk)])
                b.add("scalar", lambda e, c=c, bk=bk: e.activation(out=lat[:, c, :], in_=PSB[:, bk, :], func=AF.Identity),
                      reads=[(PSB, bk)], writes=[(lat, c)])
                b.add("scalar", lambda e, c=c: e.activation(out=sq[:, c, :], in_=lat[:, c, :], func=AF.Square),
                      reads=[(lat, c)], writes=[(sq, c)])
            for k in range(8):
                b.add("tensor", lambda e, k=k: e.matmul(PSB[0:64, 2, :], lhsT=Wl[:, k, 640:704], rhs=hT[:, k, :], start=(k == 0), stop=(k == 7)),
                      reads=[(Wl, k), (hT, k)], writes=[(PSB, 2)])
            b.add("scalar", lambda e: e.activation(out=krf[:], in_=PSB[0:64, 2, :], func=AF.Identity), reads=[(PSB, 2)], writes=[krf])
            for (c0, nchunk, on, bank, wcol, r) in ((0, 3, onq, 0, QW, 0), (3, 2, onk, 1, KW, 1)):
                for c in range(nchunk):
                    b.add("tensor", lambda e, c=c, c0=c0, on=on, bank=bank, nchunk=nchunk: e.matmul(PSA[:, bank, :], lhsT=on[:], rhs=sq[:, c0 + c, :],
                                                                                                 start=(c == 0), stop=(c == nchunk - 1)),
                          reads=[on, (sq, c0 + c)], writes=[(PSA, bank)])
                b.add("scalar", lambda e, bank=bank, r=r: e.activation(out=rs[:, r, :], in_=PSA[:, bank, :], func=AF.Sqrt, bias=P.eps5[:, 0:1]),
                      reads=[(PSA, bank), P.eps5], writes=[(rs, r)])
                b.add("vector", lambda e, r=r: e.reciprocal(out=rs[:, r, :], in_=rs[:, r, :]), reads=[(rs, r)], writes=[(rs, r)])
                for c in range(nchunk):
                    b.add("vector", lambda e, c=c, c0=c0, wcol=wcol, r=r: e.scalar_tensor_tensor(
                        out=outn[:, c0 + c, :], in0=lat[:, c0 + c, :], scalar=vec[:, wcol + c:wcol + c + 1], in1=rs[:, r, :],
                        op0=ALU.mult, op1=ALU.mult),
                          reads=[(lat, c0 + c), vec, (rs, r)], writes=[(outn, c0 + c)])
            b.add("scalar", lambda e: e.activation(out=krh[:], in_=krf[:], func=AF.Identity), reads=[krf], writes=[krh])
            b.add("vector", lambda e: e.tensor_tensor(out=krl[:], in0=krf[:], in1=krh[:], op=ALU.subtract), reads=[krf, krh], writes=[krl])
            b.add("tensor", lambda e: e.matmul(PSA[0:64, 2, :], lhsT=Rm[:], rhs=krh[:], start=True, stop=False), reads=[Rm, krh], writes=[(PSA, 2)])
            b.add("tensor", lambda e: e.matmul(PSA[0:64, 2, :], lhsT=Rm[:], rhs=krl[:], start=False, stop=True), reads=[Rm, krl], writes=[(PSA, 2)])
            b.add("vector", lambda e, ts=ts: e.tensor_tensor(out=t1[:], in0=krf[:], in1=rope[:, 0, ts], op=ALU.mult), reads=[krf, rope], writes=[t1])
            b.add("vector", lambda e, ts=ts: e.tensor_tensor(out=t2[:], in0=PSA[0:64, 2, :], in1=rope[:, 1, ts], op=ALU.mult), reads=[(PSA, 2), rope], writes=[t2])
            b.add("vector", lambda e: e.tensor_tensor(out=kpo[:], in0=t1[:], in1=t2[:], op=ALU.add), reads=[t1, t2], writes=[kpo])
            for c in range(5):
                b.add("sync", lambda e, c=c, ts=ts: e.dma_start(out=P.mlat[c][:, ts], in_=outn[:, c, :]), reads=[(outn, c)], writes=[("mlat", (c, G))], dma=True)
            b.add("sync", lambda e, ts=ts: e.dma_start(out=P.mlat[5][0:64, ts], in_=kpo[:]), reads=[kpo], writes=[("mlat", (5, G))], dma=True)


def mla2_phase(b, P, W):
    with b.phase():
        Wq = b.sbuf("m2_Wq", [128, 3, 1536], BF16)
        Wkv = b.sbuf("m2_Wkv", [128, 2, 2048], BF16)
        rope = b.sbuf("m2_rope", [64, 2, S], F32)
        Rm = b.sbuf("m2_Rm", [64, 64], BF16)
        trib = b.sbuf("m2_trib", [128, 128], BF16)
        ones1 = b.sbuf("m2_ones1", [128, 128], BF16)
        cqn = b.sbuf("m2_cqn", [128, 3, S], BF16)
        ckvn = b.sbuf("m2_ckvn", [128, 2, S], BF16)
        kpe = b.sbuf("m2_kpe", [65, S], BF16)
        sqkpe = b.sbuf("m2_sqkpe", [64, S], BF16)
        QnT = b.sbuf("m2_QnT", [128, S], BF16)
        QpT = b.sbuf("m2_QpT", [65, S], BF16)
        KnT = b.sbuf("m2_KnT", [128, S], BF16)
        V = b.sbuf("m2_V", [128, 16, 128], BF16)
        qn2s = b.sbuf("m2_qn2s", [65, S], F32)
        sqa_ = [b.sbuf("m2_sqa%d" % r, [128, 512], BF16) for r in range(2)]
        sqk_ = [b.sbuf("m2_sqk%d" % r, [128, 512], BF16) for r in range(2)]
        sqb_ = [b.sbuf("m2_sqb%d" % r, [64, 512], BF16) for r in range(2)]
        qpf_ = [b.sbuf("m2_qpf%d" % r, [64, 512], F32) for r in range(2)]
        qph_ = [b.sbuf("m2_qph%d" % r, [64, 512], BF16) for r in range(2)]
        qpl_ = [b.sbuf("m2_qpl%d" % r, [64, 512], BF16) for r in range(2)]
        t1_ = [b.sbuf("m2_t1%d" % r, [64, 512], F32) for r in range(2)]
        t2_ = [b.sbuf("m2_t2%d" % r, [64, 512], F32) for r in range(2)]
        kmx = b.sbuf("m2_kmx", [128, 8], F32)
        crow = b.sbuf("m2_crow", [65, 512], F32)
        PT = [b.sbuf("m2_PT%d" % r, [128, 512], BF16) for r in range(2)]
        rr = b.sbuf("m2_rr", [128, 128], F32)
        yatt = b.sbuf("m2_yatt", [128, S], BF16)
        PSA, PSB = P.PSA, P.PSB
        for c in range(3):
            b.add("gpsimd", lambda e, c=c: e.dma_start(out=Wq[:, c, :], in_=W["wuq"][c]), reads=[], writes=[(Wq, c)], dma=True)
            b.add("sync", lambda e, c=c: e.dma_start(out=cqn[:, c, :], in_=P.mlat[c][:, :]), reads=[("mlat", None)], writes=[(cqn, c)], dma=True)
        for c in range(2):
            b.add("gpsimd", lambda e, c=c: e.dma_start(out=Wkv[:, c, :], in_=W["wukv"][c]), reads=[], writes=[(Wkv, c)], dma=True)
            b.add("sync", lambda e, c=c: e.dma_start(out=ckvn[:, c, :], in_=P.mlat[3 + c][:, :]), reads=[("mlat", None)], writes=[(ckvn, c)], dma=True)
        b.add("sync", lambda e: e.dma_start(out=kpe[0:64, :], in_=P.mlat[5][0:64, :]), reads=[("mlat", None)], writes=[(kpe, 0)], dma=True)
        b.add("sync", lambda e: e.dma_start(out=rope[:], in_=W["rope"][:]), reads=[], writes=[rope], dma=True)
        b.add("gpsimd", lambda e: e.dma_start(out=Rm[:], in_=P.cst_d[0:64, 384:448]), reads=[], writes=[Rm], dma=True)
        b.add("gpsimd", lambda e: e.dma_start(out=trib[:], in_=P.cst_d[:, 128:256]), reads=[], writes=[trib], dma=True)
        b.add("vector", lambda e: e.memset(ones1[:], 1.0), reads=[], writes=[ones1])
        b.add("vector", lambda e: e.memset(kpe[64:65, :], 1.0), reads=[], writes=[(kpe, 1)])
        b.add("scalar", lambda e: e.activation(out=sqkpe[:], in_=kpe[0:64, :], func=AF.Square), reads=[(kpe, 0)], writes=[sqkpe])

        for h in range(8):
            for n in range(4):
                ts = slice(n * 512, (n + 1) * 512)
                r = n % 2
                sqa, sqk, sqb, qpf, qph, qpl, t1, t2 = sqa_[r], sqk_[r], sqb_[r], qpf_[r], qph_[r], qpl_[r], t1_[r], t2_[r]
                for c in range(2):
                    b.add("tensor", lambda e, c=c, h=h, ts=ts: e.matmul(PSB[:, 0, :], lhsT=Wkv[:, c, h * 256:h * 256 + 128], rhs=ckvn[:, c, ts],
                                                                       start=(c == 0), stop=(c == 1)),
                          reads=[(Wkv, c), (ckvn, c)], writes=[(PSB, 0)])
                for c in range(3):
                    b.add("tensor", lambda e, c=c, h=h, ts=ts: e.matmul(PSB[:, 1, :], lhsT=Wq[:, c, h * 192:h * 192 + 128], rhs=cqn[:, c, ts],
                                                                       start=(c == 0), stop=(c == 2)),
                          reads=[(Wq, c), (cqn, c)], writes=[(PSB, 1)])
                for c in range(3):
                    b.add("tensor", lambda e, c=c, h=h, ts=ts: e.matmul(PSB[0:64, 2, :], lhsT=Wq[:, c, h * 192 + 128:h * 192 + 192], rhs=cqn[:, c, ts],
                                                                       start=(c == 0), stop=(c == 2)),
                          reads=[(Wq, c), (cqn, c)], writes=[(PSB, 2)])
                for blk in range(4):
                    tb = slice(n * 512 + blk * 128, n * 512 + (blk + 1) * 128)
                    for c in range(2):
                        b.add("tensor", lambda e, c=c, h=h, tb=tb, blk=blk: e.matmul(PSA[:, 0, blk * 128:(blk + 1) * 128], lhsT=ckvn[:, c, tb],
                                                                                  rhs=Wkv[:, c, h * 256 + 128:h * 256 + 256], start=(c == 0), stop=(c == 1)),
                              reads=[(ckvn, c), (Wkv, c)], writes=[(PSA, 0)])
                b.add("scalar", lambda e, ts=ts: e.activation(out=KnT[:, ts], in_=PSB[:, 0, :], func=AF.Identity), reads=[(PSB, 0)], writes=[(KnT, n)])
                b.add("scalar", lambda e, ts=ts: e.activation(out=QnT[:, ts], in_=PSB[:, 1, :], func=AF.Identity, scale=QSCALE), reads=[(PSB, 1)], writes=[(QnT, n)])
                b.add("scalar", lambda e, qpf=qpf: e.activation(out=qpf[:], in_=PSB[0:64, 2, :], func=AF.Identity, scale=QSCALE), reads=[(PSB, 2)], writes=[qpf])
                b.add("scalar", lambda e, n=n: e.activation(out=V[:, n * 4:(n + 1) * 4, :], in_=PSA[:, 0, :].rearrange("p (b d) -> p b d", d=128), func=AF.Identity),
                      reads=[(PSA, 0)], writes=[(V, n)])
                b.add("scalar", lambda e, ts=ts, sqk=sqk: e.activation(out=sqk[:], in_=KnT[:, ts], func=AF.Square), reads=[(KnT, n)], writes=[sqk])
                b.add("tensor", lambda e, sqk=sqk: e.matmul(PSA[:, 3, :], lhsT=ones1[:], rhs=sqk[:], start=True, stop=False), reads=[ones1, sqk], writes=[(PSA, 3)])
                b.add("tensor", lambda e, ts=ts: e.matmul(PSA[:, 3, :], lhsT=ones1[0:64, :], rhs=sqkpe[:, ts], start=False, stop=True),
                      reads=[ones1, sqkpe], writes=[(PSA, 3)])
                b.add("vector", lambda e, n=n: e.reduce_max(out=kmx[:, n:n + 1], in_=PSA[:, 3, :], axis=AX.X), reads=[(PSA, 3)], writes=[(kmx, n)])
                b.add("scalar", lambda e, ts=ts, sqa=sqa: e.activation(out=sqa[:], in_=QnT[:, ts], func=AF.Square), reads=[(QnT, n)], writes=[sqa])
                b.add("scalar", lambda e, qpf=qpf, qph=qph: e.activation(out=qph[:], in_=qpf[:], func=AF.Identity), reads=[qpf], writes=[qph])
                b.add("vector", lambda e, qpf=qpf, qph=qph, qpl=qpl: e.tensor_tensor(out=qpl[:], in0=qpf[:], in1=qph[:], op=ALU.subtract), reads=[qpf, qph], writes=[qpl])
                b.add("tensor", lambda e, qph=qph: e.matmul(PSA[0:64, 2, :], lhsT=Rm[:], rhs=qph[:], start=True, stop=False), reads=[Rm, qph], writes=[(PSA, 2)])
                b.add("tensor", lambda e, qpl=qpl: e.matmul(PSA[0:64, 2, :], lhsT=Rm[:], rhs=qpl[:], start=False, stop=True), reads=[Rm, qpl], writes=[(PSA, 2)])
                b.add("vector", lambda e, ts=ts, qpf=qpf, t1=t1: e.tensor_tensor(out=t1[:], in0=qpf[:], in1=rope[:, 0, ts], op=ALU.mult), reads=[qpf, rope], writes=[t1])
                b.add("vector", lambda e, ts=ts, t2=t2: e.tensor_tensor(out=t2[:], in0=PSA[0:64, 2, :], in1=rope[:, 1, ts], op=ALU.mult), reads=[(PSA, 2), rope], writes=[t2])
                b.add("vector", lambda e, ts=ts, t1=t1, t2=t2: e.tensor_tensor(out=QpT[0:64, ts], in0=t1[:], in1=t2[:], op=ALU.add), reads=[t1, t2], writes=[(QpT, n)])
                b.add("scalar", lambda e, ts=ts, sqb=sqb: e.activation(out=sqb[:], in_=QpT[0:64, ts], func=AF.Square), reads=[(QpT, n)], writes=[sqb])
                b.add("tensor", lambda e, sqa=sqa: e.matmul(PSA[:, 1, :], lhsT=ones1[:], rhs=sqa[:], start=True, stop=False), reads=[ones1, sqa], writes=[(PSA, 1)])
                b.add("tensor", lambda e, sqb=sqb: e.matmul(PSA[:, 1, :], lhsT=ones1[0:64, :], rhs=sqb[:], start=False, stop=True), reads=[ones1, sqb], writes=[(PSA, 1)])
                b.add("scalar", lambda e, ts=ts: e.activation(out=qn2s[64:65, ts], in_=PSA[64:65, 1, :], func=AF.Identity), reads=[(PSA, 1)], writes=[(qn2s, n)])
            b.add("vector", lambda e: e.reduce_max(out=kmx[:, 4:5], in_=kmx[:, 0:4], axis=AX.X), reads=[(kmx, n) for n in range(4)], writes=[(kmx, 4)])
            for n in range(4):
                ts = slice(n * 512, (n + 1) * 512)
                b.add("scalar", lambda e, ts=ts: e.activation(out=crow[64:65, :], in_=qn2s[64:65, ts], func=AF.Sqrt, scale=kmx[64:65, 4:5]),
                      reads=[(qn2s, n), (kmx, 4)], writes=[crow])
                b.add("vector", lambda e, ts=ts: e.tensor_scalar(out=QpT[64:65, ts], in0=crow[64:65, :], scalar1=-1.0, scalar2=None, op0=ALU.mult),
                      reads=[crow], writes=[(QpT, (n, "c"))])
            cnt = 0
            for i in range(16):
                qs = slice(i * 128, (i + 1) * 128)
                ab = 2 * (i % 2)
                for jg in range(0, i + 1, 4):
                    js = list(range(jg, min(jg + 4, i + 1)))
                    bk = cnt % 2
                    pt = PT[cnt % 2]
                    cnt += 1
                    for jj, j in enumerate(js):
                        ks = slice(j * 128, (j + 1) * 128)
                        b.add("tensor", lambda e, jj=jj, ks=ks, qs=qs, bk=bk: e.matmul(PSB[:, bk, jj * 128:(jj + 1) * 128], lhsT=KnT[:, ks], rhs=QnT[:, qs],
                                                                                    start=True, stop=False),
                              reads=[(KnT, j // 4), (QnT, i // 4)], writes=[(PSB, bk)])
                        b.add("tensor", lambda e, jj=jj, ks=ks, qs=qs, bk=bk: e.matmul(PSB[:, bk, jj * 128:(jj + 1) * 128], lhsT=kpe[0:65, ks], rhs=QpT[0:65, qs],
                                                                                    start=False, stop=True),
                              reads=[(kpe, 0), (kpe, 1), (QpT, i // 4), (QpT, (i // 4, "c"))], writes=[(PSB, bk)])
                    nb = len(js)
                    b.add("scalar", lambda e, pt=pt, bk=bk, nb=nb: e.activation(out=pt[:, 0:nb * 128], in_=PSB[:, bk, 0:nb * 128], func=AF.Exp),
                          reads=[(PSB, bk)], writes=[pt])
                    if js[-1] == i:
                        jj = len(js) - 1
                        b.add("vector", lambda e, pt=pt, jj=jj: e.tensor_tensor(out=pt[:, jj * 128:(jj + 1) * 128], in0=pt[:, jj * 128:(jj + 1) * 128],
                                                                              in1=trib[:], op=ALU.mult),
                              reads=[pt, trib], writes=[pt])
                    for jj, j in enumerate(js):
                        b.add("tensor", lambda e, pt=pt, jj=jj, j=j, i=i, ab=ab: e.matmul(PSA[:, ab, 0:128], lhsT=V[:, j, :], rhs=pt[:, jj * 128:(jj + 1) * 128],
                                                                                       start=(j == 0), stop=(j == i)),
                              reads=[(V, j // 4), pt], writes=[(PSA, ab)])
                        b.add("tensor", lambda e, pt=pt, jj=jj, j=j, i=i, ab=ab: e.matmul(PSA[:, ab + 1, 0:128], lhsT=ones1[:], rhs=pt[:, jj * 128:(jj + 1) * 128],
                                                                                       start=(j == 0), stop=(j == i)),
                              reads=[ones1, pt], writes=[(PSA, ab + 1)])
                b.add("vector", lambda e, ab=ab: e.reciprocal(out=rr[:], in_=PSA[:, ab + 1, 0:128]), reads=[(PSA, ab + 1)], writes=[rr])
                b.add("vector", lambda e, ab=ab, qs=qs: e.tensor_tensor(out=yatt[:, qs], in0=PSA[:, ab, 0:128], in1=rr[:], op=ALU.mult),
                      reads=[(PSA, ab), rr], writes=[(yatt, i)])
            b.add("sync", lambda e, h=h: e.dma_start(out=P.ymix[8 + h][:, :], in_=yatt[:]), reads=[yatt], writes=[("ymix", (8 + h, 0))], dma=True)


def mixout_phase(b, P, L, wo):
    with b.phase():
        ybs = [b.sbuf("mo_yb%d" % r, [128, 16, 512], BF16) for r in range(2)]
        wob = [b.sbuf("mo_wo%d" % r, [128, 16, 128], BF16) for r in range(3)]
        ln = LNState(b, P, nset=2)
        PSB = P.PSB
        pend = []

        def load_wo(jj):
            t = wob[jj % 3]
            b.add("gpsimd", lambda e: e.dma_start(out=t[:].rearrange("p i c -> p (i c)"), in_=wo[jj % 8]), reads=[], writes=[t], dma=True)

        for n in range(4):
            ts = slice(n * 512, (n + 1) * 512)
            yb = ybs[n % 2]
            for c in range(16):
                b.add("sync", lambda e, c=c, ts=ts, yb=yb: e.dma_start(out=yb[:, c, :], in_=P.ymix[c][:, ts]), reads=[("ymix", None)], writes=[(yb, c)], dma=True)
            for j in range(3):
                load_wo(n * 8 + j)
            for j in range(8):
                w = wob[(n * 8 + j) % 3]
                ko = j % 2
                for c in range(16):
                    b.add("tensor", lambda e, w=w, c=c, ko=ko, yb=yb: e.matmul(PSB[:, ko, :], lhsT=w[:, c, :], rhs=yb[:, c, :], start=(c == 0), stop=(c == 15)),
                          reads=[w, (yb, c)], writes=[(PSB, ko)])
                emit_specs(b, pend, 4)
                residual_evac(b, P, ln, 1, PSB, ko, j, n, (PSB, 2), (PSB, 3))
                if j + 3 < 8:
                    load_wo(n * 8 + j + 3)
            emit_specs(b, pend)
            pend += ln_finish(b, P, ln, L, 1, n, (PSB, 2), (PSB, 3), sidx=n % 2, defer=(n < 3))

def dump_ymix(b, P, q0):
    with b.phase():
        t = b.sbuf("dbg_y", [128, S], BF16)
        for q in range(8):
            b.add("sync", lambda e, q=q: e.dma_start(out=t[:], in_=P.ymix[q0 + q][:, :]), reads=[("ymix", None)], writes=[t], dma=True)
            b.add("vector", lambda e, q=q: e.tensor_copy(out=P.xT[:, q, :], in_=t[:]), reads=[t], writes=[(P.xT, (q, n)) for n in range(4)])


def build_program(stages=None, dbg_modv=False, dump_modv=False):
    if stages is None:
        stages = []
        for L in range(2):
            stages += [("ada", L), ("ffa", L), ("mix", L), ("ffb", L)]
    b = Builder()
    nc = b.nc
    P = Prog()
    xT_d = b.dram("xT", [128, 8, S], F32, kind="ExternalInput")
    cT_d = b.dram("cT", [128, 8], F32, kind="ExternalInput")
    P.cst_d = b.dram("cst", [128, 512], F32, kind="ExternalInput")
    P.adaw, P.vec_d = {}, {}
    ffw = {}
    mixw = {}
    for L in range(2):
        P.vec_d[L] = b.dram("vec%d" % L, [128, 256], F32, kind="ExternalInput")
        if ("ada", L) in stages:
            P.adaw[L] = b.dram("adaw%d" % L, [8, 128, 9216], F32, kind="ExternalInput")
        for nm in ("ffa", "ffb"):
            if (nm, L) in stages:
                ffw[(nm, L)] = (b.dram("%s_in%d" % (nm, L), [NFF, 128, 2048], F32, kind="ExternalInput"),
                                b.dram("%s_out%d" % (nm, L), [8, 128, DFF], F32, kind="ExternalInput"))
    if ("mix", 1) in stages:
        mixw[1] = {
            "wu": b.dram("sg_wu", [16, 128, 1024], F32, kind="ExternalInput"),
            "wv": b.dram("sg_wv", [8, 128, 2048], F32, kind="ExternalInput"),
            "wo": b.dram("sg_wo", [8, 128, 2048], F32, kind="ExternalInput"),
            "wsT": b.dram("sg_wsT", [128, 2048], F32, kind="ExternalInput"),
            "grep": b.dram("sg_grep", [128, 2048], F32, kind="ExternalInput"),
            "bsb": b.dram("sg_bsb", [128, 2048], F32, kind="ExternalInput"),
            "bvrow": b.dram("sg_bvrow", [1, 2048], F32, kind="ExternalInput"),
        }
    kinds0 = [k for (k, L) in stages if L == 0]
    if any(k in ("mix", "ssd", "mla") for k in kinds0):
        mixw[0] = {
            "wz": b.dram("ev_wz", [8, 128, 1024], F32, kind="ExternalInput"),
            "wxbc": b.dram("ev_wxbc", [8, 128, 1536], F32, kind="ExternalInput"),
            "wdt": b.dram("ev_wdt", [8, 128, 16], F32, kind="ExternalInput"),
            "tokc": b.dram("ev_tokc", [128, 1056], F32, kind="ExternalInput"),
            "wlat": b.dram("ev_wlat", [8, 128, 704], F32, kind="ExternalInput"),
            "wuq": b.dram("ev_wuq", [3, 128, 1536], F32, kind="ExternalInput"),
            "wukv": b.dram("ev_wukv", [2, 128, 2048], F32, kind="ExternalInput"),
            "rope": b.dram("ev_rope", [64, 2, S], F32, kind="ExternalInput"),
            "wo": b.dram("ev_wo", [8, 128, 2048], F32, kind="ExternalInput"),
        }
        P.ymix = b.dram("ymix", [16, 128, S], BF16, kind="Internal")
        P.mlat = b.dram("mlat", [6, 128, S], BF16, kind="Internal")
    if dbg_modv:
        modv_d = b.dram("modv_dbg", [128, 72], F32, kind="ExternalInput")
    y_d = b.dram("yT", [128, 8, S], F32, kind="ExternalOutput")

    P.xT = b.sbuf("xT_sb", [128, 8, S], F32, persistent=True)
    P.cT = b.sbuf("cT_sb", [128, 8], F32, persistent=True)
    P.vec = [b.sbuf("vec_sb%d" % L, [128, 256], F32, persistent=True) for L in range(2)]
    P.modv = b.sbuf("modv", [128, 72], F32, persistent=True)
    P.dvA = b.sbuf("dvA", [128, 24], F32, persistent=True)
    P.dvG = b.sbuf("dvG", [128, 24], F32, persistent=True)
    P.onesb = b.sbuf("onesb", [128, 128], BF16, persistent=True)
    P.epsln = b.sbuf("epsln", [128, 1], F32, persistent=True)
    P.eps5 = b.sbuf("eps5", [128, 1], F32, persistent=True)
    P.PSA = b.psum("PSA", [128, 4, 512], F32, persistent=True)
    P.PSB = b.psum("PSB", [128, 4, 512], F32, persistent=True)
    b.exclusive = {"PSA", "PSB"}

    for k in range(8):
        b.add("sync", lambda e, k=k: e.dma_start(out=P.xT[:, k, :], in_=xT_d[:, k, :]),
              reads=[], writes=[(P.xT, (k, n)) for n in range(4)], dma=True)
    b.add("sync", lambda e: e.dma_start(out=P.cT[:], in_=cT_d[:]), reads=[], writes=[P.cT], dma=True)
    for L in range(2):
        b.add("sync", lambda e, L=L: e.dma_start(out=P.vec[L][:], in_=P.vec_d[L][:]), reads=[], writes=[P.vec[L]], dma=True)
    b.add("vector", lambda e: e.memset(P.onesb[:], 1.0 / 1024.0), reads=[], writes=[P.onesb])
    b.add("vector", lambda e: e.memset(P.epsln[:], EPS_LN), reads=[], writes=[P.epsln])
    b.add("vector", lambda e: e.memset(P.eps5[:], EPS), reads=[], writes=[P.eps5])
    if dbg_modv:
        b.add("sync", lambda e: e.dma_start(out=P.modv[:], in_=modv_d[:]), reads=[], writes=[P.modv], dma=True)
        derive_vecs(b, P)
    b.barrier()

    for (kind, L) in stages:
        if kind == "ada":
            ada_phase(b, P, L)
        elif kind == "ffa":
            ffn_phase(b, P, L, 0, *ffw[("ffa", L)])
        elif kind == "ffb":
            ffn_phase(b, P, L, 2, *ffw[("ffb", L)])
        elif kind == "mix" and L == 1:
            sgu_phase(b, P, L, mixw[1])
        elif kind == "ssd":
            ssd_phase(b, P, mixw[0])
            dump_ymix(b, P, 0)
        elif kind == "mla":
            mla1_phase(b, P, mixw[0])
            mla2_phase(b, P, mixw[0])
            dump_ymix(b, P, 8)
        elif kind == "mix" and L == 0:
            ssd_phase(b, P, mixw[0])
            mla1_phase(b, P, mixw[0])
            mla2_phase(b, P, mixw[0])
            mixout_phase(b, P, 0, mixw[0]["wo"])


    if dump_modv:
        b.add("vector", lambda e: e.tensor_copy(out=P.xT[:, 0, 0:72], in_=P.modv[:]), reads=[P.modv], writes=[(P.xT, (0, 0))])
    for k in range(8):
        b.add("sync", lambda e, k=k: e.dma_start(out=y_d[:, k, :], in_=P.xT[:, k, :]),
              reads=[(P.xT, (k, n)) for n in range(4)], writes=[], dma=True, out=True)
    nc = b.finish()
    return nc, b


def _fm(v):
    v = np.asarray(v, dtype=np.float32)
    return np.ascontiguousarray(v.reshape(-1, 128).T)


def prep_shared(inp):
    sh = {}
    cst = np.zeros((128, 512), np.float32)
    cst[:, 0:128] = np.eye(128, dtype=np.float32)
    cst[:, 128:256] = np.triu(np.ones((128, 128), np.float32))
    cst[:, 256:384] = (1.0 - np.triu(np.ones((128, 128), np.float32))) * -30000.0
    for mm in range(32):
        cst[mm + 32, 384 + mm] = -1.0
        cst[mm, 384 + mm + 32] = 1.0
    sh["cst"] = cst
    for L in range(2):
        p = "l%d_" % L
        sh["adaw%d" % L] = np.ascontiguousarray(inp[p + "ada_w"].reshape(8, 128, 9216))
        vec = np.zeros((128, 256), np.float32)
        vec[:, 0:72] = _fm(inp[p + "ada_b"])
        vec[:, 72:96] = _fm(inp[p + "ln_g"].reshape(-1))
        vec[:, 96:120] = _fm(inp[p + "ln_b"].reshape(-1))
        sh["vec%d" % L] = vec
        for nm in ("ffa", "ffb"):
            w_in = inp[p + nm + "_w_in"]
            t = w_in.reshape(8, 128, 2, NFF, 128)
            sh[nm + "_in%d" % L] = np.ascontiguousarray(t.transpose(3, 1, 2, 0, 4).reshape(NFF, 128, 2048))
            w_out = inp[p + nm + "_w_out"]
            t = w_out.reshape(NFF, 128, 8, 128)
            sh[nm + "_out%d" % L] = np.ascontiguousarray(t.transpose(2, 1, 0, 3).reshape(8, 128, DFF))
    w_in = inp["l0_w_in"]
    sh["ev_wz"] = np.ascontiguousarray(w_in[:, 0:1024].reshape(8, 128, 1024))
    sh["ev_wxbc"] = np.ascontiguousarray(w_in[:, 1024:2560].reshape(8, 128, 1536))
    sh["ev_wdt"] = np.ascontiguousarray(w_in[:, 2560:2576].reshape(8, 128, 16))
    sh["ev_wlat"] = np.ascontiguousarray(w_in[:, 2576:3280].reshape(8, 128, 704))
    sh["ev_wuq"] = np.ascontiguousarray(inp["l0_w_uq"].reshape(3, 128, 1536))
    sh["ev_wukv"] = np.ascontiguousarray(inp["l0_w_ukv"].reshape(2, 128, 2048))
    sh["ev_wo"] = np.ascontiguousarray(inp["l0_w_out"].reshape(16, 128, 8, 128).transpose(2, 1, 0, 3).reshape(8, 128, 2048))
    inv = (1.0 / (np.float32(10000.0) ** (np.arange(0, 64, 2, dtype=np.float32) / np.float32(64.0)))).astype(np.float32)
    ang = np.arange(S, dtype=np.float32)[:, None] * inv[None, :]
    cosT = np.cos(ang).astype(np.float32).T
    sinT = np.sin(ang).astype(np.float32).T
    rope = np.zeros((64, 2, S), np.float32)
    rope[0:32, 0] = cosT
    rope[32:64, 0] = cosT
    rope[0:32, 1] = sinT
    rope[32:64, 1] = sinT
    sh["ev_rope"] = rope
    tokc = np.zeros((128, 1056), np.float32)
    tokc[:, 0:1024] = np.repeat(inp["l0_d_skip"], 64)[None, :]
    tokc[:, 1024:1040] = inp["l0_dt_bias"][None, :]
    tokc[:, 1040:1056] = inp["l0_a_log"][None, :]
    sh["ev_tokc"] = tokc
    v0 = sh["vec0"]
    v0[:, 120:168] = np.concatenate([_fm(inp["l0_conv_w"][k]) for k in range(4)], axis=1)
    v0[:, 168:180] = _fm(inp["l0_conv_b"])
    v0[:, 180:188] = _fm(inp["l0_ssd_norm_w"])
    v0[:, 188:191] = _fm(inp["l0_q_norm_w"])
    v0[:, 191:193] = _fm(inp["l0_kv_norm_w"])
    w_uv = inp["l1_w_uv"]
    sh["sg_wu"] = np.ascontiguousarray(w_uv[:, :2048].reshape(8, 128, 16, 128).transpose(2, 1, 0, 3).reshape(16, 128, 1024))
    sh["sg_wv"] = np.ascontiguousarray(w_uv[:, 2048:].reshape(8, 128, 2048))
    sh["sg_wo"] = np.ascontiguousarray(inp["l1_w_out"].reshape(16, 128, 8, 128).transpose(2, 1, 0, 3).reshape(8, 128, 2048))
    sh["sg_wsT"] = np.ascontiguousarray(inp["l1_w_s"].transpose(2, 0, 1).reshape(128, 2048))
    sh["sg_grep"] = np.ascontiguousarray(np.repeat(_fm(inp["l1_sgu_ln_g"]), 128, axis=1))
    sh["sg_bsb"] = np.ascontiguousarray(np.broadcast_to(inp["l1_b_s"].reshape(1, 2048), (128, 2048)))
    sh["sg_bvrow"] = np.ascontiguousarray(inp["l1_b_uv"][2048:].reshape(1, 2048))
    v1 = sh["vec1"]
    v1[:, 120:136] = _fm(inp["l1_b_uv"][:2048])
    v1[:, 152:168] = _fm(inp["l1_sgu_ln_b"])
    return sh


def prep_core(inp, bi):
    x = inp["x"][bi]
    xT = np.ascontiguousarray(x.reshape(S, 8, 128).transpose(2, 1, 0))
    cT = _fm(inp["c"][bi])
    return {"xT": xT, "cT": cT}


def kernel(**inputs):
    inp = {k: np.asarray(v) for k, v in inputs.items()}
    nc, _ = build_program()
    sh = prep_shared(inp)
    decl = set()
    for nm in list(sh.keys()) + ["xT", "cT"]:
        try:
            nc.lookup_mloc(nm)
            decl.add(nm)
        except Exception:
            pass
    in_maps = []
    for bi in range(8):
        m = dict(sh)
        m.update(prep_core(inp, bi))
        in_maps.append({k: v for k, v in m.items() if k in decl})
    res = run_bass_kernel_spmd(nc, in_maps, core_ids=list(range(8)))
    out = np.empty((8, S, D), np.float32)
    for bi in range(8):
        yT = res.results[bi]["yT"]
        out[bi] = yT.transpose(2, 1, 0).reshape(S, D)
    return out
```

```python
import math
import numpy as np
from contextlib import ExitStack, contextmanager
import concourse.bass as bass
import concourse.mybir as mybir
from concourse.bass_utils import run_bass_kernel_spmd

F32 = mybir.dt.float32
BF16 = mybir.dt.bfloat16
AF = mybir.ActivationFunctionType
ALU = mybir.AluOpType
AX = mybir.AxisListType

ENGS = ("tensor", "vector", "scalar", "gpsimd", "sync")
N_DMA_SEMS = 24
import os
SSD_CUT = int(os.environ['SSD_CUT']) if 'SSD_CUT' in os.environ else None
SSD_SUB = int(os.environ['SSD_SUB']) if 'SSD_SUB' in os.environ else None

D = 1024
S = 2048
DFF = 2816
NFF = 22
ALPHA = 4.0 ** 0.25
EPS = 1e-5
EPS_LN = EPS / (ALPHA * ALPHA)


class _Op:
    __slots__ = ("eng", "fn", "deps", "dma", "idx", "needed", "semval", "dsem", "epoch")


class Builder:
    def __init__(self):
        self.nc = bass.Bass("TRN2", target_bir_lowering=False)
        self.es = ExitStack()
        self.ops = []
        self.st = {}
        self.dma_last = [None] * N_DMA_SEMS
        self.dma_cnt = [0] * N_DMA_SEMS
        self.dma_rr = [0, 0]
        self.out_dmas = []
        self.last_on = {e: None for e in ENGS}
        self.bar = {}
        self.epoch = 0
        self.pes = None
        self.exclusive = set()

    def sbuf(self, name, shape, dtype, persistent=False):
        es = self.es if (persistent or self.pes is None) else self.pes
        if es is not self.es:
            name = "%s_p%d" % (name, self.epoch)
        return es.enter_context(self.nc.sbuf_tensor(name, list(shape), dtype))

    def psum(self, name, shape, dtype=F32, persistent=False):
        es = self.es if (persistent or self.pes is None) else self.pes
        return es.enter_context(self.nc.psum_tensor(name, list(shape), dtype))

    def dram(self, name, shape, dtype, kind="Internal"):
        return self.nc.dram_tensor(name, list(shape), dtype, kind=kind)

    @contextmanager
    def phase(self):
        self.pes = ExitStack()
        try:
            yield
        finally:
            self.pes.close()
            self.pes = None
            self.barrier()

    def barrier(self):
        snap = set()
        for e in ENGS:
            if self.last_on[e] is not None:
                snap.add(self.last_on[e])
        for s in range(N_DMA_SEMS):
            if self.dma_last[s] is not None:
                snap.add(self.dma_last[s])
        self.bar = {e: set(snap) for e in ENGS}
        self.st = {}
        self.epoch += 1

    @staticmethod
    def _norm(a):
        if isinstance(a, tuple):
            b, k = a
        else:
            b, k = a, None
        if not isinstance(b, str):
            b = b.name
        return b, k

    def _entries(self, b, k):
        d = self.st.setdefault(b, {})
        if k is None:
            return list(d.keys())
        out = []
        if k in d:
            out.append(k)
        if None in d:
            out.append(None)
        return out

    def add(self, eng, fn, reads=(), writes=(), dma=False, out=False):
        op = _Op()
        op.eng, op.fn, op.dma = eng, fn, dma
        op.idx = len(self.ops)
        op.needed = False
        op.semval = None
        op.dsem = None
        op.epoch = self.epoch
        deps = set()
        if self.bar.get(eng):
            deps |= self.bar.pop(eng)
        reads = [self._norm(a) for a in reads]
        writes = [self._norm(a) for a in writes]
        for (bb, kk) in list(reads):
            if bb in self.exclusive and (bb, kk) not in writes:
                writes.append((bb, kk))
        for b, k in reads:
            d = self.st.setdefault(b, {})
            for kk in self._entries(b, k):
                w = d[kk][0]
                if w is not None:
                    deps.add(w)
        for b, k in writes:
            d = self.st.setdefault(b, {})
            for kk in self._entries(b, k):
                w, rs = d[kk]
                if w is not None:
                    deps.add(w)
                deps.update(rs)
        for b, k in reads:
            d = self.st[b]
            if k not in d:
                d[k] = [None, []]
                if k is not None and None in d:
                    d[k][0] = d[None][0]
            d[k][1].append(op.idx)
        for b, k in writes:
            d = self.st[b]
            if k is None:
                d.clear()
                d[None] = [op.idx, []]
            else:
                d[k] = [op.idx, []]
        if dma:
            half = N_DMA_SEMS // 2
            pool = 1 if eng == "gpsimd" else 0
            s = pool * half + self.dma_rr[pool]
            self.dma_rr[pool] = (self.dma_rr[pool] + 1) % half
            if self.dma_last[s] is not None:
                deps.add(self.dma_last[s])
            self.dma_last[s] = op.idx
            self.dma_cnt[s] += 1
            op.dsem = s
            op.semval = 16 * self.dma_cnt[s]
            if out:
                self.out_dmas.append(op.idx)
        deps.discard(op.idx)
        op.deps = deps
        self.ops.append(op)
        self.last_on[eng] = op.idx
        return op.idx

    def finish(self):
        nc = self.nc
        last = _Op()
        last.eng, last.fn, last.dma = "sync", (lambda e: e.nop()), False
        last.idx = len(self.ops)
        last.needed = False
        last.semval = None
        last.dsem = None
        last.epoch = self.epoch
        last.deps = set(self.out_dmas)
        self.ops.append(last)
        ops = self.ops
        for op in ops:
            nd = set()
            for j in op.deps:
                pj = ops[j]
                if (not pj.dma) and pj.eng == op.eng and (not op.dma) and op.eng == "tensor":
                    continue
                nd.add(j)
            op.deps = nd
            for j in nd:
                ops[j].needed = True
        cnt = {}
        for op in ops:
            if op.dma:
                continue
            if op.needed:
                key = (op.eng, op.epoch)
                cnt[key] = cnt.get(key, 0) + 1
                op.semval = cnt[key]
        esem = {}
        for key in cnt:
            esem[key] = self.es.enter_context(nc.semaphore("s_%s_%d" % key))
        dsem = [self.es.enter_context(nc.semaphore("d_%d" % i)) for i in range(N_DMA_SEMS)]
        self.n_sems = len(esem) + N_DMA_SEMS

        def emit_engine(ename):
            def body(eng):
                waited = {}
                for op in ops:
                    if op.eng != ename:
                        continue
                    need = {}
                    for j in op.deps:
                        pj = ops[j]
                        if pj.dma:
                            key = ("d", pj.dsem)
                            sem = dsem[pj.dsem]
                        else:
                            key = (pj.eng, pj.epoch)
                            sem = esem[key]
                        v = pj.semval
                        if waited.get(key, 0) >= v:
                            continue
                        if key not in need or need[key][1] < v:
                            need[key] = (sem, v)
                    for key, (sem, v) in need.items():
                        eng.wait_ge(sem, v)
                        waited[key] = v
                    ins = op.fn(eng)
                    if op.dma:
                        ins.then_inc(dsem[op.dsem], 16)
                    elif op.needed:
                        ins.then_inc(esem[(op.eng, op.epoch)], 1)
            return body

        with nc.Block() as block:
            block.tensor(emit_engine("tensor"))
            block.vector(emit_engine("vector"))
            block.scalar(emit_engine("scalar"))
            block.gpsimd(emit_engine("gpsimd"))
            block.sync(emit_engine("sync"))
        self.es.close()
        return nc


class Prog:
    pass


def tiles_of(k, n0, n1):
    return [(k, n) for n in range(n0, n1)]


def ada_phase(b, P, L):
    adaw = P.adaw[L]
    vec = P.vec[L]
    with b.phase():
        scb = b.sbuf("ada_sc", [128, 8], BF16)
        b.add("scalar", lambda e: e.activation(out=scb[:], in_=P.cT[:], func=AF.Silu),
              reads=[P.cT], writes=[scb])
        wb = [b.sbuf("ada_w%d" % i, [128, 9216], BF16) for i in range(3)]
        ps = P.PSA

        def load(k):
            t = wb[k % 3]
            b.add("gpsimd", lambda e: e.dma_start(out=t[:], in_=adaw[k]), reads=[], writes=[t], dma=True)
        for k in range(3):
            load(k)
        zt = b.sbuf("ada_zero", [128, 128], BF16)
        b.add("vector", lambda e: e.memset(zt[:], 0.0), reads=[], writes=[zt])
        b.add("tensor", lambda e: e.matmul(ps[:, 0, 0:72], lhsT=zt[:], rhs=zt[:, 0:72], start=True, stop=False),
              reads=[zt], writes=[(ps, 0)])
        for k in range(8):
            t = wb[k % 3]
            for j in range(72):
                b.add("tensor",
                      lambda e, t=t, j=j, k=k: e.matmul(ps[:, 0, j:j + 1], lhsT=t[:, j * 128:(j + 1) * 128],
                                                        rhs=scb[:, k:k + 1], start=False, stop=(k == 7 and j == 71)),
                      reads=[t, scb], writes=[(ps, 0)])
            if k + 3 < 8:
                load(k + 3)
        b.add("vector", lambda e: e.tensor_tensor(out=P.modv[:], in0=ps[:, 0, 0:72], in1=vec[:, 0:72], op=ALU.add),
              reads=[(ps, 0), vec], writes=[P.modv])
        derive_vecs(b, P)


def derive_vecs(b, P):
    if True:
        for i in range(3):
            coef = (1.0 / ALPHA) if i == 1 else (0.5 / ALPHA)
            b.add("vector", lambda e, i=i: e.tensor_scalar(out=P.dvA[:, i * 8:(i + 1) * 8],
                                                           in0=P.modv[:, (3 * i + 1) * 8:(3 * i + 2) * 8],
                                                           scalar1=1.0, scalar2=None, op0=ALU.add),
                  reads=[P.modv], writes=[(P.dvA, i)])
            b.add("vector", lambda e, i=i, coef=coef: e.tensor_scalar(out=P.dvG[:, i * 8:(i + 1) * 8],
                                                                      in0=P.modv[:, (3 * i + 2) * 8:(3 * i + 3) * 8],
                                                                      scalar1=1.0, scalar2=coef, op0=ALU.add, op1=ALU.mult),
                  reads=[P.modv], writes=[(P.dvG, i)])


def make_h(b, P, i, hT, t0, nt, eng="scalar"):
    for k in range(8):
        rd = [(P.xT, (k, n)) for n in range(t0 // 512, (t0 + nt + 511) // 512)]
        b.add("scalar",
              lambda e, k=k: e.activation(out=hT[:, k, 0:nt], in_=P.xT[:, k, t0:t0 + nt], func=AF.Identity,
                                          scale=P.dvA[:, i * 8 + k:i * 8 + k + 1],
                                          bias=P.modv[:, (3 * i) * 8 + k:(3 * i) * 8 + k + 1]),
              reads=rd + [(P.dvA, i), P.modv], writes=[(hT, k)])


class LNState:
    def __init__(self, b, P, nbuf=3, nt=2, nset=1):
        self.nbuf, self.nt = nbuf, nt
        self.sets = []
        self.ybf = [b.sbuf("ln_ybf%d" % i, [128, 512], BF16) for i in range(nbuf)]
        self.ysq = [b.sbuf("ln_ysq%d" % i, [128, 512], BF16) for i in range(nbuf)]
        for q in range(nset):
            self.sets.append((b.sbuf("ln_mean%d" % q, [128, 512], F32), b.sbuf("ln_msq%d" % q, [128, 512], F32),
                              b.sbuf("ln_rstd%d" % q, [128, 512], F32)))
        self.t1 = [b.sbuf("ln_t1_%d" % i, [128, 512], F32) for i in range(nt)]
        self.t2 = [b.sbuf("ln_t2_%d" % i, [128, 512], F32) for i in range(nt)]
        self.cnt = 0


def residual_evac(b, P, ln, i, pso, psk, j, n, s1, s2):
    c = ln.cnt
    ln.cnt += 1
    ybf = ln.ybf[c % ln.nbuf]
    ysq = ln.ysq[c % ln.nbuf]
    sl = slice(n * 512, (n + 1) * 512)
    b.add("vector",
          lambda e: e.scalar_tensor_tensor(out=P.xT[:, j, sl], in0=pso[:, psk, :], scalar=P.dvG[:, i * 8 + j:i * 8 + j + 1],
                                           in1=P.xT[:, j, sl], op0=ALU.mult, op1=ALU.add),
          reads=[(pso, psk), (P.dvG, i), (P.xT, (j, n))], writes=[(P.xT, (j, n))])
    b.add("scalar", lambda e: e.activation(out=ybf[:], in_=P.xT[:, j, sl], func=AF.Identity),
          reads=[(P.xT, (j, n))], writes=[ybf])
    b.add("scalar", lambda e: e.activation(out=ysq[:], in_=P.xT[:, j, sl], func=AF.Square),
          reads=[(P.xT, (j, n))], writes=[ysq])
    b.add("tensor", lambda e: e.matmul(s1[0][:, s1[1], :], lhsT=P.onesb[:], rhs=ybf[:], start=(j == 0), stop=(j == 7)),
          reads=[ybf, P.onesb], writes=[s1])
    b.add("tensor", lambda e: e.matmul(s2[0][:, s2[1], :], lhsT=P.onesb[:], rhs=ysq[:], start=(j == 0), stop=(j == 7)),
          reads=[ysq, P.onesb], writes=[s2])


def ln_finish(b, P, ln, L, i, n, s1, s2, sidx=0, defer=False):
    sl = slice(n * 512, (n + 1) * 512)
    vec = P.vec[L]
    gcol = 72 + i * 8
    bcol = 72 + 24 + i * 8
    mean, msq, rstd = ln.sets[sidx]
    b.add("scalar", lambda e: e.activation(out=mean[:], in_=s1[0][:, s1[1], :], func=AF.Identity),
          reads=[s1], writes=[mean])
    b.add("scalar", lambda e: e.activation(out=msq[:], in_=s1[0][:, s1[1], :], func=AF.Square),
          reads=[s1], writes=[msq])
    b.add("vector", lambda e: e.tensor_tensor(out=msq[:], in0=s2[0][:, s2[1], :], in1=msq[:], op=ALU.subtract),
          reads=[s2, msq], writes=[msq])
    specs = []
    specs.append(("scalar", lambda e: e.activation(out=rstd[:], in_=msq[:], func=AF.Sqrt, bias=P.epsln[:, 0:1]),
                  [msq, P.epsln], [rstd]))
    specs.append(("vector", lambda e: e.reciprocal(out=rstd[:], in_=rstd[:]), [rstd], [rstd]))
    for k in range(8):
        t1 = ln.t1[k % ln.nt]
        t2 = ln.t2[k % ln.nt]
        specs.append(("gpsimd", lambda e, k=k, t1=t1: e.tensor_tensor(out=t1[:], in0=P.xT[:, k, sl], in1=mean[:], op=ALU.subtract),
                      [(P.xT, (k, n)), mean], [t1]))
        specs.append(("vector", lambda e, k=k, t1=t1, t2=t2: e.scalar_tensor_tensor(out=t2[:], in0=t1[:], scalar=vec[:, gcol + k:gcol + k + 1],
                                                                                 in1=rstd[:], op0=ALU.mult, op1=ALU.mult),
                      [t1, rstd, vec], [t2]))
        specs.append(("scalar", lambda e, k=k, t2=t2: e.activation(out=P.xT[:, k, sl], in_=t2[:], func=AF.Identity,
                                                               bias=vec[:, bcol + k:bcol + k + 1]),
                      [t2, vec], [(P.xT, (k, n))]))
    if defer:
        return specs
    emit_specs(b, specs)
    return []


def emit_specs(b, specs, n=None):
    k = len(specs) if n is None else min(n, len(specs))
    for _ in range(k):
        eng, fn, rd, wr = specs.pop(0)
        b.add(eng, fn, reads=rd, writes=wr)


def ffn_phase(b, P, L, i, win, wout):
    with b.phase():
        hT = b.sbuf("ffn_hT", [128, 8, 1024], BF16)
        gT = b.sbuf("ffn_gT", [128, NFF, 1024], BF16)
        wib = [b.sbuf("ffn_wi%d" % r, [128, 2, 8, 128], BF16) for r in range(3)]
        wob = [b.sbuf("ffn_wo%d" % r, [128, NFF, 128], BF16) for r in range(3)]
        sa = [b.sbuf("ffn_sa%d" % r, [128, 512], F32) for r in range(2)]
        ln = LNState(b, P, nset=2)
        PSA, PSB = P.PSA, P.PSB
        pend = []

        def load_wi(m):
            t = wib[m % 3]
            b.add("gpsimd", lambda e: e.dma_start(out=t[:].rearrange("p a k c -> p (a k c)"), in_=win[m]),
                  reads=[], writes=[t], dma=True)

        def load_wo(j):
            t = wob[j % 3]
            b.add("gpsimd", lambda e: e.dma_start(out=t[:].rearrange("p m c -> p (m c)"), in_=wout[j]),
                  reads=[], writes=[t], dma=True)

        for hf in range(2):
            T0 = hf * 1024
            for m in range(3):
                load_wi(m)
            make_h(b, P, i, hT, T0, 1024)
            cnt = 0
            for m in range(NFF):
                w = wib[m % 3]
                for n in range(2):
                    ka = cnt % 2
                    cnt += 1
                    for ab in range(2):
                        bank = 2 * ka + ab
                        for k in range(8):
                            b.add("tensor",
                                  lambda e, w=w, ab=ab, k=k, bank=bank, n=n: e.matmul(
                                      PSB[:, bank, :], lhsT=w[:, ab, k, :], rhs=hT[:, k, n * 512:(n + 1) * 512],
                                      start=(k == 0), stop=(k == 7)),
                                  reads=[w, (hT, k)], writes=[(PSB, bank)])
                    s = sa[ka]
                    b.add("scalar", lambda e, s=s, ka=ka: e.activation(out=s[:], in_=PSB[:, 2 * ka, :], func=AF.Silu),
                          reads=[(PSB, 2 * ka)], writes=[s])
                    b.add("vector",
                          lambda e, s=s, ka=ka, m=m, n=n: e.tensor_tensor(out=gT[:, m, n * 512:(n + 1) * 512], in0=PSB[:, 2 * ka + 1, :],
                                                                            in1=s[:], op=ALU.mult),
                          reads=[(PSB, 2 * ka + 1), s], writes=[(gT, (m, n))])
                    emit_specs(b, pend, 2)
                if m + 3 < NFF:
                    load_wi(m + 3)
            emit_specs(b, pend)
            for j in range(3):
                load_wo(j)
            cnt = 0
            for j in range(8):
                w = wob[j % 3]
                for n in range(2):
                    ko = cnt % 2
                    cnt += 1
                    for m in range(NFF):
                        b.add("tensor",
                              lambda e, w=w, m=m, n=n, ko=ko: e.matmul(PSA[:, ko, :], lhsT=w[:, m, :],
                                                                       rhs=gT[:, m, n * 512:(n + 1) * 512],
                                                                       start=(m == 0), stop=(m == NFF - 1)),
                              reads=[w, (gT, (m, n))], writes=[(PSA, ko)])
                    st = (PSA, PSB)[n]
                    residual_evac(b, P, ln, i, PSA, ko, j, hf * 2 + n, (st, 2), (st, 3))
                if j + 3 < 8:
                    load_wo(j + 3)
            for n in range(2):
                st = (PSA, PSB)[n]
                pend += ln_finish(b, P, ln, L, i, hf * 2 + n, (st, 2), (st, 3), sidx=n, defer=(hf == 0))


def sgu_phase(b, P, L, W):
    vec = P.vec[L]
    BU, LNG2, LNB2 = 120, 136, 152
    with b.phase():
        hT = b.sbuf("sgs_hT", [128, 8, 512], BF16)
        Wv = b.sbuf("sgs_Wv", [128, 8, 2048], BF16)
        tri = b.sbuf("sgs_tri", [128, 128], F32)
        WcT = b.sbuf("sgs_WcT", [128, 2048], BF16)
        Bt = b.sbuf("sgs_Bt", [128, 2048], F32)
        grep = b.sbuf("sgs_grep", [128, 2048], F32)
        ones1 = b.sbuf("sgs_ones1", [128, 128], BF16)
        orow = b.sbuf("sgs_orow", [1, 128], BF16)
        bvrow = b.sbuf("sgs_bvrow", [1, 2048], BF16)
        uT = b.sbuf("sgs_uT", [128, 16, 512], BF16)
        PT = b.sbuf("sgs_PT", [128, 16, 512], BF16)
        vg = b.sbuf("sgs_vg", [128, 2048], F32)
        vn = b.sbuf("sgs_vn", [128, 2048], BF16)
        vsq = vn
        wsf = vg
        tt = b.sbuf("sgs_tt", [128, 1024], F32)
        st = b.sbuf("sgs_st", [128, 8], F32)
        wub = [b.sbuf("sgs_wu%d" % r, [128, 8, 128], BF16) for r in range(2)]
        wob = [b.sbuf("sgs_wo%d" % r, [128, 16, 128], BF16) for r in range(2)]
        ln = LNState(b, P, nbuf=2, nt=1)
        PSA, PSB = P.PSA, P.PSB
        pend = []
        for k in range(8):
            b.add("gpsimd", lambda e, k=k: e.dma_start(out=Wv[:, k, :], in_=W["wv"][k]), reads=[], writes=[(Wv, k)], dma=True)
        b.add("sync", lambda e: e.dma_start(out=wsf[:], in_=W["wsT"][:]), reads=[], writes=[wsf], dma=True)
        b.add("sync", lambda e: e.dma_start(out=tri[:], in_=P.cst_d[:, 128:256]), reads=[], writes=[tri], dma=True)
        b.add("sync", lambda e: e.dma_start(out=grep[:], in_=W["grep"][:]), reads=[], writes=[grep], dma=True)
        b.add("sync", lambda e: e.dma_start(out=Bt[:], in_=W["bsb"][:]), reads=[], writes=[Bt], dma=True)
        b.add("gpsimd", lambda e: e.dma_start(out=bvrow[:], in_=W["bvrow"][:]), reads=[], writes=[bvrow], dma=True)
        b.add("vector", lambda e: e.memset(ones1[:], 1.0), reads=[], writes=[ones1])
        b.add("vector", lambda e: e.memset(orow[:], 1.0), reads=[], writes=[orow])
        b.add("vector", lambda e: e.tensor_tensor(out=WcT[:].rearrange("p (g t) -> p g t", g=16),
                                                  in0=wsf[:].rearrange("p (g t) -> p g t", g=16),
                                                  in1=tri[:].unsqueeze(1).to_broadcast([128, 16, 128]), op=ALU.mult),
              reads=[wsf, tri], writes=[WcT])
        for q in range(4):
            b.add("tensor", lambda e, q=q: e.matmul(PSA[:, q, :], lhsT=ones1[:], rhs=WcT[:, q * 512:(q + 1) * 512], start=True, stop=True),
                  reads=[ones1, WcT], writes=[(PSA, q)])
        b.add("vector", lambda e: e.tensor_tensor(out=vg[:].rearrange("p (g t) -> p g t", g=16),
                                                  in0=PSA[:].rearrange("p q (g t) -> p (q g) t", g=4),
                                                  in1=vec[:, LNB2:LNB2 + 16].unsqueeze(2).to_broadcast([128, 16, 128]), op=ALU.mult),
              reads=[PSA, vec], writes=[vg])
        b.add("vector", lambda e: e.tensor_tensor(out=Bt[:], in0=Bt[:], in1=vg[:], op=ALU.add), reads=[Bt, vg], writes=[Bt])

        def load_wu(ii):
            t = wub[ii % 2]
            b.add("gpsimd", lambda e: e.dma_start(out=t[:].rearrange("p k c -> p (k c)"), in_=W["wu"][ii % 16]),
                  reads=[], writes=[t], dma=True)

        def load_wo(jj):
            t = wob[jj % 2]
            b.add("gpsimd", lambda e: e.dma_start(out=t[:].rearrange("p i c -> p (i c)"), in_=W["wo"][jj % 8]),
                  reads=[], writes=[t], dma=True)

        for G in range(4):
            make_h(b, P, 1, hT, G * 512, 512)
            for ii in range(2):
                load_wu(G * 16 + ii)
            for i in range(16):
                w = wub[(G * 16 + i) % 2]
                bk = i % 2
                for k in range(8):
                    b.add("tensor", lambda e, w=w, k=k, bk=bk: e.matmul(PSB[:, bk, :], lhsT=w[:, k, :], rhs=hT[:, k, :],
                                                                       start=(k == 0), stop=(k == 7)),
                          reads=[w, (hT, k)], writes=[(PSB, bk)])
                b.add("scalar", lambda e, i=i, bk=bk: e.activation(out=uT[:, i, :], in_=PSB[:, bk, :], func=AF.Gelu,
                                                                   bias=vec[:, BU + i:BU + i + 1]),
                      reads=[(PSB, bk), vec], writes=[(uT, i)])
                emit_specs(b, pend, 2)
                if i + 2 < 16:
                    load_wu(G * 16 + i + 2)
            emit_specs(b, pend)
            for j in range(2):
                load_wo(G * 8 + j)
            def v_proj(cc):
                tk = slice(cc * 128, (cc + 1) * 128)
                for nb in range(4):
                    for k in range(8):
                        b.add("tensor", lambda e, nb=nb, k=k, tk=tk: e.matmul(PSA[:, nb, :], lhsT=hT[:, k, tk], rhs=Wv[:, k, nb * 512:(nb + 1) * 512],
                                                                             start=(k == 0), stop=False),
                              reads=[(hT, k), (Wv, k)], writes=[(PSA, nb)])
                    b.add("tensor", lambda e, nb=nb: e.matmul(PSA[:, nb, :], lhsT=orow[0:1, :], rhs=bvrow[0:1, nb * 512:(nb + 1) * 512],
                                                             start=False, stop=True),
                          reads=[orow, bvrow], writes=[(PSA, nb)])

            v_proj(0)
            for cc in range(4):
                tk = slice(cc * 128, (cc + 1) * 128)
                b.add("scalar", lambda e: e.activation(out=vg[:], in_=PSA[:].rearrange("p q n -> p (q n)"), func=AF.Gelu),
                      reads=[PSA], writes=[vg])
                b.add("vector", lambda e: e.reduce_sum(out=st[:, 0:1], in_=vg[:], axis=AX.X), reads=[vg], writes=[(st, 0)])
                b.add("scalar", lambda e: e.activation(out=vsq[:], in_=vg[:], func=AF.Square), reads=[vg], writes=[vsq])
                b.add("vector", lambda e: e.reduce_sum(out=st[:, 1:2], in_=vsq[:], axis=AX.X), reads=[vsq], writes=[(st, 1)])
                b.add("vector", lambda e: e.tensor_scalar(out=st[:, 2:3], in0=st[:, 0:1], scalar1=1.0 / 2048.0, scalar2=None, op0=ALU.mult),
                      reads=[(st, 0)], writes=[(st, 2)])
                b.add("vector", lambda e: e.tensor_tensor(out=st[:, 3:4], in0=st[:, 2:3], in1=st[:, 2:3], op=ALU.mult),
                      reads=[(st, 2)], writes=[(st, 3)])
                b.add("vector", lambda e: e.scalar_tensor_tensor(out=st[:, 4:5], in0=st[:, 1:2], scalar=1.0 / 2048.0, in1=st[:, 3:4],
                                                                 op0=ALU.mult, op1=ALU.subtract),
                      reads=[(st, 1), (st, 3)], writes=[(st, 4)])
                b.add("scalar", lambda e: e.activation(out=st[:, 5:6], in_=st[:, 4:5], func=AF.Sqrt, bias=P.eps5[:, 0:1]),
                      reads=[(st, 4), P.eps5], writes=[(st, 5)])
                b.add("vector", lambda e: e.reciprocal(out=st[:, 6:7], in_=st[:, 5:6]), reads=[(st, 5)], writes=[(st, 6)])
                b.add("vector", lambda e: e.scalar_tensor_tensor(out=st[:, 7:8], in0=st[:, 2:3], scalar=-1.0, in1=st[:, 6:7],
                                                                 op0=ALU.mult, op1=ALU.mult),
                      reads=[(st, 2), (st, 6)], writes=[(st, 7)])
                b.add("scalar", lambda e: e.activation(out=vn[:], in_=vg[:], func=AF.Identity, scale=st[:, 6:7], bias=st[:, 7:8]),
                      reads=[vg, (st, 6), (st, 7)], writes=[vn])
                if cc + 1 < 4:
                    v_proj(cc + 1)
                for hh in range(2):
                    for g8 in range(8):
                        g = hh * 8 + g8
                        col = g8 * 128
                        b.add("tensor", lambda e, g=g, hh=hh, col=col: e.matmul(
                            PSB[:, 2 * hh + col // 512, (col % 512):(col % 512) + 128], lhsT=vn[:, g * 128:(g + 1) * 128],
                            rhs=WcT[:, g * 128:(g + 1) * 128], start=True, stop=True),
                              reads=[vn, WcT], writes=[(PSB, 2 * hh + col // 512)])
                    hs = slice(hh * 1024, (hh + 1) * 1024)
                    b.add("vector", lambda e, hh=hh, hs=hs: e.tensor_tensor(out=tt[:], in0=PSB[:, 2 * hh:2 * hh + 2, :].rearrange("p q n -> p (q n)"),
                                                                          in1=grep[:, hs], op=ALU.mult),
                          reads=[(PSB, 2 * hh), (PSB, 2 * hh + 1), grep], writes=[tt])
                    b.add("gpsimd", lambda e, hs=hs: e.tensor_tensor(out=tt[:], in0=tt[:], in1=Bt[:, hs], op=ALU.add),
                          reads=[tt, Bt], writes=[tt])
                    b.add("vector", lambda e, hh=hh, tk=tk: e.tensor_tensor(out=PT[:, hh * 8:(hh + 1) * 8, tk],
                                                                          in0=tt[:].rearrange("p (g t) -> p g t", g=8),
                                                                          in1=uT[:, hh * 8:(hh + 1) * 8, tk], op=ALU.mult),
                          reads=[tt] + [(uT, hh * 8 + q) for q in range(8)], writes=[(PT, (hh, cc))])
            for j in range(8):
                w = wob[(G * 8 + j) % 2]
                ko = j % 2
                for i in range(16):
                    b.add("tensor", lambda e, w=w, i=i, ko=ko: e.matmul(PSB[:, ko, :], lhsT=w[:, i, :], rhs=PT[:, i, :],
                                                                       start=(i == 0), stop=(i == 15)),
                          reads=[w] + [(PT, (i // 8, q)) for q in range(4)], writes=[(PSB, ko)])
                residual_evac(b, P, ln, 1, PSB, ko, j, G, (PSB, 2), (PSB, 3))
                if j + 2 < 8:
                    load_wo(G * 8 + j + 2)
            ln_finish(b, P, ln, L, 1, G, (PSB, 2), (PSB, 3))


def ssd_phase(b, P, W):
    vec = P.vec[0]
    CW, CB, NW = 120, 168, 180
    with b.phase():
        hT = b.sbuf("sd_hT", [128, 8, 512], BF16)
        Wx = b.sbuf("sd_Wx", [128, 8, 1536], BF16)
        Wz = b.sbuf("sd_Wz", [128, 8, 1024], BF16)
        Wd = b.sbuf("sd_Wd", [128, 8, 16], BF16)
        ident = b.sbuf("sd_ident", [128, 128], BF16)
        trib = b.sbuf("sd_trib", [128, 128], BF16)
        maskb = b.sbuf("sd_maskb", [128, 512], BF16)
        onesb1 = b.sbuf("sd_onesb1", [128, 128], BF16)
        adth = b.sbuf("sd_adth", [128, 16], BF16)
        adtl = b.sbuf("sd_adtl", [128, 16], BF16)
        RH = b.sbuf("sd_RH", [128, 2048], BF16)
        RL = b.sbuf("sd_RL", [128, 2048], BF16)
        one1 = b.sbuf("sd_one1", [128, 1], F32)
        tokc = b.sbuf("sd_tokc", [128, 1056], F32)
        halo = b.sbuf("sd_halo", [128, 12, 3], F32)
        xr = [b.sbuf("sd_xr%d" % i, [128, 515], F32) for i in range(2)]
        acc = [b.sbuf("sd_acc%d" % i, [128, 512], F32) for i in range(2)]
        xbcs = b.sbuf("sd_xbcs", [128, 12, 512], BF16)
        small = b.sbuf("sd_small", [128, 16 * 13], F32)
        Dm = b.sbuf("sd_Dm", [128, 2048], F32)
        MT = b.sbuf("sd_MT", [128, 2048], BF16)
        cbT = b.sbuf("sd_cbT", [128, 256], F32)
        zs = b.sbuf("sd_zs", [128, 1024], F32)
        xst = b.sbuf("sd_xst", [128, 1024], BF16)
        Btok = b.sbuf("sd_Btok", [128, 256], BF16)
        y1 = b.sbuf("sd_y1", [128, 1024], F32)
        tmp = b.sbuf("sd_tmp", [128, 1024], F32)
        xcd = b.sbuf("sd_xcd", [128, 1024], BF16)
        state = b.sbuf("sd_state", [128, 1024], F32)
        statebf = b.sbuf("sd_statebf", [128, 1024], BF16)
        yn = b.sbuf("sd_yn", [128, 1024], BF16)
        ysT = b.sbuf("sd_ysT", [128, 8, 512], BF16)
        PSA, PSB = P.PSA, P.PSB
        R3 = PSA[:].rearrange("p q (h l) -> p (q h) l", l=128)

        def sm(i, n=16):
            return small[:, i * 16:i * 16 + n]

        lim = [None]

        def A(*a, **k):
            if lim[0] is None:
                return b.add(*a, **k)
            if lim[0] > 0:
                lim[0] -= 1
                return b.add(*a, **k)
            return None

        def smk(i):
            return (small, i)

        for k in range(8):
            b.add("gpsimd", lambda e, k=k: e.dma_start(out=Wx[:, k, :], in_=W["wxbc"][k]), reads=[], writes=[(Wx, k)], dma=True)
            b.add("gpsimd", lambda e, k=k: e.dma_start(out=Wz[:, k, :], in_=W["wz"][k]), reads=[], writes=[(Wz, k)], dma=True)
            b.add("gpsimd", lambda e, k=k: e.dma_start(out=Wd[:, k, :], in_=W["wdt"][k]), reads=[], writes=[(Wd, k)], dma=True)
        b.add("gpsimd", lambda e: e.dma_start(out=ident[:], in_=P.cst_d[:, 0:128]), reads=[], writes=[ident], dma=True)
        b.add("gpsimd", lambda e: e.dma_start(out=trib[:], in_=P.cst_d[:, 128:256]), reads=[], writes=[trib], dma=True)
        for q in range(4):
            b.add("gpsimd", lambda e, q=q: e.dma_start(out=maskb[:, q * 128:(q + 1) * 128], in_=P.cst_d[:, 256:384]),
                  reads=[], writes=[(maskb, q)], dma=True)
        b.add("sync", lambda e: e.dma_start(out=tokc[:], in_=W["tokc"][:]), reads=[], writes=[tokc], dma=True)
        b.add("vector", lambda e: e.memset(onesb1[:], 1.0), reads=[], writes=[onesb1])
        if SSD_CUT is not None:
            b.add("vector", lambda e: e.memset(ysT[:], 0.0), reads=[], writes=[ysT])
        b.add("vector", lambda e: e.memset(one1[:], 1.0), reads=[], writes=[one1])
        b.add("vector", lambda e: e.memset(state[:], 0.0), reads=[], writes=[state])
        b.add("vector", lambda e: e.memset(statebf[:], 0.0), reads=[], writes=[statebf])
        b.add("scalar", lambda e: e.activation(out=sm(11), in_=tokc[:, 1040:1056], func=AF.Exp), reads=[tokc], writes=[smk(11)])
        b.add("vector", lambda e: e.tensor_scalar(out=sm(11), in0=sm(11), scalar1=-1.0, scalar2=None, op0=ALU.mult),
              reads=[smk(11)], writes=[smk(11)])

        for G in range(4):
            make_h(b, P, 1, hT, G * 512, 512)
            for q in range(12):
                bk = q % 2
                for k in range(8):
                    b.add("tensor", lambda e, q=q, k=k, bk=bk: e.matmul(PSB[:, bk, :], lhsT=Wx[:, k, q * 128:(q + 1) * 128], rhs=hT[:, k, :],
                                                                       start=(k == 0), stop=(k == 7)),
                          reads=[(Wx, k), (hT, k)], writes=[(PSB, bk)])
                x_r = xr[q % 2]
                a = acc[q % 2]
                if G == 0:
                    b.add("vector", lambda e, x_r=x_r: e.memset(x_r[:, 0:3], 0.0), reads=[], writes=[(x_r, "h")])
                else:
                    b.add("vector", lambda e, x_r=x_r, q=q: e.tensor_copy(out=x_r[:, 0:3], in_=halo[:, q, :]),
                          reads=[(halo, q)], writes=[(x_r, "h")])
                b.add("scalar", lambda e, x_r=x_r, bk=bk: e.activation(out=x_r[:, 3:515], in_=PSB[:, bk, :], func=AF.Identity),
                      reads=[(PSB, bk)], writes=[(x_r, "b")])
                if G < 3:
                    b.add("vector", lambda e, x_r=x_r, q=q: e.tensor_copy(out=halo[:, q, :], in_=x_r[:, 512:515]),
                          reads=[(x_r, "b")], writes=[(halo, q)])
                b.add("vector", lambda e, x_r=x_r, a=a, q=q: e.tensor_scalar(out=a[:], in0=x_r[:, 0:512], scalar1=vec[:, CW + q:CW + q + 1],
                                                                             scalar2=None, op0=ALU.mult),
                      reads=[(x_r, "h"), (x_r, "b"), vec], writes=[a])
                for kk in range(1, 4):
                    b.add("vector", lambda e, x_r=x_r, a=a, q=q, kk=kk: e.scalar_tensor_tensor(
                        out=a[:], in0=x_r[:, kk:kk + 512], scalar=vec[:, CW + kk * 12 + q:CW + kk * 12 + q + 1], in1=a[:],
                        op0=ALU.mult, op1=ALU.add),
                          reads=[(x_r, "h"), (x_r, "b"), vec, a], writes=[a])
                b.add("scalar", lambda e, a=a, q=q: e.activation(out=xbcs[:, q, :], in_=a[:], func=AF.Silu, bias=vec[:, CB + q:CB + q + 1]),
                      reads=[a, vec], writes=[(xbcs, q)])
            for cc in range(4):
                tk = slice(cc * 128, (cc + 1) * 128)
                if SSD_CUT is not None and SSD_CUT < 1:
                    continue
                for nb in range(2):
                    for k in range(8):
                        b.add("tensor", lambda e, nb=nb, k=k, tk=tk: e.matmul(PSB[:, 1 + nb, :], lhsT=hT[:, k, tk], rhs=Wz[:, k, nb * 512:(nb + 1) * 512],
                                                                             start=(k == 0), stop=(k == 7)),
                              reads=[(hT, k), (Wz, k)], writes=[(PSB, 1 + nb)])
                b.add("scalar", lambda e: e.activation(out=zs[:], in_=PSB[:, 1:3, :].rearrange("p q n -> p (q n)"), func=AF.Silu),
                      reads=[(PSB, 1), (PSB, 2)], writes=[zs])
                if SSD_CUT is not None and SSD_CUT < 2:
                    continue
                for k in range(8):
                    b.add("tensor", lambda e, k=k, tk=tk: e.matmul(PSB[:, 0, 0:16], lhsT=hT[:, k, tk], rhs=Wd[:, k, :], start=(k == 0), stop=(k == 7)),
                          reads=[(hT, k), (Wd, k)], writes=[(PSB, 0)])
                b.add("vector", lambda e: e.tensor_tensor(out=sm(0), in0=PSB[:, 0, 0:16], in1=tokc[:, 1024:1040], op=ALU.add),
                      reads=[(PSB, 0), tokc], writes=[smk(0)])
                b.add("scalar", lambda e: e.activation(out=sm(1), in_=sm(0), func=AF.Abs), reads=[smk(0)], writes=[smk(1)])
                b.add("scalar", lambda e: e.activation(out=sm(2), in_=sm(1), func=AF.Exp, scale=-1.0), reads=[smk(1)], writes=[smk(2)])
                b.add("scalar", lambda e: e.activation(out=sm(3), in_=sm(2), func=AF.Ln, bias=one1[:, 0:1]), reads=[smk(2), one1], writes=[smk(3)])
                b.add("vector", lambda e: e.scalar_tensor_tensor(out=sm(4), in0=sm(0), scalar=0.0, in1=sm(3), op0=ALU.max, op1=ALU.add),
                      reads=[smk(0), smk(3)], writes=[smk(4)])
                b.add("scalar", lambda e: e.activation(out=sm(5), in_=sm(4), func=AF.Ln), reads=[smk(4)], writes=[smk(5)])
                b.add("vector", lambda e: e.tensor_tensor(out=sm(6), in0=sm(4), in1=sm(11), op=ALU.mult), reads=[smk(4), smk(11)], writes=[smk(6)])
                if SSD_CUT is not None and SSD_CUT < 3:
                    continue
                lim[0] = SSD_SUB
                A("scalar", lambda e: e.activation(out=adth[:], in_=sm(6), func=AF.Identity), reads=[smk(6)], writes=[adth])
                A("vector", lambda e: e.tensor_tensor(out=adtl[:], in0=sm(6), in1=adth[:], op=ALU.subtract), reads=[smk(6), adth], writes=[adtl])
                for (src, dst) in ((adth, RH), (adtl, RL)):
                    A("vector", lambda e, src=src, dst=dst: e.tensor_tensor(out=dst[:].rearrange("p (h l) -> p h l", l=128),
                                                                              in0=src[:].unsqueeze(2).to_broadcast([128, 16, 128]),
                                                                              in1=trib[:].unsqueeze(1).to_broadcast([128, 16, 128]), op=ALU.mult),
                          reads=[src, trib], writes=[dst])
                for q4 in range(4):
                    A("tensor", lambda e, q4=q4: e.matmul(PSA[:, q4, :], lhsT=onesb1[:], rhs=RH[:, q4 * 512:(q4 + 1) * 512], start=True, stop=False),
                          reads=[onesb1, RH], writes=[(PSA, q4)])
                    A("tensor", lambda e, q4=q4: e.matmul(PSA[:, q4, :], lhsT=onesb1[:], rhs=RL[:, q4 * 512:(q4 + 1) * 512], start=False, stop=False),
                          reads=[onesb1, RL], writes=[(PSA, q4)])
                    A("tensor", lambda e, q4=q4: e.matmul(PSA[:, q4, :], lhsT=ident[:], rhs=maskb[:], start=False, stop=True),
                          reads=[ident, maskb], writes=[(PSA, q4)])
                A("tensor", lambda e: e.matmul(PSB[:, 0, 16:32], lhsT=trib[:], rhs=adth[:], start=True, stop=False),
                      reads=[trib, adth], writes=[(PSB, 0)])
                A("tensor", lambda e: e.matmul(PSB[:, 0, 16:32], lhsT=trib[:], rhs=adtl[:], start=False, stop=True),
                      reads=[trib, adtl], writes=[(PSB, 0)])
                A("vector", lambda e: e.tensor_tensor(out=sm(7), in0=PSB[:, 0, 16:32], in1=sm(5), op=ALU.subtract),
                      reads=[(PSB, 0), smk(5)], writes=[smk(7)])
                A("scalar", lambda e: e.activation(out=sm(8), in_=PSB[:, 0, 16:32], func=AF.Exp), reads=[(PSB, 0)], writes=[smk(8)])
                A("vector", lambda e: e.tensor_tensor(out=sm(9), in0=R3[:, :, 127], in1=sm(7), op=ALU.subtract),
                      reads=[PSA, smk(7)], writes=[smk(9)])
                A("scalar", lambda e: e.activation(out=sm(9), in_=sm(9), func=AF.Exp), reads=[smk(9)], writes=[smk(9)])
                A("scalar", lambda e: e.activation(out=sm(10), in_=R3[:, :, 127], func=AF.Exp), reads=[PSA], writes=[smk(10)])
                A("vector", lambda e: e.tensor_tensor(out=Dm[:].rearrange("p (h l) -> p h l", l=128), in0=R3,
                                                          in1=sm(7).unsqueeze(2).to_broadcast([128, 16, 128]), op=ALU.subtract),
                      reads=[PSA, smk(7)], writes=[Dm])
                A("scalar", lambda e: e.activation(out=Dm[:], in_=Dm[:], func=AF.Exp), reads=[Dm], writes=[Dm])
                if SSD_CUT is not None and SSD_CUT < 5:
                    continue
                for g in range(2):
                    b.add("tensor", lambda e, g=g, tk=tk: e.matmul(PSB[:, 0, 128 + g * 128:256 + g * 128], lhsT=xbcs[:, 8 + g, tk], rhs=xbcs[:, 10 + g, tk],
                                                                 start=True, stop=True),
                          reads=[(xbcs, 8 + g), (xbcs, 10 + g)], writes=[(PSB, 0)])
                b.add("scalar", lambda e: e.activation(out=cbT[:], in_=PSB[:, 0, 128:384], func=AF.Identity), reads=[(PSB, 0)], writes=[cbT])
                b.add("vector", lambda e: e.tensor_tensor(out=MT[:].rearrange("p (g r l) -> p g r l", g=2, r=8),
                                                          in0=Dm[:].rearrange("p (g r l) -> p g r l", g=2, r=8),
                                                          in1=cbT[:].rearrange("p (g l) -> p g l", g=2).unsqueeze(2).to_broadcast([128, 2, 8, 128]),
                                                          op=ALU.mult),
                      reads=[Dm, cbT], writes=[MT])
                if SSD_CUT is not None and SSD_CUT < 6:
                    continue
                for q in range(8):
                    b.add("tensor", lambda e, q=q, tk=tk: e.matmul(PSB[:, 1 + q // 4, (q % 4) * 128:(q % 4) * 128 + 128], lhsT=xbcs[:, q, tk], rhs=ident[:],
                                                                 start=True, stop=True),
                          reads=[(xbcs, q), ident], writes=[(PSB, 1 + q // 4)])
                b.add("scalar", lambda e: e.activation(out=xst[:], in_=PSB[:, 1:3, :].rearrange("p q n -> p (q n)"), func=AF.Identity),
                      reads=[(PSB, 1), (PSB, 2)], writes=[xst])
                for g in range(2):
                    b.add("tensor", lambda e, g=g, tk=tk: e.matmul(PSB[:, 3, g * 128:(g + 1) * 128], lhsT=xbcs[:, 8 + g, tk], rhs=ident[:], start=True, stop=True),
                          reads=[(xbcs, 8 + g), ident], writes=[(PSB, 3)])
                b.add("scalar", lambda e: e.activation(out=Btok[:], in_=PSB[:, 3, 0:256], func=AF.Identity), reads=[(PSB, 3)], writes=[Btok])
                if SSD_CUT is not None and SSD_CUT < 7:
                    continue
                for h in range(16):
                    b.add("tensor", lambda e, h=h: e.matmul(PSA[:, h // 8, (h % 8) * 64:(h % 8) * 64 + 64], lhsT=MT[:, h * 128:(h + 1) * 128],
                                                           rhs=xst[:, h * 64:(h + 1) * 64], start=True, stop=True),
                          reads=[MT, xst], writes=[(PSA, h // 8)])
                for g in range(2):
                    b.add("tensor", lambda e, g=g, tk=tk: e.matmul(PSA[:, 2 + g, :], lhsT=xbcs[:, 10 + g, tk], rhs=statebf[:, g * 512:(g + 1) * 512],
                                                                 start=True, stop=True),
                          reads=[(xbcs, 10 + g), statebf], writes=[(PSA, 2 + g)])
                b.add("vector", lambda e: e.tensor_tensor(out=y1[:].rearrange("p (h d) -> p h d", d=64),
                                                          in0=PSA[:, 2:4, :].rearrange("p q (h d) -> p (q h) d", d=64),
                                                          in1=sm(8).unsqueeze(2).to_broadcast([128, 16, 64]), op=ALU.mult),
                      reads=[(PSA, 2), (PSA, 3), smk(8)], writes=[y1])
                b.add("vector", lambda e: e.tensor_tensor(out=y1[:], in0=y1[:], in1=PSA[:, 0:2, :].rearrange("p q n -> p (q n)"), op=ALU.add),
                      reads=[y1, (PSA, 0), (PSA, 1)], writes=[y1])
                b.add("vector", lambda e: e.tensor_tensor(out=tmp[:], in0=xst[:], in1=tokc[:, 0:1024], op=ALU.mult), reads=[xst, tokc], writes=[tmp])
                b.add("vector", lambda e: e.tensor_tensor(out=y1[:], in0=y1[:], in1=tmp[:], op=ALU.add), reads=[y1, tmp], writes=[y1])
                b.add("vector", lambda e: e.tensor_tensor(out=y1[:], in0=y1[:], in1=zs[:], op=ALU.mult), reads=[y1, zs], writes=[y1])
                if SSD_CUT is not None and SSD_CUT < 8:
                    continue
                b.add("vector", lambda e: e.tensor_tensor(out=xcd[:].rearrange("p (h d) -> p h d", d=64), in0=xst[:].rearrange("p (h d) -> p h d", d=64),
                                                          in1=sm(9).unsqueeze(2).to_broadcast([128, 16, 64]), op=ALU.mult),
                      reads=[xst, smk(9)], writes=[xcd])
                for g in range(2):
                    b.add("tensor", lambda e, g=g: e.matmul(PSB[:, 1 + g, :], lhsT=Btok[:, g * 128:(g + 1) * 128], rhs=xcd[:, g * 512:(g + 1) * 512],
                                                           start=True, stop=True),
                          reads=[Btok, xcd], writes=[(PSB, 1 + g)])
                b.add("vector", lambda e: e.tensor_tensor(out=state[:].rearrange("p (h d) -> p h d", d=64), in0=state[:].rearrange("p (h d) -> p h d", d=64),
                                                          in1=sm(10).unsqueeze(2).to_broadcast([128, 16, 64]), op=ALU.mult),
                      reads=[state, smk(10)], writes=[state])
                b.add("vector", lambda e: e.tensor_tensor(out=state[:], in0=state[:], in1=PSB[:, 1:3, :].rearrange("p q n -> p (q n)"), op=ALU.add),
                      reads=[state, (PSB, 1), (PSB, 2)], writes=[state])
                b.add("scalar", lambda e: e.activation(out=statebf[:], in_=state[:], func=AF.Identity), reads=[state], writes=[statebf])
                if SSD_CUT is not None and SSD_CUT < 9:
                    continue
                b.add("scalar", lambda e: e.activation(out=tmp[:], in_=y1[:], func=AF.Square), reads=[y1], writes=[tmp])
                for g in range(2):
                    b.add("vector", lambda e, g=g: e.reduce_sum(out=small[:, 192 + g:193 + g], in_=tmp[:, g * 512:(g + 1) * 512], axis=AX.X),
                          reads=[tmp], writes=[(small, 12)])
                b.add("scalar", lambda e: e.activation(out=small[:, 194:196], in_=small[:, 192:194], func=AF.Sqrt, scale=1.0 / 512.0, bias=P.eps5[:, 0:1]),
                      reads=[(small, 12), P.eps5], writes=[(small, 12)])
                b.add("vector", lambda e: e.reciprocal(out=small[:, 196:198], in_=small[:, 194:196]), reads=[(small, 12)], writes=[(small, 12)])
                for g in range(2):
                    b.add("scalar", lambda e, g=g: e.activation(out=yn[:, g * 512:(g + 1) * 512], in_=y1[:, g * 512:(g + 1) * 512], func=AF.Identity,
                                                                scale=small[:, 196 + g:197 + g]),
                          reads=[y1, (small, 12)], writes=[(yn, g)])
                for q in range(8):
                    b.add("tensor", lambda e, q=q: e.matmul(PSB[:, 1 + q // 4, (q % 4) * 128:(q % 4) * 128 + 128], lhsT=yn[:, q * 128:(q + 1) * 128], rhs=ident[:],
                                                           start=True, stop=True),
                          reads=[(yn, q // 4), ident], writes=[(PSB, 1 + q // 4)])
                b.add("vector", lambda e, tk=tk: e.tensor_tensor(out=ysT[:, :, tk], in0=PSB[:, 1:3, :].rearrange("p q (c t) -> p (q c) t", t=128),
                                                               in1=vec[:, NW:NW + 8].unsqueeze(2).to_broadcast([128, 8, 128]), op=ALU.mult),
                      reads=[(PSB, 1), (PSB, 2), vec], writes=[(ysT, cc)])
            for q in range(8):
                b.add("sync", lambda e, q=q, G=G: e.dma_start(out=P.ymix[q][:, G * 512:(G + 1) * 512], in_=ysT[:, q, :]),
                      reads=[(ysT, c4) for c4 in range(4)], writes=[("ymix", (q, G))], dma=True)


QSCALE = 192.0 ** -0.5


def mla1_phase(b, P, W):
    vec = P.vec[0]
    QW, KW = 188, 191
    with b.phase():
        hT = b.sbuf("m1_hT", [128, 8, 512], BF16)
        Wl = b.sbuf("m1_Wl", [128, 8, 704], BF16)
        rope = b.sbuf("m1_rope", [64, 2, S], F32)
        Rm = b.sbuf("m1_Rm", [64, 64], BF16)
        onq = b.sbuf("m1_onq", [128, 128], BF16)
        onk = b.sbuf("m1_onk", [128, 128], BF16)
        lat = b.sbuf("m1_lat", [128, 5, 512], F32)
        sq = b.sbuf("m1_sq", [128, 5, 512], BF16)
        krf = b.sbuf("m1_krf", [64, 512], F32)
        krh = b.sbuf("m1_krh", [64, 512], BF16)
        krl = b.sbuf("m1_krl", [64, 512], BF16)
        t1 = b.sbuf("m1_t1", [64, 512], F32)
        t2 = b.sbuf("m1_t2", [64, 512], F32)
        rs = b.sbuf("m1_rs", [128, 2, 512], F32)
        outn = b.sbuf("m1_outn", [128, 5, 512], BF16)
        kpo = b.sbuf("m1_kpo", [64, 512], BF16)
        PSA, PSB = P.PSA, P.PSB
        for k in range(8):
            b.add("gpsimd", lambda e, k=k: e.dma_start(out=Wl[:, k, :], in_=W["wlat"][k]), reads=[], writes=[(Wl, k)], dma=True)
        b.add("sync", lambda e: e.dma_start(out=rope[:], in_=W["rope"][:]), reads=[], writes=[rope], dma=True)
        b.add("gpsimd", lambda e: e.dma_start(out=Rm[:], in_=P.cst_d[0:64, 384:448]), reads=[], writes=[Rm], dma=True)
        b.add("vector", lambda e: e.memset(onq[:], 1.0 / 384.0), reads=[], writes=[onq])
        b.add("vector", lambda e: e.memset(onk[:], 1.0 / 256.0), reads=[], writes=[onk])
        for G in range(4):
            ts = slice(G * 512, (G + 1) * 512)
            make_h(b, P, 1, hT, G * 512, 512)
            for c in range(5):
                bk = c % 2
                for k in range(8):
                    b.add("tensor", lambda e, c=c, k=k, bk=bk: e.matmul(PSB[:, bk, :], lhsT=Wl[:, k, c * 128:(c + 1) * 128], rhs=hT[:, k, :],
                                                                       start=(k == 0), stop=(k == 7)),
                          reads=[(Wl, k), (hT, k)], writes=[(PSB, bk)])
                b.add("scalar", lambda e, c=c, bk=bk: e.activation(out=lat[:, c, :], in_=PSB[:, bk, :], func=AF.Identity),
                      reads=[(PSB, bk)], writes=[(lat, c)])
                b.add("scalar", lambda e, c=c: e.activation(out=sq[:, c, :], in_=lat[:, c, :], func=AF.Square),
                      reads=[(lat, c)], writes=[(sq, c)])
            for k in range(8):
                b.add("tensor", lambda e, k=k: e.matmul(PSB[0:64, 2, :], lhsT=Wl[:, k, 640:704], rhs=hT[:, k, :], start=(k == 0), stop=(k == 7)),
                      reads=[(Wl, k), (hT, k)], writes=[(PSB, 2)])
            b.add("scalar", lambda e: e.activation(out=krf[:], in_=PSB[0:64, 2, :], func=AF.Identity), reads=[(PSB, 2)], writes=[krf])
            for (c0, nchunk, on, bank, wcol, r) in ((0, 3, onq, 0, QW, 0), (3, 2, onk, 1, KW, 1)):
                for c in range(nchunk):
                    b.add("tensor", lambda e, c=c, c0=c0, on=on, bank=bank, nchunk=nchunk: e.matmul(PSA[:, bank, :], lhsT=on[:], rhs=sq[:, c0 + c, :],
                                                                                                 start=(c == 0), stop=(c == nchunk - 1)),
                          reads=[on, (sq, c0 + c)], writes=[(PSA, bank)])
                b.add("scalar", lambda e, bank=bank, r=r: e.activation(out=rs[:, r, :], in_=PSA[:, bank, :], func=AF.Sqrt, bias=P.eps5[:, 0:1]),
                      reads=[(PSA, bank), P.eps5], writes=[(rs, r)])
                b.add("vector", lambda e, r=r: e.reciprocal(out=rs[:, r, :], in_=rs[:, r, :]), reads=[(rs, r)], writes=[(rs, r)])
                for c in range(nchunk):
                    b.add("vector", lambda e, c=c, c0=c0, wcol=wcol, r=r: e.scalar_tensor_tensor(
                        out=outn[:, c0 + c, :], in0=lat[:, c0 + c, :], scalar=vec[:, wcol + c:wcol + c + 1], in1=rs[:, r, :],
                        op0=ALU.mult, op1=ALU.mult),
                          reads=[(lat, c0 + c), vec, (rs, r)], writes=[(outn, c0 + c)])
            b.add("scalar", lambda e: e.activation(out=krh[:], in_=krf[:], func=AF.Identity), reads=[krf], writes=[krh])
            b.add("vector", lambda e: e.tensor_tensor(out=krl[:], in0=krf[:], in1=krh[:], op=ALU.subtract), reads=[krf, krh], writes=[krl])
            b.add("tensor", lambda e: e.matmul(PSA[0:64, 2, :], lhsT=Rm[:], rhs=krh[:], start=True, stop=False), reads=[Rm, krh], writes=[(PSA, 2)])
            b.add("tensor", lambda e: e.matmul(PSA[0:64, 2, :], lhsT=Rm[:], rhs=krl[:], start=False, stop=True), reads=[Rm, krl], writes=[(PSA, 2)])
            b.add("vector", lambda e, ts=ts: e.tensor_tensor(out=t1[:], in0=krf[:], in1=rope[:, 0, ts], op=ALU.mult), reads=[krf, rope], writes=[t1])
            b.add("vector", lambda e, ts=ts: e.tensor_tensor(out=t2[:], in0=PSA[0:64, 2, :], in1=rope[:, 1, ts], op=ALU.mult), reads=[(PSA, 2), rope], writes=[t2])
            b.add("vector", lambda e: e.tensor_tensor(out=kpo[:], in0=t1[:], in1=t2[:], op=ALU.add), reads=[t1, t2], writes=[kpo])
            for c in range(5):
                b.add("sync", lambda e, c=c, ts=ts: e.dma_start(out=P.mlat[c][:, ts], in_=outn[:, c, :]), reads=[(outn, c)], writes=[("mlat", (c, G))], dma=True)
            b.add("sync", lambda e, ts=ts: e.dma_start(out=P.mlat[5][0:64, ts], in_=kpo[:]), reads=[kpo], writes=[("mlat", (5, G))], dma=True)


def mla2_phase(b, P, W):
    with b.phase():
        Wq = b.sbuf("m2_Wq", [128, 3, 1536], BF16)
        Wkv = b.sbuf("m2_Wkv", [128, 2, 2048], BF16)
        rope = b.sbuf("m2_rope", [64, 2, S], F32)
        Rm = b.sbuf("m2_Rm", [64, 64], BF16)
        trib = b.sbuf("m2_trib", [128, 128], BF16)
        ones1 = b.sbuf("m2_ones1", [128, 128], BF16)
        cqn = b.sbuf("m2_cqn", [128, 3, S], BF16)
        ckvn = b.sbuf("m2_ckvn", [128, 2, S], BF16)
        kpe = b.sbuf("m2_kpe", [65, S], BF16)
        sqkpe = b.sbuf("m2_sqkpe", [64, S], BF16)
        QnT = b.sbuf("m2_QnT", [128, S], BF16)
        QpT = b.sbuf("m2_QpT", [65, S], BF16)
        KnT = b.sbuf("m2_KnT", [128, S], BF16)
        V = b.sbuf("m2_V", [128, 16, 128], BF16)
        qn2s = b.sbuf("m2_qn2s", [65, S], F32)
        sqa_ = [b.sbuf("m2_sqa%d" % r, [128, 512], BF16) for r in range(2)]
        sqk_ = [b.sbuf("m2_sqk%d" % r, [128, 512], BF16) for r in range(2)]
        sqb_ = [b.sbuf("m2_sqb%d" % r, [64, 512], BF16) for r in range(2)]
        qpf_ = [b.sbuf("m2_qpf%d" % r, [64, 512], F32) for r in range(2)]
        qph_ = [b.sbuf("m2_qph%d" % r, [64, 512], BF16) for r in range(2)]
        qpl_ = [b.sbuf("m2_qpl%d" % r, [64, 512], BF16) for r in range(2)]
        t1_ = [b.sbuf("m2_t1%d" % r, [64, 512], F32) for r in range(2)]
        t2_ = [b.sbuf("m2_t2%d" % r, [64, 512], F32) for r in range(2)]
        kmx = b.sbuf("m2_kmx", [128, 8], F32)
        crow = b.sbuf("m2_crow", [65, 512], F32)
        PT = [b.sbuf("m2_PT%d" % r, [128, 512], BF16) for r in range(2)]
        rr = b.sbuf("m2_rr", [128, 128], F32)
        yatt = b.sbuf("m2_yatt", [128, S], BF16)
        PSA, PSB = P.PSA, P.PSB
        for c in range(3):
            b.add("gpsimd", lambda e, c=c: e.dma_start(out=Wq[:, c, :], in_=W["wuq"][c]), reads=[], writes=[(Wq, c)], dma=True)
            b.add("sync", lambda e, c=c: e.dma_start(out=cqn[:, c, :], in_=P.mlat[c][:, :]), reads=[("mlat", None)], writes=[(cqn, c)], dma=True)
        for c in range(2):
            b.add("gpsimd", lambda e, c=c: e.dma_start(out=Wkv[:, c, :], in_=W["wukv"][c]), reads=[], writes=[(Wkv, c)], dma=True)
            b.add("sync", lambda e, c=c: e.dma_start(out=ckvn[:, c, :], in_=P.mlat[3 + c][:, :]), reads=[("mlat", None)], writes=[(ckvn, c)], dma=True)
        b.add("sync", lambda e: e.dma_start(out=kpe[0:64, :], in_=P.mlat[5][0:64, :]), reads=[("mlat", None)], writes=[(kpe, 0)], dma=True)
        b.add("sync", lambda e: e.dma_start(out=rope[:], in_=W["rope"][:]), reads=[], writes=[rope], dma=True)
        b.add("gpsimd", lambda e: e.dma_start(out=Rm[:], in_=P.cst_d[0:64, 384:448]), reads=[], writes=[Rm], dma=True)
        b.add("gpsimd", lambda e: e.dma_start(out=trib[:], in_=P.cst_d[:, 128:256]), reads=[], writes=[trib], dma=True)
        b.add("vector", lambda e: e.memset(ones1[:], 1.0), reads=[], writes=[ones1])
        b.add("vector", lambda e: e.memset(kpe[64:65, :], 1.0), reads=[], writes=[(kpe, 1)])
        b.add("scalar", lambda e: e.activation(out=sqkpe[:], in_=kpe[0:64, :], func=AF.Square), reads=[(kpe, 0)], writes=[sqkpe])

        for h in range(8):
            for n in range(4):
                ts = slice(n * 512, (n + 1) * 512)
                r = n % 2
                sqa, sqk, sqb, qpf, qph, qpl, t1, t2 = sqa_[r], sqk_[r], sqb_[r], qpf_[r], qph_[r], qpl_[r], t1_[r], t2_[r]
                for c in range(2):
                    b.add("tensor", lambda e, c=c, h=h, ts=ts: e.matmul(PSB[:, 0, :], lhsT=Wkv[:, c, h * 256:h * 256 + 128], rhs=ckvn[:, c, ts],
                                                                       start=(c == 0), stop=(c == 1)),
                          reads=[(Wkv, c), (ckvn, c)], writes=[(PSB, 0)])
                for c in range(3):
                    b.add("tensor", lambda e, c=c, h=h, ts=ts: e.matmul(PSB[:, 1, :], lhsT=Wq[:, c, h * 192:h * 192 + 128], rhs=cqn[:, c, ts],
                                                                       start=(c == 0), stop=(c == 2)),
                          reads=[(Wq, c), (cqn, c)], writes=[(PSB, 1)])
                for c in range(3):
                    b.add("tensor", lambda e, c=c, h=h, ts=ts: e.matmul(PSB[0:64, 2, :], lhsT=Wq[:, c, h * 192 + 128:h * 192 + 192], rhs=cqn[:, c, ts],
                                                                       start=(c == 0), stop=(c == 2)),
                          reads=[(Wq, c), (cqn, c)], writes=[(PSB, 2)])
                for blk in range(4):
                    tb = slice(n * 512 + blk * 128, n * 512 + (blk + 1) * 128)
                    for c in range(2):
                        b.add("tensor", lambda e, c=c, h=h, tb=tb, blk=blk: e.matmul(PSA[:, 0, blk * 128:(blk + 1) * 128], lhsT=ckvn[:, c, tb],
                                                                                  rhs=Wkv[:, c, h * 256 + 128:h * 256 + 256], start=(c == 0), stop=(c == 1)),
                              reads=[(ckvn, c), (Wkv, c)], writes=[(PSA, 0)])
                b.add("scalar", lambda e, ts=ts: e.activation(out=KnT[:, ts], in_=PSB[:, 0, :], func=AF.Identity), reads=[(PSB, 0)], writes=[(KnT, n)])
                b.add("scalar", lambda e, ts=ts: e.activation(out=QnT[:, ts], in_=PSB[:, 1, :], func=AF.Identity, scale=QSCALE), reads=[(PSB, 1)], writes=[(QnT, n)])
                b.add("scalar", lambda e, qpf=qpf: e.activation(out=qpf[:], in_=PSB[0:64, 2, :], func=AF.Identity, scale=QSCALE), reads=[(PSB, 2)], writes=[qpf])
                b.add("scalar", lambda e, n=n: e.activation(out=V[:, n * 4:(n + 1) * 4, :], in_=PSA[:, 0, :].rearrange("p (b d) -> p b d", d=128), func=AF.Identity),
                      reads=[(PSA, 0)], writes=[(V, n)])
                b.add("scalar", lambda e, ts=ts, sqk=sqk: e.activation(out=sqk[:], in_=KnT[:, ts], func=AF.Square), reads=[(KnT, n)], writes=[sqk])
                b.add("tensor", lambda e, sqk=sqk: e.matmul(PSA[:, 3, :], lhsT=ones1[:], rhs=sqk[:], start=True, stop=False), reads=[ones1, sqk], writes=[(PSA, 3)])
                b.add("tensor", lambda e, ts=ts: e.matmul(PSA[:, 3, :], lhsT=ones1[0:64, :], rhs=sqkpe[:, ts], start=False, stop=True),
                      reads=[ones1, sqkpe], writes=[(PSA, 3)])
                b.add("vector", lambda e, n=n: e.reduce_max(out=kmx[:, n:n + 1], in_=PSA[:, 3, :], axis=AX.X), reads=[(PSA, 3)], writes=[(kmx, n)])
                b.add("scalar", lambda e, ts=ts, sqa=sqa: e.activation(out=sqa[:], in_=QnT[:, ts], func=AF.Square), reads=[(QnT, n)], writes=[sqa])
                b.add("scalar", lambda e, qpf=qpf, qph=qph: e.activation(out=qph[:], in_=qpf[:], func=AF.Identity), reads=[qpf], writes=[qph])
                b.add("vector", lambda e, qpf=qpf, qph=qph, qpl=qpl: e.tensor_tensor(out=qpl[:], in0=qpf[:], in1=qph[:], op=ALU.subtract), reads=[qpf, qph], writes=[qpl])
                b.add("tensor", lambda e, qph=qph: e.matmul(PSA[0:64, 2, :], lhsT=Rm[:], rhs=qph[:], start=True, stop=False), reads=[Rm, qph], writes=[(PSA, 2)])
                b.add("tensor", lambda e, qpl=qpl: e.matmul(PSA[0:64, 2, :], lhsT=Rm[:], rhs=qpl[:], start=False, stop=True), reads=[Rm, qpl], writes=[(PSA, 2)])
                b.add("vector", lambda e, ts=ts, qpf=qpf, t1=t1: e.tensor_tensor(out=t1[:], in0=qpf[:], in1=rope[:, 0, ts], op=ALU.mult), reads=[qpf, rope], writes=[t1])
                b.add("vector", lambda e, ts=ts, t2=t2: e.tensor_tensor(out=t2[:], in0=PSA[0:64, 2, :], in1=rope[:, 1, ts], op=ALU.mult), reads=[(PSA, 2), rope], writes=[t2])
                b.add("vector", lambda e, ts=ts, t1=t1, t2=t2: e.tensor_tensor(out=QpT[0:64, ts], in0=t1[:], in1=t2[:], op=ALU.add), reads=[t1, t2], writes=[(QpT, n)])
                b.add("scalar", lambda e, ts=ts, sqb=sqb: e.activation(out=sqb[:], in_=QpT[0:64, ts], func=AF.Square), reads=[(QpT, n)], writes=[sqb])
                b.add("tensor", lambda e, sqa=sqa: e.matmul(PSA[:, 1, :], lhsT=ones1[:], rhs=sqa[:], start=True, stop=False), reads=[ones1, sqa], writes=[(PSA, 1)])
                b.add("tensor", lambda e, sqb=sqb: e.matmul(PSA[:, 1, :], lhsT=ones1[0:64, :], rhs=sqb[:], start=False, stop=True), reads=[ones1, sqb], writes=[(PSA, 1)])
                b.add("scalar", lambda e, ts=ts: e.activation(out=qn2s[64:65, ts], in_=PSA[64:65, 1, :], func=AF.Identity), reads=[(PSA, 1)], writes=[(qn2s, n)])
            b.add("vector", lambda e: e.reduce_max(out=kmx[:, 4:5], in_=kmx[:, 0:4], axis=AX.X), reads=[(kmx, n) for n in range(4)], writes=[(kmx, 4)])
            for n in range(4):
                ts = slice(n * 512, (n + 1) * 512)
                b.add("scalar", lambda e, ts=ts: e.activation(out=crow[64:65, :], in_=qn2s[64:65, ts], func=AF.Sqrt, scale=kmx[64:65, 4:5]),
                      reads=[(qn2s, n), (kmx, 4)], writes=[crow])
                b.add("vector", lambda e, ts=ts: e.tensor_scalar(out=QpT[64:65, ts], in0=crow[64:65, :], scalar1=-1.0, scalar2=None, op0=ALU.mult),
                      reads=[crow], writes=[(QpT, (n, "c"))])
            groups = []
            for i in range(16):
                for jg in range(0, i + 1, 4):
                    groups.append((i, list(range(jg, min(jg + 4, i + 1)))))

            def emit_st(g):
                i, js = groups[g]
                qs = slice(i * 128, (i + 1) * 128)
                bk = g % 2
                for jj, j in enumerate(js):
                    ks = slice(j * 128, (j + 1) * 128)
                    b.add("tensor", lambda e, jj=jj, ks=ks, qs=qs, bk=bk: e.matmul(PSB[:, bk, jj * 128:(jj + 1) * 128], lhsT=KnT[:, ks], rhs=QnT[:, qs],
                                                                                start=True, stop=False),
                          reads=[(KnT, j // 4), (QnT, i // 4)], writes=[(PSB, bk)])
                    b.add("tensor", lambda e, jj=jj, ks=ks, qs=qs, bk=bk: e.matmul(PSB[:, bk, jj * 128:(jj + 1) * 128], lhsT=kpe[0:65, ks], rhs=QpT[0:65, qs],
                                                                                start=False, stop=True),
                          reads=[(kpe, 0), (kpe, 1), (QpT, i // 4), (QpT, (i // 4, "c"))], writes=[(PSB, bk)])

            def emit_exp(g):
                i, js = groups[g]
                bk = g % 2
                pt = PT[g % 2]
                nb = len(js)
                b.add("scalar", lambda e: e.activation(out=pt[:, 0:nb * 128], in_=PSB[:, bk, 0:nb * 128], func=AF.Exp),
                      reads=[(PSB, bk)], writes=[pt])
                if js[-1] == i:
                    jj = len(js) - 1
                    b.add("vector", lambda e: e.tensor_tensor(out=pt[:, jj * 128:(jj + 1) * 128], in0=pt[:, jj * 128:(jj + 1) * 128],
                                                              in1=trib[:], op=ALU.mult),
                          reads=[pt, trib], writes=[pt])

            def emit_pv(g):
                i, js = groups[g]
                qs = slice(i * 128, (i + 1) * 128)
                ab = 2 * (i % 2)
                pt = PT[g % 2]
                for jj, j in enumerate(js):
                    b.add("tensor", lambda e, jj=jj, j=j: e.matmul(PSA[:, ab, 0:128], lhsT=V[:, j, :], rhs=pt[:, jj * 128:(jj + 1) * 128],
                                                                 start=(j == 0), stop=(j == i)),
                          reads=[(V, j // 4), pt], writes=[(PSA, ab)])
                    b.add("tensor", lambda e, jj=jj, j=j: e.matmul(PSA[:, ab + 1, 0:128], lhsT=ones1[:], rhs=pt[:, jj * 128:(jj + 1) * 128],
                                                                 start=(j == 0), stop=(j == i)),
                          reads=[ones1, pt], writes=[(PSA, ab + 1)])
                if js[-1] == i:
                    b.add("vector", lambda e: e.reciprocal(out=rr[:], in_=PSA[:, ab + 1, 0:128]), reads=[(PSA, ab + 1)], writes=[rr])
                    b.add("vector", lambda e: e.tensor_tensor(out=yatt[:, qs], in0=PSA[:, ab, 0:128], in1=rr[:], op=ALU.mult),
                          reads=[(PSA, ab), rr], writes=[(yatt, i)])

            emit_st(0)
            for g in range(len(groups)):
                emit_exp(g)
                if g + 1 < len(groups):
                    emit_st(g + 1)
                emit_pv(g)
            b.add("sync", lambda e, h=h: e.dma_start(out=P.ymix[8 + h][:, :], in_=yatt[:]), reads=[yatt], writes=[("ymix", (8 + h, 0))], dma=True)


def mixout_phase(b, P, L, wo):
    with b.phase():
        ybs = [b.sbuf("mo_yb%d" % r, [128, 16, 512], BF16) for r in range(2)]
        wob = [b.sbuf("mo_wo%d" % r, [128, 16, 128], BF16) for r in range(3)]
        ln = LNState(b, P, nset=2)
        PSB = P.PSB
        pend = []

        def load_wo(jj):
            t = wob[jj % 3]
            b.add("gpsimd", lambda e: e.dma_start(out=t[:].rearrange("p i c -> p (i c)"), in_=wo[jj % 8]), reads=[], writes=[t], dma=True)

        for n in range(4):
            ts = slice(n * 512, (n + 1) * 512)
            yb = ybs[n % 2]
            for c in range(16):
                b.add("sync", lambda e, c=c, ts=ts, yb=yb: e.dma_start(out=yb[:, c, :], in_=P.ymix[c][:, ts]), reads=[("ymix", None)], writes=[(yb, c)], dma=True)
            for j in range(3):
                load_wo(n * 8 + j)
            for j in range(8):
                w = wob[(n * 8 + j) % 3]
                ko = j % 2
                for c in range(16):
                    b.add("tensor", lambda e, w=w, c=c, ko=ko, yb=yb: e.matmul(PSB[:, ko, :], lhsT=w[:, c, :], rhs=yb[:, c, :], start=(c == 0), stop=(c == 15)),
                          reads=[w, (yb, c)], writes=[(PSB, ko)])
                emit_specs(b, pend, 4)
                residual_evac(b, P, ln, 1, PSB, ko, j, n, (PSB, 2), (PSB, 3))
                if j + 3 < 8:
                    load_wo(n * 8 + j + 3)
            emit_specs(b, pend)
            pend += ln_finish(b, P, ln, L, 1, n, (PSB, 2), (PSB, 3), sidx=n % 2, defer=(n < 3))

def dump_ymix(b, P, q0):
    with b.phase():
        t = b.sbuf("dbg_y", [128, S], BF16)
        for q in range(8):
            b.add("sync", lambda e, q=q: e.dma_start(out=t[:], in_=P.ymix[q0 + q][:, :]), reads=[("ymix", None)], writes=[t], dma=True)
            b.add("vector", lambda e, q=q: e.tensor_copy(out=P.xT[:, q, :], in_=t[:]), reads=[t], writes=[(P.xT, (q, n)) for n in range(4)])


def build_program(stages=None, dbg_modv=False, dump_modv=False):
    if stages is None:
        stages = []
        for L in range(2):
            stages += [("ada", L), ("ffa", L), ("mix", L), ("ffb", L)]
    b = Builder()
    nc = b.nc
    P = Prog()
    xT_d = b.dram("xT", [128, 8, S], F32, kind="ExternalInput")
    cT_d = b.dram("cT", [128, 8], F32, kind="ExternalInput")
    P.cst_d = b.dram("cst", [128, 512], F32, kind="ExternalInput")
    P.adaw, P.vec_d = {}, {}
    ffw = {}
    mixw = {}
    for L in range(2):
        P.vec_d[L] = b.dram("vec%d" % L, [128, 256], F32, kind="ExternalInput")
        if ("ada", L) in stages:
            P.adaw[L] = b.dram("adaw%d" % L, [8, 128, 9216], F32, kind="ExternalInput")
        for nm in ("ffa", "ffb"):
            if (nm, L) in stages:
                ffw[(nm, L)] = (b.dram("%s_in%d" % (nm, L), [NFF, 128, 2048], F32, kind="ExternalInput"),
                                b.dram("%s_out%d" % (nm, L), [8, 128, DFF], F32, kind="ExternalInput"))
    if ("mix", 1) in stages:
        mixw[1] = {
            "wu": b.dram("sg_wu", [16, 128, 1024], F32, kind="ExternalInput"),
            "wv": b.dram("sg_wv", [8, 128, 2048], F32, kind="ExternalInput"),
            "wo": b.dram("sg_wo", [8, 128, 2048], F32, kind="ExternalInput"),
            "wsT": b.dram("sg_wsT", [128, 2048], F32, kind="ExternalInput"),
            "grep": b.dram("sg_grep", [128, 2048], F32, kind="ExternalInput"),
            "bsb": b.dram("sg_bsb", [128, 2048], F32, kind="ExternalInput"),
            "bvrow": b.dram("sg_bvrow", [1, 2048], F32, kind="ExternalInput"),
        }
    kinds0 = [k for (k, L) in stages if L == 0]
    if any(k in ("mix", "ssd", "mla") for k in kinds0):
        mixw[0] = {
            "wz": b.dram("ev_wz", [8, 128, 1024], F32, kind="ExternalInput"),
            "wxbc": b.dram("ev_wxbc", [8, 128, 1536], F32, kind="ExternalInput"),
            "wdt": b.dram("ev_wdt", [8, 128, 16], F32, kind="ExternalInput"),
            "tokc": b.dram("ev_tokc", [128, 1056], F32, kind="ExternalInput"),
            "wlat": b.dram("ev_wlat", [8, 128, 704], F32, kind="ExternalInput"),
            "wuq": b.dram("ev_wuq", [3, 128, 1536], F32, kind="ExternalInput"),
            "wukv": b.dram("ev_wukv", [2, 128, 2048], F32, kind="ExternalInput"),
            "rope": b.dram("ev_rope", [64, 2, S], F32, kind="ExternalInput"),
            "wo": b.dram("ev_wo", [8, 128, 2048], F32, kind="ExternalInput"),
        }
        P.ymix = b.dram("ymix", [16, 128, S], BF16, kind="Internal")
        P.mlat = b.dram("mlat", [6, 128, S], BF16, kind="Internal")
    if dbg_modv:
        modv_d = b.dram("modv_dbg", [128, 72], F32, kind="ExternalInput")
    y_d = b.dram("yT", [128, 8, S], F32, kind="ExternalOutput")

    P.xT = b.sbuf("xT_sb", [128, 8, S], F32, persistent=True)
    P.cT = b.sbuf("cT_sb", [128, 8], F32, persistent=True)
    P.vec = [b.sbuf("vec_sb%d" % L, [128, 256], F32, persistent=True) for L in range(2)]
    P.modv = b.sbuf("modv", [128, 72], F32, persistent=True)
    P.dvA = b.sbuf("dvA", [128, 24], F32, persistent=True)
    P.dvG = b.sbuf("dvG", [128, 24], F32, persistent=True)
    P.onesb = b.sbuf("onesb", [128, 128], BF16, persistent=True)
    P.epsln = b.sbuf("epsln", [128, 1], F32, persistent=True)
    P.eps5 = b.sbuf("eps5", [128, 1], F32, persistent=True)
    P.PSA = b.psum("PSA", [128, 4, 512], F32, persistent=True)
    P.PSB = b.psum("PSB", [128, 4, 512], F32, persistent=True)
    b.exclusive = {"PSA", "PSB"}

    for k in range(8):
        b.add("sync", lambda e, k=k: e.dma_start(out=P.xT[:, k, :], in_=xT_d[:, k, :]),
              reads=[], writes=[(P.xT, (k, n)) for n in range(4)], dma=True)
    b.add("sync", lambda e: e.dma_start(out=P.cT[:], in_=cT_d[:]), reads=[], writes=[P.cT], dma=True)
    for L in range(2):
        b.add("sync", lambda e, L=L: e.dma_start(out=P.vec[L][:], in_=P.vec_d[L][:]), reads=[], writes=[P.vec[L]], dma=True)
    b.add("vector", lambda e: e.memset(P.onesb[:], 1.0 / 1024.0), reads=[], writes=[P.onesb])
    b.add("vector", lambda e: e.memset(P.epsln[:], EPS_LN), reads=[], writes=[P.epsln])
    b.add("vector", lambda e: e.memset(P.eps5[:], EPS), reads=[], writes=[P.eps5])
    if dbg_modv:
        b.add("sync", lambda e: e.dma_start(out=P.modv[:], in_=modv_d[:]), reads=[], writes=[P.modv], dma=True)
        derive_vecs(b, P)
    b.barrier()

    for (kind, L) in stages:
        if kind == "ada":
            ada_phase(b, P, L)
        elif kind == "ffa":
            ffn_phase(b, P, L, 0, *ffw[("ffa", L)])
        elif kind == "ffb":
            ffn_phase(b, P, L, 2, *ffw[("ffb", L)])
        elif kind == "mix" and L == 1:
            sgu_phase(b, P, L, mixw[1])
        elif kind == "ssd":
            ssd_phase(b, P, mixw[0])
            dump_ymix(b, P, 0)
        elif kind == "mla":
            mla1_phase(b, P, mixw[0])
            mla2_phase(b, P, mixw[0])
            dump_ymix(b, P, 8)
        elif kind == "mix" and L == 0:
            ssd_phase(b, P, mixw[0])
            mla1_phase(b, P, mixw[0])
            mla2_phase(b, P, mixw[0])
            mixout_phase(b, P, 0, mixw[0]["wo"])


    if dump_modv:
        b.add("vector", lambda e: e.tensor_copy(out=P.xT[:, 0, 0:72], in_=P.modv[:]), reads=[P.modv], writes=[(P.xT, (0, 0))])
    for k in range(8):
        b.add("sync", lambda e, k=k: e.dma_start(out=y_d[:, k, :], in_=P.xT[:, k, :]),
              reads=[(P.xT, (k, n)) for n in range(4)], writes=[], dma=True, out=True)
    nc = b.finish()
    return nc, b


def _fm(v):
    v = np.asarray(v, dtype=np.float32)
    return np.ascontiguousarray(v.reshape(-1, 128).T)


def prep_shared(inp):
    sh = {}
    cst = np.zeros((128, 512), np.float32)
    cst[:, 0:128] = np.eye(128, dtype=np.float32)
    cst[:, 128:256] = np.triu(np.ones((128, 128), np.float32))
    cst[:, 256:384] = (1.0 - np.triu(np.ones((128, 128), np.float32))) * -30000.0
    for mm in range(32):
        cst[mm + 32, 384 + mm] = -1.0
        cst[mm, 384 + mm + 32] = 1.0
    sh["cst"] = cst
    for L in range(2):
        p = "l%d_" % L
        sh["adaw%d" % L] = np.ascontiguousarray(inp[p + "ada_w"].reshape(8, 128, 9216))
        vec = np.zeros((128, 256), np.float32)
        vec[:, 0:72] = _fm(inp[p + "ada_b"])
        vec[:, 72:96] = _fm(inp[p + "ln_g"].reshape(-1))
        vec[:, 96:120] = _fm(inp[p + "ln_b"].reshape(-1))
        sh["vec%d" % L] = vec
        for nm in ("ffa", "ffb"):
            w_in = inp[p + nm + "_w_in"]
            t = w_in.reshape(8, 128, 2, NFF, 128)
            sh[nm + "_in%d" % L] = np.ascontiguousarray(t.transpose(3, 1, 2, 0, 4).reshape(NFF, 128, 2048))
            w_out = inp[p + nm + "_w_out"]
            t = w_out.reshape(NFF, 128, 8, 128)
            sh[nm + "_out%d" % L] = np.ascontiguousarray(t.transpose(2, 1, 0, 3).reshape(8, 128, DFF))
    w_in = inp["l0_w_in"]
    sh["ev_wz"] = np.ascontiguousarray(w_in[:, 0:1024].reshape(8, 128, 1024))
    sh["ev_wxbc"] = np.ascontiguousarray(w_in[:, 1024:2560].reshape(8, 128, 1536))
    sh["ev_wdt"] = np.ascontiguousarray(w_in[:, 2560:2576].reshape(8, 128, 16))
    sh["ev_wlat"] = np.ascontiguousarray(w_in[:, 2576:3280].reshape(8, 128, 704))
    sh["ev_wuq"] = np.ascontiguousarray(inp["l0_w_uq"].reshape(3, 128, 1536))
    sh["ev_wukv"] = np.ascontiguousarray(inp["l0_w_ukv"].reshape(2, 128, 2048))
    sh["ev_wo"] = np.ascontiguousarray(inp["l0_w_out"].reshape(16, 128, 8, 128).transpose(2, 1, 0, 3).reshape(8, 128, 2048))
    inv = (1.0 / (np.float32(10000.0) ** (np.arange(0, 64, 2, dtype=np.float32) / np.float32(64.0)))).astype(np.float32)
    ang = np.arange(S, dtype=np.float32)[:, None] * inv[None, :]
    cosT = np.cos(ang).astype(np.float32).T
    sinT = np.sin(ang).astype(np.float32).T
    rope = np.zeros((64, 2, S), np.float32)
    rope[0:32, 0] = cosT
    rope[32:64, 0] = cosT
    rope[0:32, 1] = sinT
    rope[32:64, 1] = sinT
    sh["ev_rope"] = rope
    tokc = np.zeros((128, 1056), np.float32)
    tokc[:, 0:1024] = np.repeat(inp["l0_d_skip"], 64)[None, :]
    tokc[:, 1024:1040] = inp["l0_dt_bias"][None, :]
    tokc[:, 1040:1056] = inp["l0_a_log"][None, :]
    sh["ev_tokc"] = tokc
    v0 = sh["vec0"]
    v0[:, 120:168] = np.concatenate([_fm(inp["l0_conv_w"][k]) for k in range(4)], axis=1)
    v0[:, 168:180] = _fm(inp["l0_conv_b"])
    v0[:, 180:188] = _fm(inp["l0_ssd_norm_w"])
    v0[:, 188:191] = _fm(inp["l0_q_norm_w"])
    v0[:, 191:193] = _fm(inp["l0_kv_norm_w"])
    w_uv = inp["l1_w_uv"]
    sh["sg_wu"] = np.ascontiguousarray(w_uv[:, :2048].reshape(8, 128, 16, 128).transpose(2, 1, 0, 3).reshape(16, 128, 1024))
    sh["sg_wv"] = np.ascontiguousarray(w_uv[:, 2048:].reshape(8, 128, 2048))
    sh["sg_wo"] = np.ascontiguousarray(inp["l1_w_out"].reshape(16, 128, 8, 128).transpose(2, 1, 0, 3).reshape(8, 128, 2048))
    sh["sg_wsT"] = np.ascontiguousarray(inp["l1_w_s"].transpose(2, 0, 1).reshape(128, 2048))
    sh["sg_grep"] = np.ascontiguousarray(np.repeat(_fm(inp["l1_sgu_ln_g"]), 128, axis=1))
    sh["sg_bsb"] = np.ascontiguousarray(np.broadcast_to(inp["l1_b_s"].reshape(1, 2048), (128, 2048)))
    sh["sg_bvrow"] = np.ascontiguousarray(inp["l1_b_uv"][2048:].reshape(1, 2048))
    v1 = sh["vec1"]
    v1[:, 120:136] = _fm(inp["l1_b_uv"][:2048])
    v1[:, 152:168] = _fm(inp["l1_sgu_ln_b"])
    return sh


def prep_core(inp, bi):
    x = inp["x"][bi]
    xT = np.ascontiguousarray(x.reshape(S, 8, 128).transpose(2, 1, 0))
    cT = _fm(inp["c"][bi])
    return {"xT": xT, "cT": cT}


def kernel(**inputs):
    inp = {k: np.asarray(v) for k, v in inputs.items()}
    nc, _ = build_program()
    sh = prep_shared(inp)
    decl = set()
    for nm in list(sh.keys()) + ["xT", "cT"]:
        try:
            nc.lookup_mloc(nm)
            decl.add(nm)
        except Exception:
            pass
    in_maps = []
    for bi in range(8):
        m = dict(sh)
        m.update(prep_core(inp, bi))
        in_maps.append({k: v for k, v in m.items() if k in decl})
    res = run_bass_kernel_spmd(nc, in_maps, core_ids=list(range(8)))
    out = np.empty((8, S, D), np.float32)
    for bi in range(8):
        yT = res.results[bi]["yT"]
        out[bi] = yT.transpose(2, 1, 0).reshape(S, D)
    return out
```

```python
import math
import numpy as np
from contextlib import ExitStack, contextmanager
import concourse.bass as bass
import concourse.mybir as mybir
from concourse.bass_utils import run_bass_kernel_spmd

F32 = mybir.dt.float32
BF16 = mybir.dt.bfloat16
AF = mybir.ActivationFunctionType
ALU = mybir.AluOpType
AX = mybir.AxisListType

ENGS = ("tensor", "vector", "scalar", "gpsimd", "sync")
N_DMA_SEMS = 24
import os
SSD_CUT = int(os.environ['SSD_CUT']) if 'SSD_CUT' in os.environ else None
SSD_SUB = int(os.environ['SSD_SUB']) if 'SSD_SUB' in os.environ else None

D = 1024
S = 2048
DFF = 2816
NFF = 22
ALPHA = 4.0 ** 0.25
EPS = 1e-5
EPS_LN = EPS / (ALPHA * ALPHA)


class _Op:
    __slots__ = ("eng", "fn", "deps", "dma", "idx", "needed", "semval", "dsem", "epoch")


class Builder:
    def __init__(self):
        self.nc = bass.Bass("TRN2", target_bir_lowering=False)
        self.es = ExitStack()
        self.ops = []
        self.st = {}
        self.dma_last = [None] * N_DMA_SEMS
        self.dma_cnt = [0] * N_DMA_SEMS
        self.dma_rr = [0, 0]
        self.out_dmas = []
        self.last_on = {e: None for e in ENGS}
        self.bar = {}
        self.epoch = 0
        self.pes = None
        self.exclusive = set()

    def sbuf(self, name, shape, dtype, persistent=False):
        es = self.es if (persistent or self.pes is None) else self.pes
        if es is not self.es:
            name = "%s_p%d" % (name, self.epoch)
        return es.enter_context(self.nc.sbuf_tensor(name, list(shape), dtype))

    def psum(self, name, shape, dtype=F32, persistent=False):
        es = self.es if (persistent or self.pes is None) else self.pes
        return es.enter_context(self.nc.psum_tensor(name, list(shape), dtype))

    def dram(self, name, shape, dtype, kind="Internal"):
        return self.nc.dram_tensor(name, list(shape), dtype, kind=kind)

    @contextmanager
    def phase(self):
        self.pes = ExitStack()
        try:
            yield
        finally:
            self.pes.close()
            self.pes = None
            self.barrier()

    def barrier(self):
        snap = set()
        for e in ENGS:
            if self.last_on[e] is not None:
                snap.add(self.last_on[e])
        for s in range(N_DMA_SEMS):
            if self.dma_last[s] is not None:
                snap.add(self.dma_last[s])
        self.bar = {e: set(snap) for e in ENGS}
        self.st = {}
        self.epoch += 1

    @staticmethod
    def _norm(a):
        if isinstance(a, tuple):
            b, k = a
        else:
            b, k = a, None
        if not isinstance(b, str):
            b = b.name
        return b, k

    def _entries(self, b, k):
        d = self.st.setdefault(b, {})
        if k is None:
            return list(d.keys())
        out = []
        if k in d:
            out.append(k)
        if None in d:
            out.append(None)
        return out

    def add(self, eng, fn, reads=(), writes=(), dma=False, out=False):
        op = _Op()
        op.eng, op.fn, op.dma = eng, fn, dma
        op.idx = len(self.ops)
        op.needed = False
        op.semval = None
        op.dsem = None
        op.epoch = self.epoch
        deps = set()
        if self.bar.get(eng):
            deps |= self.bar.pop(eng)
        reads = [self._norm(a) for a in reads]
        writes = [self._norm(a) for a in writes]
        for (bb, kk) in list(reads):
            if bb in self.exclusive and (bb, kk) not in writes:
                writes.append((bb, kk))
        for b, k in reads:
            d = self.st.setdefault(b, {})
            for kk in self._entries(b, k):
                w = d[kk][0]
                if w is not None:
                    deps.add(w)
        for b, k in writes:
            d = self.st.setdefault(b, {})
            for kk in self._entries(b, k):
                w, rs = d[kk]
                if w is not None:
                    deps.add(w)
                deps.update(rs)
        for b, k in reads:
            d = self.st[b]
            if k not in d:
                d[k] = [None, []]
                if k is not None and None in d:
                    d[k][0] = d[None][0]
            d[k][1].append(op.idx)
        for b, k in writes:
            d = self.st[b]
            if k is None:
                d.clear()
                d[None] = [op.idx, []]
            else:
                d[k] = [op.idx, []]
        if dma:
            half = N_DMA_SEMS // 2
            pool = 1 if eng == "gpsimd" else 0
            s = pool * half + self.dma_rr[pool]
            self.dma_rr[pool] = (self.dma_rr[pool] + 1) % half
            if self.dma_last[s] is not None:
                deps.add(self.dma_last[s])
            self.dma_last[s] = op.idx
            self.dma_cnt[s] += 1
            op.dsem = s
            op.semval = 16 * self.dma_cnt[s]
            if out:
                self.out_dmas.append(op.idx)
        deps.discard(op.idx)
        op.deps = deps
        self.ops.append(op)
        self.last_on[eng] = op.idx
        return op.idx

    def finish(self):
        nc = self.nc
        last = _Op()
        last.eng, last.fn, last.dma = "sync", (lambda e: e.nop()), False
        last.idx = len(self.ops)
        last.needed = False
        last.semval = None
        last.dsem = None
        last.epoch = self.epoch
        last.deps = set(self.out_dmas)
        self.ops.append(last)
        ops = self.ops
        for op in ops:
            nd = set()
            for j in op.deps:
                pj = ops[j]
                if (not pj.dma) and pj.eng == op.eng and (not op.dma) and op.eng == "tensor":
                    continue
                nd.add(j)
            op.deps = nd
            for j in nd:
                ops[j].needed = True
        cnt = {}
        for op in ops:
            if op.dma:
                continue
            if op.needed:
                key = (op.eng, op.epoch)
                cnt[key] = cnt.get(key, 0) + 1
                op.semval = cnt[key]
        esem = {}
        for key in cnt:
            esem[key] = self.es.enter_context(nc.semaphore("s_%s_%d" % key))
        dsem = [self.es.enter_context(nc.semaphore("d_%d" % i)) for i in range(N_DMA_SEMS)]
        self.n_sems = len(esem) + N_DMA_SEMS

        def emit_engine(ename):
            def body(eng):
                waited = {}
                for op in ops:
                    if op.eng != ename:
                        continue
                    need = {}
                    for j in op.deps:
                        pj = ops[j]
                        if pj.dma:
                            key = ("d", pj.dsem)
                            sem = dsem[pj.dsem]
                        else:
                            key = (pj.eng, pj.epoch)
                            sem = esem[key]
                        v = pj.semval
                        if waited.get(key, 0) >= v:
                            continue
                        if key not in need or need[key][1] < v:
                            need[key] = (sem, v)
                    for key, (sem, v) in need.items():
                        eng.wait_ge(sem, v)
                        waited[key] = v
                    ins = op.fn(eng)
                    if op.dma:
                        ins.then_inc(dsem[op.dsem], 16)
                    elif op.needed:
                        ins.then_inc(esem[(op.eng, op.epoch)], 1)
            return body

        with nc.Block() as block:
            block.tensor(emit_engine("tensor"))
            block.vector(emit_engine("vector"))
            block.scalar(emit_engine("scalar"))
            block.gpsimd(emit_engine("gpsimd"))
            block.sync(emit_engine("sync"))
        self.es.close()
        return nc


class Prog:
    pass


def tiles_of(k, n0, n1):
    return [(k, n) for n in range(n0, n1)]


def ada_phase(b, P, L):
    adaw = P.adaw[L]
    vec = P.vec[L]
    with b.phase():
        scb = b.sbuf("ada_sc", [128, 8], BF16)
        b.add("scalar", lambda e: e.activation(out=scb[:], in_=P.cT[:], func=AF.Silu),
              reads=[P.cT], writes=[scb])
        wb = [b.sbuf("ada_w%d" % i, [128, 9216], BF16) for i in range(3)]
        ps = P.PSA

        def load(k):
            t = wb[k % 3]
            b.add("gpsimd", lambda e: e.dma_start(out=t[:], in_=adaw[k]), reads=[], writes=[t], dma=True)
        for k in range(3):
            load(k)
        zt = b.sbuf("ada_zero", [128, 128], BF16)
        b.add("vector", lambda e: e.memset(zt[:], 0.0), reads=[], writes=[zt])
        b.add("tensor", lambda e: e.matmul(ps[:, 0, 0:72], lhsT=zt[:], rhs=zt[:, 0:72], start=True, stop=False),
              reads=[zt], writes=[(ps, 0)])
        for k in range(8):
            t = wb[k % 3]
            for j in range(72):
                b.add("tensor",
                      lambda e, t=t, j=j, k=k: e.matmul(ps[:, 0, j:j + 1], lhsT=t[:, j * 128:(j + 1) * 128],
                                                        rhs=scb[:, k:k + 1], start=False, stop=(k == 7 and j == 71)),
                      reads=[t, scb], writes=[(ps, 0)])
            if k + 3 < 8:
                load(k + 3)
        b.add("vector", lambda e: e.tensor_tensor(out=P.modv[:], in0=ps[:, 0, 0:72], in1=vec[:, 0:72], op=ALU.add),
              reads=[(ps, 0), vec], writes=[P.modv])
        derive_vecs(b, P)


def derive_vecs(b, P):
    if True:
        for i in range(3):
            coef = (1.0 / ALPHA) if i == 1 else (0.5 / ALPHA)
            b.add("vector", lambda e, i=i: e.tensor_scalar(out=P.dvA[:, i * 8:(i + 1) * 8],
                                                           in0=P.modv[:, (3 * i + 1) * 8:(3 * i + 2) * 8],
                                                           scalar1=1.0, scalar2=None, op0=ALU.add),
                  reads=[P.modv], writes=[(P.dvA, i)])
            b.add("vector", lambda e, i=i, coef=coef: e.tensor_scalar(out=P.dvG[:, i * 8:(i + 1) * 8],
                                                                      in0=P.modv[:, (3 * i + 2) * 8:(3 * i + 3) * 8],
                                                                      scalar1=1.0, scalar2=coef, op0=ALU.add, op1=ALU.mult),
                  reads=[P.modv], writes=[(P.dvG, i)])


def make_h(b, P, i, hT, t0, nt, eng="scalar"):
    for k in range(8):
        rd = [(P.xT, (k, n)) for n in range(t0 // 512, (t0 + nt + 511) // 512)]
        b.add("scalar",
              lambda e, k=k: e.activation(out=hT[:, k, 0:nt], in_=P.xT[:, k, t0:t0 + nt], func=AF.Identity,
                                          scale=P.dvA[:, i * 8 + k:i * 8 + k + 1],
                                          bias=P.modv[:, (3 * i) * 8 + k:(3 * i) * 8 + k + 1]),
              reads=rd + [(P.dvA, i), P.modv], writes=[(hT, k)])


class LNState:
    def __init__(self, b, P, nbuf=3, nt=2, nset=1):
        self.nbuf, self.nt = nbuf, nt
        self.sets = []
        self.ybf = [b.sbuf("ln_ybf%d" % i, [128, 512], BF16) for i in range(nbuf)]
        self.ysq = [b.sbuf("ln_ysq%d" % i, [128, 512], BF16) for i in range(nbuf)]
        for q in range(nset):
            self.sets.append((b.sbuf("ln_mean%d" % q, [128, 512], F32), b.sbuf("ln_msq%d" % q, [128, 512], F32),
                              b.sbuf("ln_rstd%d" % q, [128, 512], F32)))
        self.t1 = [b.sbuf("ln_t1_%d" % i, [128, 512], F32) for i in range(nt)]
        self.t2 = [b.sbuf("ln_t2_%d" % i, [128, 512], F32) for i in range(nt)]
        self.cnt = 0


def residual_evac(b, P, ln, i, pso, psk, j, n, s1, s2):
    c = ln.cnt
    ln.cnt += 1
    ybf = ln.ybf[c % ln.nbuf]
    ysq = ln.ysq[c % ln.nbuf]
    sl = slice(n * 512, (n + 1) * 512)
    b.add("vector",
          lambda e: e.scalar_tensor_tensor(out=P.xT[:, j, sl], in0=pso[:, psk, :], scalar=P.dvG[:, i * 8 + j:i * 8 + j + 1],
                                           in1=P.xT[:, j, sl], op0=ALU.mult, op1=ALU.add),
          reads=[(pso, psk), (P.dvG, i), (P.xT, (j, n))], writes=[(P.xT, (j, n))])
    b.add("scalar", lambda e: e.activation(out=ybf[:], in_=P.xT[:, j, sl], func=AF.Identity),
          reads=[(P.xT, (j, n))], writes=[ybf])
    b.add("scalar", lambda e: e.activation(out=ysq[:], in_=P.xT[:, j, sl], func=AF.Square),
          reads=[(P.xT, (j, n))], writes=[ysq])
    b.add("tensor", lambda e: e.matmul(s1[0][:, s1[1], :], lhsT=P.onesb[:], rhs=ybf[:], start=(j == 0), stop=(j == 7)),
          reads=[ybf, P.onesb], writes=[s1])
    b.add("tensor", lambda e: e.matmul(s2[0][:, s2[1], :], lhsT=P.onesb[:], rhs=ysq[:], start=(j == 0), stop=(j == 7)),
          reads=[ysq, P.onesb], writes=[s2])


def ln_finish(b, P, ln, L, i, n, s1, s2, sidx=0, defer=False):
    sl = slice(n * 512, (n + 1) * 512)
    vec = P.vec[L]
    gcol = 72 + i * 8
    bcol = 72 + 24 + i * 8
    mean, msq, rstd = ln.sets[sidx]
    b.add("scalar", lambda e: e.activation(out=mean[:], in_=s1[0][:, s1[1], :], func=AF.Identity),
          reads=[s1], writes=[mean])
    b.add("scalar", lambda e: e.activation(out=msq[:], in_=s1[0][:, s1[1], :], func=AF.Square),
          reads=[s1], writes=[msq])
    b.add("vector", lambda e: e.tensor_tensor(out=msq[:], in0=s2[0][:, s2[1], :], in1=msq[:], op=ALU.subtract),
          reads=[s2, msq], writes=[msq])
    specs = []
    specs.append(("scalar", lambda e: e.activation(out=rstd[:], in_=msq[:], func=AF.Sqrt, bias=P.epsln[:, 0:1]),
                  [msq, P.epsln], [rstd]))
    specs.append(("vector", lambda e: e.reciprocal(out=rstd[:], in_=rstd[:]), [rstd], [rstd]))
    for k in range(8):
        t1 = ln.t1[k % ln.nt]
        t2 = ln.t2[k % ln.nt]
        specs.append(("gpsimd", lambda e, k=k, t1=t1: e.tensor_tensor(out=t1[:], in0=P.xT[:, k, sl], in1=mean[:], op=ALU.subtract),
                      [(P.xT, (k, n)), mean], [t1]))
        specs.append(("vector", lambda e, k=k, t1=t1, t2=t2: e.scalar_tensor_tensor(out=t2[:], in0=t1[:], scalar=vec[:, gcol + k:gcol + k + 1],
                                                                                 in1=rstd[:], op0=ALU.mult, op1=ALU.mult),
                      [t1, rstd, vec], [t2]))
        specs.append(("scalar", lambda e, k=k, t2=t2: e.activation(out=P.xT[:, k, sl], in_=t2[:], func=AF.Identity,
                                                               bias=vec[:, bcol + k:bcol + k + 1]),
                      [t2, vec], [(P.xT, (k, n))]))
    if defer:
        return specs
    emit_specs(b, specs)
    return []


def emit_specs(b, specs, n=None):
    k = len(specs) if n is None else min(n, len(specs))
    for _ in range(k):
        eng, fn, rd, wr = specs.pop(0)
        b.add(eng, fn, reads=rd, writes=wr)


def ffn_phase(b, P, L, i, win, wout):
    with b.phase():
        hT = b.sbuf("ffn_hT", [128, 8, 1024], BF16)
        gT = b.sbuf("ffn_gT", [128, NFF, 1024], BF16)
        wib = [b.sbuf("ffn_wi%d" % r, [128, 2, 8, 128], BF16) for r in range(3)]
        wob = [b.sbuf("ffn_wo%d" % r, [128, NFF, 128], BF16) for r in range(3)]
        sa = [b.sbuf("ffn_sa%d" % r, [128, 512], F32) for r in range(2)]
        ln = LNState(b, P, nset=2)
        PSA, PSB = P.PSA, P.PSB
        pend = []

        def load_wi(m):
            t = wib[m % 3]
            b.add("gpsimd", lambda e: e.dma_start(out=t[:].rearrange("p a k c -> p (a k c)"), in_=win[m]),
                  reads=[], writes=[t], dma=True)

        def load_wo(j):
            t = wob[j % 3]
            b.add("gpsimd", lambda e: e.dma_start(out=t[:].rearrange("p m c -> p (m c)"), in_=wout[j]),
                  reads=[], writes=[t], dma=True)

        for hf in range(2):
            T0 = hf * 1024
            for m in range(3):
                load_wi(m)
            make_h(b, P, i, hT, T0, 1024)
            cnt = 0
            for m in range(NFF):
                w = wib[m % 3]
                for n in range(2):
                    ka = cnt % 2
                    cnt += 1
                    for ab in range(2):
                        bank = 2 * ka + ab
                        for k in range(8):
                            b.add("tensor",
                                  lambda e, w=w, ab=ab, k=k, bank=bank, n=n: e.matmul(
                                      PSB[:, bank, :], lhsT=w[:, ab, k, :], rhs=hT[:, k, n * 512:(n + 1) * 512],
                                      start=(k == 0), stop=(k == 7)),
                                  reads=[w, (hT, k)], writes=[(PSB, bank)])
                    s = sa[ka]
                    b.add("scalar", lambda e, s=s, ka=ka: e.activation(out=s[:], in_=PSB[:, 2 * ka, :], func=AF.Silu),
                          reads=[(PSB, 2 * ka)], writes=[s])
                    b.add("vector",
                          lambda e, s=s, ka=ka, m=m, n=n: e.tensor_tensor(out=gT[:, m, n * 512:(n + 1) * 512], in0=PSB[:, 2 * ka + 1, :],
                                                                            in1=s[:], op=ALU.mult),
                          reads=[(PSB, 2 * ka + 1), s], writes=[(gT, (m, n))])
                    emit_specs(b, pend, 2)
                if m + 3 < NFF:
                    load_wi(m + 3)
            emit_specs(b, pend)
            for j in range(3):
                load_wo(j)
            cnt = 0
            for j in range(8):
                w = wob[j % 3]
                for n in range(2):
                    ko = cnt % 2
                    cnt += 1
                    for m in range(NFF):
                        b.add("tensor",
                              lambda e, w=w, m=m, n=n, ko=ko: e.matmul(PSA[:, ko, :], lhsT=w[:, m, :],
                                                                       rhs=gT[:, m, n * 512:(n + 1) * 512],
                                                                       start=(m == 0), stop=(m == NFF - 1)),
                              reads=[w, (gT, (m, n))], writes=[(PSA, ko)])
                    st = (PSA, PSB)[n]
                    residual_evac(b, P, ln, i, PSA, ko, j, hf * 2 + n, (st, 2), (st, 3))
                if j + 3 < 8:
                    load_wo(j + 3)
            for n in range(2):
                st = (PSA, PSB)[n]
                pend += ln_finish(b, P, ln, L, i, hf * 2 + n, (st, 2), (st, 3), sidx=n, defer=(hf == 0))


def sgu_phase(b, P, L, W):
    vec = P.vec[L]
    BU, LNG2, LNB2 = 120, 136, 152
    with b.phase():
        hT = b.sbuf("sgs_hT", [128, 8, 512], BF16)
        Wv = b.sbuf("sgs_Wv", [128, 8, 2048], BF16)
        tri = b.sbuf("sgs_tri", [128, 128], F32)
        WcT = b.sbuf("sgs_WcT", [128, 2048], BF16)
        Bt = b.sbuf("sgs_Bt", [128, 2048], F32)
        ones1 = b.sbuf("sgs_ones1", [128, 128], BF16)
        orow = b.sbuf("sgs_orow", [1, 128], BF16)
        bvrow = b.sbuf("sgs_bvrow", [1, 2048], BF16)
        uT = b.sbuf("sgs_uT", [128, 16, 512], BF16)
        PT = b.sbuf("sgs_PT", [128, 16, 512], BF16)
        vg = b.sbuf("sgs_vg", [128, 2048], F32)
        vn = b.sbuf("sgs_vn", [128, 2048], BF16)
        vsq = vn
        wsf = vg
        tt = b.sbuf("sgs_tt", [128, 1024], F32)
        st = b.sbuf("sgs_st", [128, 8], F32)
        wub = [b.sbuf("sgs_wu%d" % r, [128, 8, 128], BF16) for r in range(2)]
        wob = [b.sbuf("sgs_wo%d" % r, [128, 16, 128], BF16) for r in range(2)]
        ln = LNState(b, P, nbuf=2, nt=1, nset=2)
        PSA, PSB = P.PSA, P.PSB
        pend = []
        for k in range(8):
            b.add("gpsimd", lambda e, k=k: e.dma_start(out=Wv[:, k, :], in_=W["wv"][k]), reads=[], writes=[(Wv, k)], dma=True)
        b.add("sync", lambda e: e.dma_start(out=wsf[:], in_=W["wsT"][:]), reads=[], writes=[wsf], dma=True)
        b.add("sync", lambda e: e.dma_start(out=tri[:], in_=P.cst_d[:, 128:256]), reads=[], writes=[tri], dma=True)
        b.add("sync", lambda e: e.dma_start(out=Bt[:], in_=W["bsb"][:]), reads=[], writes=[Bt], dma=True)
        b.add("gpsimd", lambda e: e.dma_start(out=bvrow[:], in_=W["bvrow"][:]), reads=[], writes=[bvrow], dma=True)
        b.add("vector", lambda e: e.memset(ones1[:], 1.0), reads=[], writes=[ones1])
        b.add("vector", lambda e: e.memset(orow[:], 1.0), reads=[], writes=[orow])
        b.add("vector", lambda e: e.tensor_tensor(out=WcT[:].rearrange("p (g t) -> p g t", g=16),
                                                  in0=wsf[:].rearrange("p (g t) -> p g t", g=16),
                                                  in1=tri[:].unsqueeze(1).to_broadcast([128, 16, 128]), op=ALU.mult),
              reads=[wsf, tri], writes=[WcT])
        for q in range(4):
            b.add("tensor", lambda e, q=q: e.matmul(PSA[:, q, :], lhsT=ones1[:], rhs=WcT[:, q * 512:(q + 1) * 512], start=True, stop=True),
                  reads=[ones1, WcT], writes=[(PSA, q)])
        b.add("vector", lambda e: e.tensor_tensor(out=vg[:].rearrange("p (g t) -> p g t", g=16),
                                                  in0=PSA[:].rearrange("p q (g t) -> p (q g) t", g=4),
                                                  in1=vec[:, LNB2:LNB2 + 16].unsqueeze(2).to_broadcast([128, 16, 128]), op=ALU.mult),
              reads=[PSA, vec], writes=[vg])
        b.add("vector", lambda e: e.tensor_tensor(out=Bt[:], in0=Bt[:], in1=vg[:], op=ALU.add), reads=[Bt, vg], writes=[Bt])

        def load_wu(ii):
            t = wub[ii % 2]
            b.add("gpsimd", lambda e: e.dma_start(out=t[:].rearrange("p k c -> p (k c)"), in_=W["wu"][ii % 16]),
                  reads=[], writes=[t], dma=True)

        def load_wo(jj):
            t = wob[jj % 2]
            b.add("gpsimd", lambda e: e.dma_start(out=t[:].rearrange("p i c -> p (i c)"), in_=W["wo"][jj % 8]),
                  reads=[], writes=[t], dma=True)

        for G in range(4):
            make_h(b, P, 1, hT, G * 512, 512)
            for ii in range(2):
                load_wu(G * 16 + ii)
            for i in range(16):
                w = wub[(G * 16 + i) % 2]
                bk = i % 2
                for k in range(8):
                    b.add("tensor", lambda e, w=w, k=k, bk=bk: e.matmul(PSB[:, bk, :], lhsT=w[:, k, :], rhs=hT[:, k, :],
                                                                       start=(k == 0), stop=(k == 7)),
                          reads=[w, (hT, k)], writes=[(PSB, bk)])
                b.add("scalar", lambda e, i=i, bk=bk: e.activation(out=uT[:, i, :], in_=PSB[:, bk, :], func=AF.Gelu,
                                                                   bias=vec[:, BU + i:BU + i + 1]),
                      reads=[(PSB, bk), vec], writes=[(uT, i)])
                emit_specs(b, pend, 2)
                if i + 2 < 16:
                    load_wu(G * 16 + i + 2)
            emit_specs(b, pend)
            for j in range(2):
                load_wo(G * 8 + j)
            def v_proj(cc):
                tk = slice(cc * 128, (cc + 1) * 128)
                for nb in range(4):
                    for k in range(8):
                        b.add("tensor", lambda e, nb=nb, k=k, tk=tk: e.matmul(PSA[:, nb, :], lhsT=hT[:, k, tk], rhs=Wv[:, k, nb * 512:(nb + 1) * 512],
                                                                             start=(k == 0), stop=False),
                              reads=[(hT, k), (Wv, k)], writes=[(PSA, nb)])
                    b.add("tensor", lambda e, nb=nb: e.matmul(PSA[:, nb, :], lhsT=orow[0:1, :], rhs=bvrow[0:1, nb * 512:(nb + 1) * 512],
                                                             start=False, stop=True),
                          reads=[orow, bvrow], writes=[(PSA, nb)])

            v_proj(0)
            for cc in range(4):
                tk = slice(cc * 128, (cc + 1) * 128)
                b.add("scalar", lambda e: e.activation(out=vg[:], in_=PSA[:].rearrange("p q n -> p (q n)"), func=AF.Gelu),
                      reads=[PSA], writes=[vg])
                b.add("vector", lambda e: e.reduce_sum(out=st[:, 0:1], in_=vg[:], axis=AX.X), reads=[vg], writes=[(st, 0)])
                b.add("scalar", lambda e: e.activation(out=vsq[:], in_=vg[:], func=AF.Square), reads=[vg], writes=[vsq])
                b.add("vector", lambda e: e.reduce_sum(out=st[:, 1:2], in_=vsq[:], axis=AX.X), reads=[vsq], writes=[(st, 1)])
                b.add("vector", lambda e: e.tensor_scalar(out=st[:, 2:3], in0=st[:, 0:1], scalar1=1.0 / 2048.0, scalar2=None, op0=ALU.mult),
                      reads=[(st, 0)], writes=[(st, 2)])
                b.add("vector", lambda e: e.tensor_tensor(out=st[:, 3:4], in0=st[:, 2:3], in1=st[:, 2:3], op=ALU.mult),
                      reads=[(st, 2)], writes=[(st, 3)])
                b.add("vector", lambda e: e.scalar_tensor_tensor(out=st[:, 4:5], in0=st[:, 1:2], scalar=1.0 / 2048.0, in1=st[:, 3:4],
                                                                 op0=ALU.mult, op1=ALU.subtract),
                      reads=[(st, 1), (st, 3)], writes=[(st, 4)])
                b.add("scalar", lambda e: e.activation(out=st[:, 5:6], in_=st[:, 4:5], func=AF.Sqrt, bias=P.eps5[:, 0:1]),
                      reads=[(st, 4), P.eps5], writes=[(st, 5)])
                b.add("vector", lambda e: e.reciprocal(out=st[:, 6:7], in_=st[:, 5:6]), reads=[(st, 5)], writes=[(st, 6)])
                b.add("vector", lambda e: e.scalar_tensor_tensor(out=st[:, 7:8], in0=st[:, 2:3], scalar=-1.0, in1=st[:, 6:7],
                                                                 op0=ALU.mult, op1=ALU.mult),
                      reads=[(st, 2), (st, 6)], writes=[(st, 7)])
                b.add("scalar", lambda e: e.activation(out=vn[:], in_=vg[:], func=AF.Identity, scale=st[:, 6:7], bias=st[:, 7:8]),
                      reads=[vg, (st, 6), (st, 7)], writes=[vn])
                if cc + 1 < 4:
                    v_proj(cc + 1)
                for hh in range(2):
                    for g8 in range(8):
                        g = hh * 8 + g8
                        col = g8 * 128
                        b.add("tensor", lambda e, g=g, hh=hh, col=col: e.matmul(
                            PSB[:, 2 * hh + col // 512, (col % 512):(col % 512) + 128], lhsT=vn[:, g * 128:(g + 1) * 128],
                            rhs=WcT[:, g * 128:(g + 1) * 128], start=True, stop=True),
                              reads=[vn, WcT], writes=[(PSB, 2 * hh + col // 512)])
                    hs = slice(hh * 1024, (hh + 1) * 1024)
                    b.add("vector", lambda e, hh=hh, hs=hs: e.tensor_tensor(out=tt[:].rearrange("p (g t) -> p g t", g=8),
                                                                          in0=PSB[:, 2 * hh:2 * hh + 2, :].rearrange("p q (g t) -> p (q g) t", t=128),
                                                                          in1=vec[:, LNG2 + hh * 8:LNG2 + hh * 8 + 8].unsqueeze(2).to_broadcast([128, 8, 128]),
                                                                          op=ALU.mult),
                          reads=[(PSB, 2 * hh), (PSB, 2 * hh + 1), vec], writes=[tt])
                    b.add("gpsimd", lambda e, hs=hs: e.tensor_tensor(out=tt[:], in0=tt[:], in1=Bt[:, hs], op=ALU.add),
                          reads=[tt, Bt], writes=[tt])
                    b.add("vector", lambda e, hh=hh, tk=tk: e.tensor_tensor(out=PT[:, hh * 8:(hh + 1) * 8, tk],
                                                                          in0=tt[:].rearrange("p (g t) -> p g t", g=8),
                                                                          in1=uT[:, hh * 8:(hh + 1) * 8, tk], op=ALU.mult),
                          reads=[tt] + [(uT, hh * 8 + q) for q in range(8)], writes=[(PT, (hh, cc))])
            for j in range(8):
                w = wob[(G * 8 + j) % 2]
                ko = j % 2
                for i in range(16):
                    b.add("tensor", lambda e, w=w, i=i, ko=ko: e.matmul(PSB[:, ko, :], lhsT=w[:, i, :], rhs=PT[:, i, :],
                                                                       start=(i == 0), stop=(i == 15)),
                          reads=[w] + [(PT, (i // 8, q)) for q in range(4)], writes=[(PSB, ko)])
                residual_evac(b, P, ln, 1, PSB, ko, j, G, (PSB, 2), (PSB, 3))
                if j + 2 < 8:
                    load_wo(G * 8 + j + 2)
            pend += ln_finish(b, P, ln, L, 1, G, (PSB, 2), (PSB, 3), sidx=G % 2, defer=(G < 3))


def ssd_phase(b, P, W):
    vec = P.vec[0]
    CW, CB, NW = 120, 168, 180
    with b.phase():
        hT = b.sbuf("sd_hT", [128, 8, 512], BF16)
        Wx = b.sbuf("sd_Wx", [128, 8, 1536], BF16)
        Wz = b.sbuf("sd_Wz", [128, 8, 1024], BF16)
        Wd = b.sbuf("sd_Wd", [128, 8, 16], BF16)
        ident = b.sbuf("sd_ident", [128, 128], BF16)
        trib = b.sbuf("sd_trib", [128, 128], BF16)
        maskb = b.sbuf("sd_maskb", [128, 512], BF16)
        onesb1 = b.sbuf("sd_onesb1", [128, 128], BF16)
        adth = b.sbuf("sd_adth", [128, 16], BF16)
        adtl = b.sbuf("sd_adtl", [128, 16], BF16)
        RH = b.sbuf("sd_RH", [128, 2048], BF16)
        RL = b.sbuf("sd_RL", [128, 2048], BF16)
        one1 = b.sbuf("sd_one1", [128, 1], F32)
        tokc = b.sbuf("sd_tokc", [128, 1056], F32)
        halo = b.sbuf("sd_halo", [128, 12, 3], F32)
        xr = [b.sbuf("sd_xr%d" % i, [128, 515], F32) for i in range(2)]
        acc = [b.sbuf("sd_acc%d" % i, [128, 512], F32) for i in range(2)]
        xbcs = b.sbuf("sd_xbcs", [128, 12, 512], BF16)
        small = b.sbuf("sd_small", [128, 16 * 13], F32)
        Dm = b.sbuf("sd_Dm", [128, 2048], F32)
        MT = b.sbuf("sd_MT", [128, 2048], BF16)
        cbT = b.sbuf("sd_cbT", [128, 256], F32)
        zs = b.sbuf("sd_zs", [128, 1024], F32)
        xst = b.sbuf("sd_xst", [128, 1024], BF16)
        Btok = b.sbuf("sd_Btok", [128, 256], BF16)
        y1 = b.sbuf("sd_y1", [128, 1024], F32)
        tmp = b.sbuf("sd_tmp", [128, 1024], F32)
        xcd = b.sbuf("sd_xcd", [128, 1024], BF16)
        state = b.sbuf("sd_state", [128, 1024], F32)
        statebf = b.sbuf("sd_statebf", [128, 1024], BF16)
        yn = b.sbuf("sd_yn", [128, 1024], BF16)
        ysT = b.sbuf("sd_ysT", [128, 8, 512], BF16)
        PSA, PSB = P.PSA, P.PSB
        R3 = PSA[:].rearrange("p q (h l) -> p (q h) l", l=128)

        def sm(i, n=16):
            return small[:, i * 16:i * 16 + n]

        lim = [None]

        def A(*a, **k):
            if lim[0] is None:
                return b.add(*a, **k)
            if lim[0] > 0:
                lim[0] -= 1
                return b.add(*a, **k)
            return None

        def smk(i):
            return (small, i)

        for k in range(8):
            b.add("gpsimd", lambda e, k=k: e.dma_start(out=Wx[:, k, :], in_=W["wxbc"][k]), reads=[], writes=[(Wx, k)], dma=True)
            b.add("gpsimd", lambda e, k=k: e.dma_start(out=Wz[:, k, :], in_=W["wz"][k]), reads=[], writes=[(Wz, k)], dma=True)
            b.add("gpsimd", lambda e, k=k: e.dma_start(out=Wd[:, k, :], in_=W["wdt"][k]), reads=[], writes=[(Wd, k)], dma=True)
        b.add("gpsimd", lambda e: e.dma_start(out=ident[:], in_=P.cst_d[:, 0:128]), reads=[], writes=[ident], dma=True)
        b.add("gpsimd", lambda e: e.dma_start(out=trib[:], in_=P.cst_d[:, 128:256]), reads=[], writes=[trib], dma=True)
        for q in range(4):
            b.add("gpsimd", lambda e, q=q: e.dma_start(out=maskb[:, q * 128:(q + 1) * 128], in_=P.cst_d[:, 256:384]),
                  reads=[], writes=[(maskb, q)], dma=True)
        b.add("sync", lambda e: e.dma_start(out=tokc[:], in_=W["tokc"][:]), reads=[], writes=[tokc], dma=True)
        b.add("vector", lambda e: e.memset(onesb1[:], 1.0), reads=[], writes=[onesb1])
        if SSD_CUT is not None:
            b.add("vector", lambda e: e.memset(ysT[:], 0.0), reads=[], writes=[ysT])
        b.add("vector", lambda e: e.memset(one1[:], 1.0), reads=[], writes=[one1])
        b.add("vector", lambda e: e.memset(state[:], 0.0), reads=[], writes=[state])
        b.add("vector", lambda e: e.memset(statebf[:], 0.0), reads=[], writes=[statebf])
        b.add("scalar", lambda e: e.activation(out=sm(11), in_=tokc[:, 1040:1056], func=AF.Exp), reads=[tokc], writes=[smk(11)])
        b.add("vector", lambda e: e.tensor_scalar(out=sm(11), in0=sm(11), scalar1=-1.0, scalar2=None, op0=ALU.mult),
              reads=[smk(11)], writes=[smk(11)])

        for G in range(4):
            make_h(b, P, 1, hT, G * 512, 512)
            for q in range(12):
                bk = q % 2
                for k in range(8):
                    b.add("tensor", lambda e, q=q, k=k, bk=bk: e.matmul(PSB[:, bk, :], lhsT=Wx[:, k, q * 128:(q + 1) * 128], rhs=hT[:, k, :],
                                                                       start=(k == 0), stop=(k == 7)),
                          reads=[(Wx, k), (hT, k)], writes=[(PSB, bk)])
                x_r = xr[q % 2]
                a = acc[q % 2]
                if G == 0:
                    b.add("vector", lambda e, x_r=x_r: e.memset(x_r[:, 0:3], 0.0), reads=[], writes=[(x_r, "h")])
                else:
                    b.add("vector", lambda e, x_r=x_r, q=q: e.tensor_copy(out=x_r[:, 0:3], in_=halo[:, q, :]),
                          reads=[(halo, q)], writes=[(x_r, "h")])
                b.add("scalar", lambda e, x_r=x_r, bk=bk: e.activation(out=x_r[:, 3:515], in_=PSB[:, bk, :], func=AF.Identity),
                      reads=[(PSB, bk)], writes=[(x_r, "b")])
                if G < 3:
                    b.add("vector", lambda e, x_r=x_r, q=q: e.tensor_copy(out=halo[:, q, :], in_=x_r[:, 512:515]),
                          reads=[(x_r, "b")], writes=[(halo, q)])
                b.add("vector", lambda e, x_r=x_r, a=a, q=q: e.tensor_scalar(out=a[:], in0=x_r[:, 0:512], scalar1=vec[:, CW + q:CW + q + 1],
                                                                             scalar2=None, op0=ALU.mult),
                      reads=[(x_r, "h"), (x_r, "b"), vec], writes=[a])
                for kk in range(1, 4):
                    b.add("vector", lambda e, x_r=x_r, a=a, q=q, kk=kk: e.scalar_tensor_tensor(
                        out=a[:], in0=x_r[:, kk:kk + 512], scalar=vec[:, CW + kk * 12 + q:CW + kk * 12 + q + 1], in1=a[:],
                        op0=ALU.mult, op1=ALU.add),
                          reads=[(x_r, "h"), (x_r, "b"), vec, a], writes=[a])
                b.add("scalar", lambda e, a=a, q=q: e.activation(out=xbcs[:, q, :], in_=a[:], func=AF.Silu, bias=vec[:, CB + q:CB + q + 1]),
                      reads=[a, vec], writes=[(xbcs, q)])
            for cc in range(4):
                tk = slice(cc * 128, (cc + 1) * 128)
                if SSD_CUT is not None and SSD_CUT < 1:
                    continue
                for nb in range(2):
                    for k in range(8):
                        b.add("tensor", lambda e, nb=nb, k=k, tk=tk: e.matmul(PSB[:, 1 + nb, :], lhsT=hT[:, k, tk], rhs=Wz[:, k, nb * 512:(nb + 1) * 512],
                                                                             start=(k == 0), stop=(k == 7)),
                              reads=[(hT, k), (Wz, k)], writes=[(PSB, 1 + nb)])
                b.add("scalar", lambda e: e.activation(out=zs[:], in_=PSB[:, 1:3, :].rearrange("p q n -> p (q n)"), func=AF.Silu),
                      reads=[(PSB, 1), (PSB, 2)], writes=[zs])
                if SSD_CUT is not None and SSD_CUT < 2:
                    continue
                for k in range(8):
                    b.add("tensor", lambda e, k=k, tk=tk: e.matmul(PSB[:, 0, 0:16], lhsT=hT[:, k, tk], rhs=Wd[:, k, :], start=(k == 0), stop=(k == 7)),
                          reads=[(hT, k), (Wd, k)], writes=[(PSB, 0)])
                b.add("vector", lambda e: e.tensor_tensor(out=sm(0), in0=PSB[:, 0, 0:16], in1=tokc[:, 1024:1040], op=ALU.add),
                      reads=[(PSB, 0), tokc], writes=[smk(0)])
                b.add("scalar", lambda e: e.activation(out=sm(1), in_=sm(0), func=AF.Abs), reads=[smk(0)], writes=[smk(1)])
                b.add("scalar", lambda e: e.activation(out=sm(2), in_=sm(1), func=AF.Exp, scale=-1.0), reads=[smk(1)], writes=[smk(2)])
                b.add("scalar", lambda e: e.activation(out=sm(3), in_=sm(2), func=AF.Ln, bias=one1[:, 0:1]), reads=[smk(2), one1], writes=[smk(3)])
                b.add("vector", lambda e: e.scalar_tensor_tensor(out=sm(4), in0=sm(0), scalar=0.0, in1=sm(3), op0=ALU.max, op1=ALU.add),
                      reads=[smk(0), smk(3)], writes=[smk(4)])
                b.add("scalar", lambda e: e.activation(out=sm(5), in_=sm(4), func=AF.Ln), reads=[smk(4)], writes=[smk(5)])
                b.add("vector", lambda e: e.tensor_tensor(out=sm(6), in0=sm(4), in1=sm(11), op=ALU.mult), reads=[smk(4), smk(11)], writes=[smk(6)])
                if SSD_CUT is not None and SSD_CUT < 3:
                    continue
                lim[0] = SSD_SUB
                A("scalar", lambda e: e.activation(out=adth[:], in_=sm(6), func=AF.Identity), reads=[smk(6)], writes=[adth])
                A("vector", lambda e: e.tensor_tensor(out=adtl[:], in0=sm(6), in1=adth[:], op=ALU.subtract), reads=[smk(6), adth], writes=[adtl])
                for (src, dst) in ((adth, RH), (adtl, RL)):
                    A("vector", lambda e, src=src, dst=dst: e.tensor_tensor(out=dst[:].rearrange("p (h l) -> p h l", l=128),
                                                                              in0=src[:].unsqueeze(2).to_broadcast([128, 16, 128]),
                                                                              in1=trib[:].unsqueeze(1).to_broadcast([128, 16, 128]), op=ALU.mult),
                          reads=[src, trib], writes=[dst])
                for q4 in range(4):
                    A("tensor", lambda e, q4=q4: e.matmul(PSA[:, q4, :], lhsT=onesb1[:], rhs=RH[:, q4 * 512:(q4 + 1) * 512], start=True, stop=False),
                          reads=[onesb1, RH], writes=[(PSA, q4)])
                    A("tensor", lambda e, q4=q4: e.matmul(PSA[:, q4, :], lhsT=onesb1[:], rhs=RL[:, q4 * 512:(q4 + 1) * 512], start=False, stop=False),
                          reads=[onesb1, RL], writes=[(PSA, q4)])
                    A("tensor", lambda e, q4=q4: e.matmul(PSA[:, q4, :], lhsT=ident[:], rhs=maskb[:], start=False, stop=True),
                          reads=[ident, maskb], writes=[(PSA, q4)])
                A("tensor", lambda e: e.matmul(PSB[:, 0, 16:32], lhsT=trib[:], rhs=adth[:], start=True, stop=False),
                      reads=[trib, adth], writes=[(PSB, 0)])
                A("tensor", lambda e: e.matmul(PSB[:, 0, 16:32], lhsT=trib[:], rhs=adtl[:], start=False, stop=True),
                      reads=[trib, adtl], writes=[(PSB, 0)])
                A("vector", lambda e: e.tensor_tensor(out=sm(7), in0=PSB[:, 0, 16:32], in1=sm(5), op=ALU.subtract),
                      reads=[(PSB, 0), smk(5)], writes=[smk(7)])
                A("scalar", lambda e: e.activation(out=sm(8), in_=PSB[:, 0, 16:32], func=AF.Exp), reads=[(PSB, 0)], writes=[smk(8)])
                A("vector", lambda e: e.tensor_tensor(out=sm(9), in0=R3[:, :, 127], in1=sm(7), op=ALU.subtract),
                      reads=[PSA, smk(7)], writes=[smk(9)])
                A("scalar", lambda e: e.activation(out=sm(9), in_=sm(9), func=AF.Exp), reads=[smk(9)], writes=[smk(9)])
                A("scalar", lambda e: e.activation(out=sm(10), in_=R3[:, :, 127], func=AF.Exp), reads=[PSA], writes=[smk(10)])
                A("vector", lambda e: e.tensor_tensor(out=Dm[:].rearrange("p (h l) -> p h l", l=128), in0=R3,
                                                          in1=sm(7).unsqueeze(2).to_broadcast([128, 16, 128]), op=ALU.subtract),
                      reads=[PSA, smk(7)], writes=[Dm])
                A("scalar", lambda e: e.activation(out=Dm[:], in_=Dm[:], func=AF.Exp), reads=[Dm], writes=[Dm])
                if SSD_CUT is not None and SSD_CUT < 5:
                    continue
                for g in range(2):
                    b.add("tensor", lambda e, g=g, tk=tk: e.matmul(PSB[:, 0, 128 + g * 128:256 + g * 128], lhsT=xbcs[:, 8 + g, tk], rhs=xbcs[:, 10 + g, tk],
                                                                 start=True, stop=True),
                          reads=[(xbcs, 8 + g), (xbcs, 10 + g)], writes=[(PSB, 0)])
                b.add("scalar", lambda e: e.activation(out=cbT[:], in_=PSB[:, 0, 128:384], func=AF.Identity), reads=[(PSB, 0)], writes=[cbT])
                b.add("vector", lambda e: e.tensor_tensor(out=MT[:].rearrange("p (g r l) -> p g r l", g=2, r=8),
                                                          in0=Dm[:].rearrange("p (g r l) -> p g r l", g=2, r=8),
                                                          in1=cbT[:].rearrange("p (g l) -> p g l", g=2).unsqueeze(2).to_broadcast([128, 2, 8, 128]),
                                                          op=ALU.mult),
                      reads=[Dm, cbT], writes=[MT])
                if SSD_CUT is not None and SSD_CUT < 6:
                    continue
                for q in range(8):
                    b.add("tensor", lambda e, q=q, tk=tk: e.matmul(PSB[:, 1 + q // 4, (q % 4) * 128:(q % 4) * 128 + 128], lhsT=xbcs[:, q, tk], rhs=ident[:],
                                                                 start=True, stop=True),
                          reads=[(xbcs, q), ident], writes=[(PSB, 1 + q // 4)])
                b.add("scalar", lambda e: e.activation(out=xst[:], in_=PSB[:, 1:3, :].rearrange("p q n -> p (q n)"), func=AF.Identity),
                      reads=[(PSB, 1), (PSB, 2)], writes=[xst])
                for g in range(2):
                    b.add("tensor", lambda e, g=g, tk=tk: e.matmul(PSB[:, 3, g * 128:(g + 1) * 128], lhsT=xbcs[:, 8 + g, tk], rhs=ident[:], start=True, stop=True),
                          reads=[(xbcs, 8 + g), ident], writes=[(PSB, 3)])
                b.add("scalar", lambda e: e.activation(out=Btok[:], in_=PSB[:, 3, 0:256], func=AF.Identity), reads=[(PSB, 3)], writes=[Btok])
                if SSD_CUT is not None and SSD_CUT < 7:
                    continue
                for h in range(16):
                    b.add("tensor", lambda e, h=h: e.matmul(PSA[:, h // 8, (h % 8) * 64:(h % 8) * 64 + 64], lhsT=MT[:, h * 128:(h + 1) * 128],
                                                           rhs=xst[:, h * 64:(h + 1) * 64], start=True, stop=True),
                          reads=[MT, xst], writes=[(PSA, h // 8)])
                for g in range(2):
                    b.add("tensor", lambda e, g=g, tk=tk: e.matmul(PSA[:, 2 + g, :], lhsT=xbcs[:, 10 + g, tk], rhs=statebf[:, g * 512:(g + 1) * 512],
                                                                 start=True, stop=True),
                          reads=[(xbcs, 10 + g), statebf], writes=[(PSA, 2 + g)])
                b.add("vector", lambda e: e.tensor_tensor(out=y1[:].rearrange("p (h d) -> p h d", d=64),
                                                          in0=PSA[:, 2:4, :].rearrange("p q (h d) -> p (q h) d", d=64),
                                                          in1=sm(8).unsqueeze(2).to_broadcast([128, 16, 64]), op=ALU.mult),
                      reads=[(PSA, 2), (PSA, 3), smk(8)], writes=[y1])
                b.add("vector", lambda e: e.tensor_tensor(out=y1[:], in0=y1[:], in1=PSA[:, 0:2, :].rearrange("p q n -> p (q n)"), op=ALU.add),
                      reads=[y1, (PSA, 0), (PSA, 1)], writes=[y1])
                b.add("vector", lambda e: e.tensor_tensor(out=tmp[:], in0=xst[:], in1=tokc[:, 0:1024], op=ALU.mult), reads=[xst, tokc], writes=[tmp])
                b.add("vector", lambda e: e.tensor_tensor(out=y1[:], in0=y1[:], in1=tmp[:], op=ALU.add), reads=[y1, tmp], writes=[y1])
                b.add("vector", lambda e: e.tensor_tensor(out=y1[:], in0=y1[:], in1=zs[:], op=ALU.mult), reads=[y1, zs], writes=[y1])
                if SSD_CUT is not None and SSD_CUT < 8:
                    continue
                b.add("vector", lambda e: e.tensor_tensor(out=xcd[:].rearrange("p (h d) -> p h d", d=64), in0=xst[:].rearrange("p (h d) -> p h d", d=64),
                                                          in1=sm(9).unsqueeze(2).to_broadcast([128, 16, 64]), op=ALU.mult),
                      reads=[xst, smk(9)], writes=[xcd])
                for g in range(2):
                    b.add("tensor", lambda e, g=g: e.matmul(PSB[:, 1 + g, :], lhsT=Btok[:, g * 128:(g + 1) * 128], rhs=xcd[:, g * 512:(g + 1) * 512],
                                                           start=True, stop=True),
                          reads=[Btok, xcd], writes=[(PSB, 1 + g)])
                b.add("vector", lambda e: e.tensor_tensor(out=state[:].rearrange("p (h d) -> p h d", d=64), in0=state[:].rearrange("p (h d) -> p h d", d=64),
                                                          in1=sm(10).unsqueeze(2).to_broadcast([128, 16, 64]), op=ALU.mult),
                      reads=[state, smk(10)], writes=[state])
                b.add("vector", lambda e: e.tensor_tensor(out=state[:], in0=state[:], in1=PSB[:, 1:3, :].rearrange("p q n -> p (q n)"), op=ALU.add),
                      reads=[state, (PSB, 1), (PSB, 2)], writes=[state])
                b.add("scalar", lambda e: e.activation(out=statebf[:], in_=state[:], func=AF.Identity), reads=[state], writes=[statebf])
                if SSD_CUT is not None and SSD_CUT < 9:
                    continue
                b.add("scalar", lambda e: e.activation(out=tmp[:], in_=y1[:], func=AF.Square), reads=[y1], writes=[tmp])
                for g in range(2):
                    b.add("vector", lambda e, g=g: e.reduce_sum(out=small[:, 192 + g:193 + g], in_=tmp[:, g * 512:(g + 1) * 512], axis=AX.X),
                          reads=[tmp], writes=[(small, 12)])
                b.add("scalar", lambda e: e.activation(out=small[:, 194:196], in_=small[:, 192:194], func=AF.Sqrt, scale=1.0 / 512.0, bias=P.eps5[:, 0:1]),
                      reads=[(small, 12), P.eps5], writes=[(small, 12)])
                b.add("vector", lambda e: e.reciprocal(out=small[:, 196:198], in_=small[:, 194:196]), reads=[(small, 12)], writes=[(small, 12)])
                for g in range(2):
                    b.add("scalar", lambda e, g=g: e.activation(out=yn[:, g * 512:(g + 1) * 512], in_=y1[:, g * 512:(g + 1) * 512], func=AF.Identity,
                                                                scale=small[:, 196 + g:197 + g]),
                          reads=[y1, (small, 12)], writes=[(yn, g)])
                for q in range(8):
                    b.add("tensor", lambda e, q=q: e.matmul(PSB[:, 1 + q // 4, (q % 4) * 128:(q % 4) * 128 + 128], lhsT=yn[:, q * 128:(q + 1) * 128], rhs=ident[:],
                                                           start=True, stop=True),
                          reads=[(yn, q // 4), ident], writes=[(PSB, 1 + q // 4)])
                b.add("vector", lambda e, tk=tk: e.tensor_tensor(out=ysT[:, :, tk], in0=PSB[:, 1:3, :].rearrange("p q (c t) -> p (q c) t", t=128),
                                                               in1=vec[:, NW:NW + 8].unsqueeze(2).to_broadcast([128, 8, 128]), op=ALU.mult),
                      reads=[(PSB, 1), (PSB, 2), vec], writes=[(ysT, cc)])
            for q in range(8):
                b.add("sync", lambda e, q=q, G=G: e.dma_start(out=P.ymix[q][:, G * 512:(G + 1) * 512], in_=ysT[:, q, :]),
                      reads=[(ysT, c4) for c4 in range(4)], writes=[("ymix", (q, G))], dma=True)


QSCALE = 192.0 ** -0.5


def mla1_phase(b, P, W):
    vec = P.vec[0]
    QW, KW = 188, 191
    with b.phase():
        hT = b.sbuf("m1_hT", [128, 8, 512], BF16)
        Wl = b.sbuf("m1_Wl", [128, 8, 704], BF16)
        rope = b.sbuf("m1_rope", [64, 2, S], F32)
        Rm = b.sbuf("m1_Rm", [64, 64], BF16)
        onq = b.sbuf("m1_onq", [128, 128], BF16)
        onk = b.sbuf("m1_onk", [128, 128], BF16)
        lat = b.sbuf("m1_lat", [128, 5, 512], F32)
        sq = b.sbuf("m1_sq", [128, 5, 512], BF16)
        krf = b.sbuf("m1_krf", [64, 512], F32)
        krh = b.sbuf("m1_krh", [64, 512], BF16)
        krl = b.sbuf("m1_krl", [64, 512], BF16)
        t1 = b.sbuf("m1_t1", [64, 512], F32)
        t2 = b.sbuf("m1_t2", [64, 512], F32)
        rs = b.sbuf("m1_rs", [128, 2, 512], F32)
        outn = b.sbuf("m1_outn", [128, 5, 512], BF16)
        kpo = b.sbuf("m1_kpo", [64, 512], BF16)
        PSA, PSB = P.PSA, P.PSB
        for k in range(8):
            b.add("gpsimd", lambda e, k=k: e.dma_start(out=Wl[:, k, :], in_=W["wlat"][k]), reads=[], writes=[(Wl, k)], dma=True)
        b.add("sync", lambda e: e.dma_start(out=rope[:], in_=W["rope"][:]), reads=[], writes=[rope], dma=True)
        b.add("gpsimd", lambda e: e.dma_start(out=Rm[:], in_=P.cst_d[0:64, 384:448]), reads=[], writes=[Rm], dma=True)
        b.add("vector", lambda e: e.memset(onq[:], 1.0 / 384.0), reads=[], writes=[onq])
        b.add("vector", lambda e: e.memset(onk[:], 1.0 / 256.0), reads=[], writes=[onk])
        for G in range(4):
            ts = slice(G * 512, (G + 1) * 512)
            make_h(b, P, 1, hT, G * 512, 512)
            for c in range(5):
                bk = c % 2
                for k in range(8):
                    b.add("tensor", lambda e, c=c, k=k, bk=bk: e.matmul(PSB[:, bk, :], lhsT=Wl[:, k, c * 128:(c + 1) * 128], rhs=hT[:, k, :],
                                                                       start=(k == 0), stop=(k == 7)),
                          reads=[(Wl, k), (hT, k)], writes=[(PSB, bk)])
                b.add("scalar", lambda e, c=c, bk=bk: e.activation(out=lat[:, c, :], in_=PSB[:, bk, :], func=AF.Identity),
                      reads=[(PSB, bk)], writes=[(lat, c)])
                b.add("scalar", lambda e, c=c: e.activation(out=sq[:, c, :], in_=lat[:, c, :], func=AF.Square),
                      reads=[(lat, c)], writes=[(sq, c)])
            for k in range(8):
                b.add("tensor", lambda e, k=k: e.matmul(PSB[0:64, 2, :], lhsT=Wl[:, k, 640:704], rhs=hT[:, k, :], start=(k == 0), stop=(k == 7)),
                      reads=[(Wl, k), (hT, k)], writes=[(PSB, 2)])
            b.add("scalar", lambda e: e.activation(out=krf[:], in_=PSB[0:64, 2, :], func=AF.Identity), reads=[(PSB, 2)], writes=[krf])
            for (c0, nchunk, on, bank, wcol, r) in ((0, 3, onq, 0, QW, 0), (3, 2, onk, 1, KW, 1)):
                for c in range(nchunk):
                    b.add("tensor", lambda e, c=c, c0=c0, on=on, bank=bank, nchunk=nchunk: e.matmul(PSA[:, bank, :], lhsT=on[:], rhs=sq[:, c0 + c, :],
                                                                                                 start=(c == 0), stop=(c == nchunk - 1)),
                          reads=[on, (sq, c0 + c)], writes=[(PSA, bank)])
                b.add("scalar", lambda e, bank=bank, r=r: e.activation(out=rs[:, r, :], in_=PSA[:, bank, :], func=AF.Sqrt, bias=P.eps5[:, 0:1]),
                      reads=[(PSA, bank), P.eps5], writes=[(rs, r)])
                b.add("vector", lambda e, r=r: e.reciprocal(out=rs[:, r, :], in_=rs[:, r, :]), reads=[(rs, r)], writes=[(rs, r)])
                for c in range(nchunk):
                    b.add("vector", lambda e, c=c, c0=c0, wcol=wcol, r=r: e.scalar_tensor_tensor(
                        out=outn[:, c0 + c, :], in0=lat[:, c0 + c, :], scalar=vec[:, wcol + c:wcol + c + 1], in1=rs[:, r, :],
                        op0=ALU.mult, op1=ALU.mult),
                          reads=[(lat, c0 + c), vec, (rs, r)], writes=[(outn, c0 + c)])
            b.add("scalar", lambda e: e.activation(out=krh[:], in_=krf[:], func=AF.Identity), reads=[krf], writes=[krh])
            b.add("vector", lambda e: e.tensor_tensor(out=krl[:], in0=krf[:], in1=krh[:], op=ALU.subtract), reads=[krf, krh], writes=[krl])
            b.add("tensor", lambda e: e.matmul(PSA[0:64, 2, :], lhsT=Rm[:], rhs=krh[:], start=True, stop=False), reads=[Rm, krh], writes=[(PSA, 2)])
            b.add("tensor", lambda e: e.matmul(PSA[0:64, 2, :], lhsT=Rm[:], rhs=krl[:], start=False, stop=True), reads=[Rm, krl], writes=[(PSA, 2)])
            b.add("vector", lambda e, ts=ts: e.tensor_tensor(out=t1[:], in0=krf[:], in1=rope[:, 0, ts], op=ALU.mult), reads=[krf, rope], writes=[t1])
            b.add("vector", lambda e, ts=ts: e.tensor_tensor(out=t2[:], in0=PSA[0:64, 2, :], in1=rope[:, 1, ts], op=ALU.mult), reads=[(PSA, 2), rope], writes=[t2])
            b.add("vector", lambda e: e.tensor_tensor(out=kpo[:], in0=t1[:], in1=t2[:], op=ALU.add), reads=[t1, t2], writes=[kpo])
            for c in range(5):
                b.add("sync", lambda e, c=c, ts=ts: e.dma_start(out=P.mlat[c][:, ts], in_=outn[:, c, :]), reads=[(outn, c)], writes=[("mlat", (c, G))], dma=True)
            b.add("sync", lambda e, ts=ts: e.dma_start(out=P.mlat[5][0:64, ts], in_=kpo[:]), reads=[kpo], writes=[("mlat", (5, G))], dma=True)


def mla2_phase(b, P, W):
    with b.phase():
        Wq = b.sbuf("m2_Wq", [128, 3, 1536], BF16)
        Wkv = b.sbuf("m2_Wkv", [128, 2, 2048], BF16)
        rope = b.sbuf("m2_rope", [64, 2, S], F32)
        Rm = b.sbuf("m2_Rm", [64, 64], BF16)
        trib = b.sbuf("m2_trib", [128, 128], BF16)
        ones1 = b.sbuf("m2_ones1", [128, 128], BF16)
        cqn = b.sbuf("m2_cqn", [128, 3, S], BF16)
        ckvn = b.sbuf("m2_ckvn", [128, 2, S], BF16)
        kpe = b.sbuf("m2_kpe", [65, S], BF16)
        sqkpe = b.sbuf("m2_sqkpe", [64, S], BF16)
        QnT = b.sbuf("m2_QnT", [128, S], BF16)
        QpT = b.sbuf("m2_QpT", [65, S], BF16)
        KnT = b.sbuf("m2_KnT", [128, S], BF16)
        V = b.sbuf("m2_V", [128, 16, 128], BF16)
        qn2s = b.sbuf("m2_qn2s", [65, S], F32)
        sqa_ = [b.sbuf("m2_sqa%d" % r, [128, 512], BF16) for r in range(2)]
        sqk_ = [b.sbuf("m2_sqk%d" % r, [128, 512], BF16) for r in range(2)]
        sqb_ = [b.sbuf("m2_sqb%d" % r, [64, 512], BF16) for r in range(2)]
        qpf_ = [b.sbuf("m2_qpf%d" % r, [64, 512], F32) for r in range(2)]
        qph_ = [b.sbuf("m2_qph%d" % r, [64, 512], BF16) for r in range(2)]
        qpl_ = [b.sbuf("m2_qpl%d" % r, [64, 512], BF16) for r in range(2)]
        t1_ = [b.sbuf("m2_t1%d" % r, [64, 512], F32) for r in range(2)]
        t2_ = [b.sbuf("m2_t2%d" % r, [64, 512], F32) for r in range(2)]
        kmx = b.sbuf("m2_kmx", [128, 8], F32)
        crow = b.sbuf("m2_crow", [65, 512], F32)
        PT = [b.sbuf("m2_PT%d" % r, [128, 512], BF16) for r in range(2)]
        rr = b.sbuf("m2_rr", [128, 128], F32)
        yatt = b.sbuf("m2_yatt", [128, S], BF16)
        PSA, PSB = P.PSA, P.PSB
        for c in range(3):
            b.add("gpsimd", lambda e, c=c: e.dma_start(out=Wq[:, c, :], in_=W["wuq"][c]), reads=[], writes=[(Wq, c)], dma=True)
            b.add("sync", lambda e, c=c: e.dma_start(out=cqn[:, c, :], in_=P.mlat[c][:, :]), reads=[("mlat", None)], writes=[(cqn, c)], dma=True)
        for c in range(2):
            b.add("gpsimd", lambda e, c=c: e.dma_start(out=Wkv[:, c, :], in_=W["wukv"][c]), reads=[], writes=[(Wkv, c)], dma=True)
            b.add("sync", lambda e, c=c: e.dma_start(out=ckvn[:, c, :], in_=P.mlat[3 + c][:, :]), reads=[("mlat", None)], writes=[(ckvn, c)], dma=True)
        b.add("sync", lambda e: e.dma_start(out=kpe[0:64, :], in_=P.mlat[5][0:64, :]), reads=[("mlat", None)], writes=[(kpe, 0)], dma=True)
        b.add("sync", lambda e: e.dma_start(out=rope[:], in_=W["rope"][:]), reads=[], writes=[rope], dma=True)
        b.add("gpsimd", lambda e: e.dma_start(out=Rm[:], in_=P.cst_d[0:64, 384:448]), reads=[], writes=[Rm], dma=True)
        b.add("gpsimd", lambda e: e.dma_start(out=trib[:], in_=P.cst_d[:, 128:256]), reads=[], writes=[trib], dma=True)
        b.add("vector", lambda e: e.memset(ones1[:], 1.0), reads=[], writes=[ones1])
        b.add("vector", lambda e: e.memset(kpe[64:65, :], 1.0), reads=[], writes=[(kpe, 1)])
        b.add("scalar", lambda e: e.activation(out=sqkpe[:], in_=kpe[0:64, :], func=AF.Square), reads=[(kpe, 0)], writes=[sqkpe])

        for h in range(8):
            for n in range(4):
                ts = slice(n * 512, (n + 1) * 512)
                r = n % 2
                sqa, sqk, sqb, qpf, qph, qpl, t1, t2 = sqa_[r], sqk_[r], sqb_[r], qpf_[r], qph_[r], qpl_[r], t1_[r], t2_[r]
                for c in range(2):
                    b.add("tensor", lambda e, c=c, h=h, ts=ts: e.matmul(PSB[:, 0, :], lhsT=Wkv[:, c, h * 256:h * 256 + 128], rhs=ckvn[:, c, ts],
                                                                       start=(c == 0), stop=(c == 1)),
                          reads=[(Wkv, c), (ckvn, c)], writes=[(PSB, 0)])
                for c in range(3):
                    b.add("tensor", lambda e, c=c, h=h, ts=ts: e.matmul(PSB[:, 1, :], lhsT=Wq[:, c, h * 192:h * 192 + 128], rhs=cqn[:, c, ts],
                                                                       start=(c == 0), stop=(c == 2)),
                          reads=[(Wq, c), (cqn, c)], writes=[(PSB, 1)])
                for c in range(3):
                    b.add("tensor", lambda e, c=c, h=h, ts=ts: e.matmul(PSB[0:64, 2, :], lhsT=Wq[:, c, h * 192 + 128:h * 192 + 192], rhs=cqn[:, c, ts],
                                                                       start=(c == 0), stop=(c == 2)),
                          reads=[(Wq, c), (cqn, c)], writes=[(PSB, 2)])
                for blk in range(4):
                    tb = slice(n * 512 + blk * 128, n * 512 + (blk + 1) * 128)
                    for c in range(2):
                        b.add("tensor", lambda e, c=c, h=h, tb=tb, blk=blk: e.matmul(PSA[:, 0, blk * 128:(blk + 1) * 128], lhsT=ckvn[:, c, tb],
                                                                                  rhs=Wkv[:, c, h * 256 + 128:h * 256 + 256], start=(c == 0), stop=(c == 1)),
                              reads=[(ckvn, c), (Wkv, c)], writes=[(PSA, 0)])
                b.add("scalar", lambda e, ts=ts: e.activation(out=KnT[:, ts], in_=PSB[:, 0, :], func=AF.Identity), reads=[(PSB, 0)], writes=[(KnT, n)])
                b.add("scalar", lambda e, ts=ts: e.activation(out=QnT[:, ts], in_=PSB[:, 1, :], func=AF.Identity, scale=QSCALE), reads=[(PSB, 1)], writes=[(QnT, n)])
                b.add("scalar", lambda e, qpf=qpf: e.activation(out=qpf[:], in_=PSB[0:64, 2, :], func=AF.Identity, scale=QSCALE), reads=[(PSB, 2)], writes=[qpf])
                b.add("scalar", lambda e, n=n: e.activation(out=V[:, n * 4:(n + 1) * 4, :], in_=PSA[:, 0, :].rearrange("p (b d) -> p b d", d=128), func=AF.Identity),
                      reads=[(PSA, 0)], writes=[(V, n)])
                b.add("scalar", lambda e, ts=ts, sqk=sqk: e.activation(out=sqk[:], in_=KnT[:, ts], func=AF.Square), reads=[(KnT, n)], writes=[sqk])
                b.add("tensor", lambda e, sqk=sqk: e.matmul(PSA[:, 3, :], lhsT=ones1[:], rhs=sqk[:], start=True, stop=False), reads=[ones1, sqk], writes=[(PSA, 3)])
                b.add("tensor", lambda e, ts=ts: e.matmul(PSA[:, 3, :], lhsT=ones1[0:64, :], rhs=sqkpe[:, ts], start=False, stop=True),
                      reads=[ones1, sqkpe], writes=[(PSA, 3)])
                b.add("vector", lambda e, n=n: e.reduce_max(out=kmx[:, n:n + 1], in_=PSA[:, 3, :], axis=AX.X), reads=[(PSA, 3)], writes=[(kmx, n)])
                b.add("scalar", lambda e, ts=ts, sqa=sqa: e.activation(out=sqa[:], in_=QnT[:, ts], func=AF.Square), reads=[(QnT, n)], writes=[sqa])
                b.add("scalar", lambda e, qpf=qpf, qph=qph: e.activation(out=qph[:], in_=qpf[:], func=AF.Identity), reads=[qpf], writes=[qph])
                b.add("vector", lambda e, qpf=qpf, qph=qph, qpl=qpl: e.tensor_tensor(out=qpl[:], in0=qpf[:], in1=qph[:], op=ALU.subtract), reads=[qpf, qph], writes=[qpl])
                b.add("tensor", lambda e, qph=qph: e.matmul(PSA[0:64, 2, :], lhsT=Rm[:], rhs=qph[:], start=True, stop=False), reads=[Rm, qph], writes=[(PSA, 2)])
                b.add("tensor", lambda e, qpl=qpl: e.matmul(PSA[0:64, 2, :], lhsT=Rm[:], rhs=qpl[:], start=False, stop=True), reads=[Rm, qpl], writes=[(PSA, 2)])
                b.add("vector", lambda e, ts=ts, qpf=qpf, t1=t1: e.tensor_tensor(out=t1[:], in0=qpf[:], in1=rope[:, 0, ts], op=ALU.mult), reads=[qpf, rope], writes=[t1])
                b.add("vector", lambda e, ts=ts, t2=t2: e.tensor_tensor(out=t2[:], in0=PSA[0:64, 2, :], in1=rope[:, 1, ts], op=ALU.mult), reads=[(PSA, 2), rope], writes=[t2])
                b.add("vector", lambda e, ts=ts, t1=t1, t2=t2: e.tensor_tensor(out=QpT[0:64, ts], in0=t1[:], in1=t2[:], op=ALU.add), reads=[t1, t2], writes=[(QpT, n)])
                b.add("scalar", lambda e, ts=ts, sqb=sqb: e.activation(out=sqb[:], in_=QpT[0:64, ts], func=AF.Square), reads=[(QpT, n)], writes=[sqb])
                b.add("tensor", lambda e, sqa=sqa: e.matmul(PSA[:, 1, :], lhsT=ones1[:], rhs=sqa[:], start=True, stop=False), reads=[ones1, sqa], writes=[(PSA, 1)])
                b.add("tensor", lambda e, sqb=sqb: e.matmul(PSA[:, 1, :], lhsT=ones1[0:64, :], rhs=sqb[:], start=False, stop=True), reads=[ones1, sqb], writes=[(PSA, 1)])
                b.add("scalar", lambda e, ts=ts: e.activation(out=qn2s[64:65, ts], in_=PSA[64:65, 1, :], func=AF.Identity), reads=[(PSA, 1)], writes=[(qn2s, n)])
            b.add("vector", lambda e: e.reduce_max(out=kmx[:, 4:5], in_=kmx[:, 0:4], axis=AX.X), reads=[(kmx, n) for n in range(4)], writes=[(kmx, 4)])
            for n in range(4):
                ts = slice(n * 512, (n + 1) * 512)
                b.add("scalar", lambda e, ts=ts: e.activation(out=crow[64:65, :], in_=qn2s[64:65, ts], func=AF.Sqrt, scale=kmx[64:65, 4:5]),
                      reads=[(qn2s, n), (kmx, 4)], writes=[crow])
                b.add("vector", lambda e, ts=ts: e.tensor_scalar(out=QpT[64:65, ts], in0=crow[64:65, :], scalar1=-1.0, scalar2=None, op0=ALU.mult),
                      reads=[crow], writes=[(QpT, (n, "c"))])
            groups = []
            for i in range(16):
                for jg in range(0, i + 1, 4):
                    groups.append((i, list(range(jg, min(jg + 4, i + 1)))))

            def emit_st(g):
                i, js = groups[g]
                qs = slice(i * 128, (i + 1) * 128)
                bk = g % 2
                for jj, j in enumerate(js):
                    ks = slice(j * 128, (j + 1) * 128)
                    b.add("tensor", lambda e, jj=jj, ks=ks, qs=qs, bk=bk: e.matmul(PSB[:, bk, jj * 128:(jj + 1) * 128], lhsT=KnT[:, ks], rhs=QnT[:, qs],
                                                                                start=True, stop=False),
                          reads=[(KnT, j // 4), (QnT, i // 4)], writes=[(PSB, bk)])
                    b.add("tensor", lambda e, jj=jj, ks=ks, qs=qs, bk=bk: e.matmul(PSB[:, bk, jj * 128:(jj + 1) * 128], lhsT=kpe[0:65, ks], rhs=QpT[0:65, qs],
                                                                                start=False, stop=True),
                          reads=[(kpe, 0), (kpe, 1), (QpT, i // 4), (QpT, (i // 4, "c"))], writes=[(PSB, bk)])

            def emit_exp(g):
                i, js = groups[g]
                bk = g % 2
                pt = PT[g % 2]
                nb = len(js)
                b.add("scalar", lambda e: e.activation(out=pt[:, 0:nb * 128], in_=PSB[:, bk, 0:nb * 128], func=AF.Exp),
                      reads=[(PSB, bk)], writes=[pt])
                if js[-1] == i:
                    jj = len(js) - 1
                    b.add("vector", lambda e: e.tensor_tensor(out=pt[:, jj * 128:(jj + 1) * 128], in0=pt[:, jj * 128:(jj + 1) * 128],
                                                              in1=trib[:], op=ALU.mult),
                          reads=[pt, trib], writes=[pt])

            def emit_pv(g):
                i, js = groups[g]
                qs = slice(i * 128, (i + 1) * 128)
                ab = 2 * (i % 2)
                pt = PT[g % 2]
                for jj, j in enumerate(js):
                    b.add("tensor", lambda e, jj=jj, j=j: e.matmul(PSA[:, ab, 0:128], lhsT=V[:, j, :], rhs=pt[:, jj * 128:(jj + 1) * 128],
                                                                 start=(j == 0), stop=(j == i)),
                          reads=[(V, j // 4), pt], writes=[(PSA, ab)])
                    b.add("tensor", lambda e, jj=jj, j=j: e.matmul(PSA[:, ab + 1, 0:128], lhsT=ones1[:], rhs=pt[:, jj * 128:(jj + 1) * 128],
                                                                 start=(j == 0), stop=(j == i)),
                          reads=[ones1, pt], writes=[(PSA, ab + 1)])
                if js[-1] == i:
                    b.add("vector", lambda e: e.reciprocal(out=rr[:], in_=PSA[:, ab + 1, 0:128]), reads=[(PSA, ab + 1)], writes=[rr])
                    b.add("vector", lambda e: e.tensor_tensor(out=yatt[:, qs], in0=PSA[:, ab, 0:128], in1=rr[:], op=ALU.mult),
                          reads=[(PSA, ab), rr], writes=[(yatt, i)])

            emit_st(0)
            for g in range(len(groups)):
                emit_exp(g)
                if g + 1 < len(groups):
                    emit_st(g + 1)
                emit_pv(g)
            b.add("sync", lambda e, h=h: e.dma_start(out=P.ymix[8 + h][:, :], in_=yatt[:]), reads=[yatt], writes=[("ymix", (8 + h, 0))], dma=True)


def mixout_phase(b, P, L, wo):
    with b.phase():
        ybs = [b.sbuf("mo_yb%d" % r, [128, 16, 512], BF16) for r in range(2)]
        wob = [b.sbuf("mo_wo%d" % r, [128, 16, 128], BF16) for r in range(3)]
        ln = LNState(b, P, nset=2)
        PSB = P.PSB
        pend = []

        def load_wo(jj):
            t = wob[jj % 3]
            b.add("gpsimd", lambda e: e.dma_start(out=t[:].rearrange("p i c -> p (i c)"), in_=wo[jj % 8]), reads=[], writes=[t], dma=True)

        for n in range(4):
            ts = slice(n * 512, (n + 1) * 512)
            yb = ybs[n % 2]
            for c in range(16):
                b.add("sync", lambda e, c=c, ts=ts, yb=yb: e.dma_start(out=yb[:, c, :], in_=P.ymix[c][:, ts]), reads=[("ymix", None)], writes=[(yb, c)], dma=True)
            for j in range(3):
                load_wo(n * 8 + j)
            for j in range(8):
                w = wob[(n * 8 + j) % 3]
                ko = j % 2
                for c in range(16):
                    b.add("tensor", lambda e, w=w, c=c, ko=ko, yb=yb: e.matmul(PSB[:, ko, :], lhsT=w[:, c, :], rhs=yb[:, c, :], start=(c == 0), stop=(c == 15)),
                          reads=[w, (yb, c)], writes=[(PSB, ko)])
                emit_specs(b, pend, 4)
                residual_evac(b, P, ln, 1, PSB, ko, j, n, (PSB, 2), (PSB, 3))
                if j + 3 < 8:
                    load_wo(n * 8 + j + 3)
            emit_specs(b, pend)
            pend += ln_finish(b, P, ln, L, 1, n, (PSB, 2), (PSB, 3), sidx=n % 2, defer=(n < 3))

def dump_ymix(b, P, q0):
    with b.phase():
        t = b.sbuf("dbg_y", [128, S], BF16)
        for q in range(8):
            b.add("sync", lambda e, q=q: e.dma_start(out=t[:], in_=P.ymix[q0 + q][:, :]), reads=[("ymix", None)], writes=[t], dma=True)
            b.add("vector", lambda e, q=q: e.tensor_copy(out=P.xT[:, q, :], in_=t[:]), reads=[t], writes=[(P.xT, (q, n)) for n in range(4)])


def build_program(stages=None, dbg_modv=False, dump_modv=False):
    if stages is None:
        stages = []
        for L in range(2):
            stages += [("ada", L), ("ffa", L), ("mix", L), ("ffb", L)]
    b = Builder()
    nc = b.nc
    P = Prog()
    xT_d = b.dram("xT", [128, 8, S], F32, kind="ExternalInput")
    cT_d = b.dram("cT", [128, 8], F32, kind="ExternalInput")
    P.cst_d = b.dram("cst", [128, 512], F32, kind="ExternalInput")
    P.adaw, P.vec_d = {}, {}
    ffw = {}
    mixw = {}
    for L in range(2):
        P.vec_d[L] = b.dram("vec%d" % L, [128, 256], F32, kind="ExternalInput")
        if ("ada", L) in stages:
            P.adaw[L] = b.dram("adaw%d" % L, [8, 128, 9216], F32, kind="ExternalInput")
        for nm in ("ffa", "ffb"):
            if (nm, L) in stages:
                ffw[(nm, L)] = (b.dram("%s_in%d" % (nm, L), [NFF, 128, 2048], F32, kind="ExternalInput"),
                                b.dram("%s_out%d" % (nm, L), [8, 128, DFF], F32, kind="ExternalInput"))
    if ("mix", 1) in stages:
        mixw[1] = {
            "wu": b.dram("sg_wu", [16, 128, 1024], F32, kind="ExternalInput"),
            "wv": b.dram("sg_wv", [8, 128, 2048], F32, kind="ExternalInput"),
            "wo": b.dram("sg_wo", [8, 128, 2048], F32, kind="ExternalInput"),
            "wsT": b.dram("sg_wsT", [128, 2048], F32, kind="ExternalInput"),
            "bsb": b.dram("sg_bsb", [128, 2048], F32, kind="ExternalInput"),
            "bvrow": b.dram("sg_bvrow", [1, 2048], F32, kind="ExternalInput"),
        }
    kinds0 = [k for (k, L) in stages if L == 0]
    if any(k in ("mix", "ssd", "mla") for k in kinds0):
        mixw[0] = {
            "wz": b.dram("ev_wz", [8, 128, 1024], F32, kind="ExternalInput"),
            "wxbc": b.dram("ev_wxbc", [8, 128, 1536], F32, kind="ExternalInput"),
            "wdt": b.dram("ev_wdt", [8, 128, 16], F32, kind="ExternalInput"),
            "tokc": b.dram("ev_tokc", [128, 1056], F32, kind="ExternalInput"),
            "wlat": b.dram("ev_wlat", [8, 128, 704], F32, kind="ExternalInput"),
            "wuq": b.dram("ev_wuq", [3, 128, 1536], F32, kind="ExternalInput"),
            "wukv": b.dram("ev_wukv", [2, 128, 2048], F32, kind="ExternalInput"),
            "rope": b.dram("ev_rope", [64, 2, S], F32, kind="ExternalInput"),
            "wo": b.dram("ev_wo", [8, 128, 2048], F32, kind="ExternalInput"),
        }
        P.ymix = b.dram("ymix", [16, 128, S], BF16, kind="Internal")
        P.mlat = b.dram("mlat", [6, 128, S], BF16, kind="Internal")
    if dbg_modv:
        modv_d = b.dram("modv_dbg", [128, 72], F32, kind="ExternalInput")
    y_d = b.dram("yT", [128, 8, S], F32, kind="ExternalOutput")

    P.xT = b.sbuf("xT_sb", [128, 8, S], F32, persistent=True)
    P.cT = b.sbuf("cT_sb", [128, 8], F32, persistent=True)
    P.vec = [b.sbuf("vec_sb%d" % L, [128, 256], F32, persistent=True) for L in range(2)]
    P.modv = b.sbuf("modv", [128, 72], F32, persistent=True)
    P.dvA = b.sbuf("dvA", [128, 24], F32, persistent=True)
    P.dvG = b.sbuf("dvG", [128, 24], F32, persistent=True)
    P.onesb = b.sbuf("onesb", [128, 128], BF16, persistent=True)
    P.epsln = b.sbuf("epsln", [128, 1], F32, persistent=True)
    P.eps5 = b.sbuf("eps5", [128, 1], F32, persistent=True)
    P.PSA = b.psum("PSA", [128, 4, 512], F32, persistent=True)
    P.PSB = b.psum("PSB", [128, 4, 512], F32, persistent=True)
    b.exclusive = {"PSA", "PSB"}

    for k in range(8):
        b.add("sync", lambda e, k=k: e.dma_start(out=P.xT[:, k, :], in_=xT_d[:, k, :]),
              reads=[], writes=[(P.xT, (k, n)) for n in range(4)], dma=True)
    b.add("sync", lambda e: e.dma_start(out=P.cT[:], in_=cT_d[:]), reads=[], writes=[P.cT], dma=True)
    for L in range(2):
        b.add("sync", lambda e, L=L: e.dma_start(out=P.vec[L][:], in_=P.vec_d[L][:]), reads=[], writes=[P.vec[L]], dma=True)
    b.add("vector", lambda e: e.memset(P.onesb[:], 1.0 / 1024.0), reads=[], writes=[P.onesb])
    b.add("vector", lambda e: e.memset(P.epsln[:], EPS_LN), reads=[], writes=[P.epsln])
    b.add("vector", lambda e: e.memset(P.eps5[:], EPS), reads=[], writes=[P.eps5])
    if dbg_modv:
        b.add("sync", lambda e: e.dma_start(out=P.modv[:], in_=modv_d[:]), reads=[], writes=[P.modv], dma=True)
        derive_vecs(b, P)
    b.barrier()

    for (kind, L) in stages:
        if kind == "ada":
            ada_phase(b, P, L)
        elif kind == "ffa":
            ffn_phase(b, P, L, 0, *ffw[("ffa", L)])
        elif kind == "ffb":
            ffn_phase(b, P, L, 2, *ffw[("ffb", L)])
        elif kind == "mix" and L == 1:
            sgu_phase(b, P, L, mixw[1])
        elif kind == "ssd":
            ssd_phase(b, P, mixw[0])
            dump_ymix(b, P, 0)
        elif kind == "mla":
            mla1_phase(b, P, mixw[0])
            mla2_phase(b, P, mixw[0])
            dump_ymix(b, P, 8)
        elif kind == "mix" and L == 0:
            ssd_phase(b, P, mixw[0])
            mla1_phase(b, P, mixw[0])
            mla2_phase(b, P, mixw[0])
            mixout_phase(b, P, 0, mixw[0]["wo"])


    if dump_modv:
        b.add("vector", lambda e: e.tensor_copy(out=P.xT[:, 0, 0:72], in_=P.modv[:]), reads=[P.modv], writes=[(P.xT, (0, 0))])
    for k in range(8):
        b.add("sync", lambda e, k=k: e.dma_start(out=y_d[:, k, :], in_=P.xT[:, k, :]),
              reads=[(P.xT, (k, n)) for n in range(4)], writes=[], dma=True, out=True)
    nc = b.finish()
    return nc, b


def _fm(v):
    v = np.asarray(v, dtype=np.float32)
    return np.ascontiguousarray(v.reshape(-1, 128).T)


def prep_shared(inp):
    sh = {}
    cst = np.zeros((128, 512), np.float32)
    cst[:, 0:128] = np.eye(128, dtype=np.float32)
    cst[:, 128:256] = np.triu(np.ones((128, 128), np.float32))
    cst[:, 256:384] = (1.0 - np.triu(np.ones((128, 128), np.float32))) * -30000.0
    for mm in range(32):
        cst[mm + 32, 384 + mm] = -1.0
        cst[mm, 384 + mm + 32] = 1.0
    sh["cst"] = cst
    for L in range(2):
        p = "l%d_" % L
        sh["adaw%d" % L] = np.ascontiguousarray(inp[p + "ada_w"].reshape(8, 128, 9216))
        vec = np.zeros((128, 256), np.float32)
        vec[:, 0:72] = _fm(inp[p + "ada_b"])
        vec[:, 72:96] = _fm(inp[p + "ln_g"].reshape(-1))
        vec[:, 96:120] = _fm(inp[p + "ln_b"].reshape(-1))
        sh["vec%d" % L] = vec
        for nm in ("ffa", "ffb"):
            w_in = inp[p + nm + "_w_in"]
            t = w_in.reshape(8, 128, 2, NFF, 128)
            sh[nm + "_in%d" % L] = np.ascontiguousarray(t.transpose(3, 1, 2, 0, 4).reshape(NFF, 128, 2048))
            w_out = inp[p + nm + "_w_out"]
            t = w_out.reshape(NFF, 128, 8, 128)
            sh[nm + "_out%d" % L] = np.ascontiguousarray(t.transpose(2, 1, 0, 3).reshape(8, 128, DFF))
    w_in = inp["l0_w_in"]
    sh["ev_wz"] = np.ascontiguousarray(w_in[:, 0:1024].reshape(8, 128, 1024))
    sh["ev_wxbc"] = np.ascontiguousarray(w_in[:, 1024:2560].reshape(8, 128, 1536))
    sh["ev_wdt"] = np.ascontiguousarray(w_in[:, 2560:2576].reshape(8, 128, 16))
    sh["ev_wlat"] = np.ascontiguousarray(w_in[:, 2576:3280].reshape(8, 128, 704))
    sh["ev_wuq"] = np.ascontiguousarray(inp["l0_w_uq"].reshape(3, 128, 1536))
    sh["ev_wukv"] = np.ascontiguousarray(inp["l0_w_ukv"].reshape(2, 128, 2048))
    sh["ev_wo"] = np.ascontiguousarray(inp["l0_w_out"].reshape(16, 128, 8, 128).transpose(2, 1, 0, 3).reshape(8, 128, 2048))
    inv = (1.0 / (np.float32(10000.0) ** (np.arange(0, 64, 2, dtype=np.float32) / np.float32(64.0)))).astype(np.float32)
    ang = np.arange(S, dtype=np.float32)[:, None] * inv[None, :]
    cosT = np.cos(ang).astype(np.float32).T
    sinT = np.sin(ang).astype(np.float32).T
    rope = np.zeros((64, 2, S), np.float32)
    rope[0:32, 0] = cosT
    rope[32:64, 0] = cosT
    rope[0:32, 1] = sinT
    rope[32:64, 1] = sinT
    sh["ev_rope"] = rope
    tokc = np.zeros((128, 1056), np.float32)
    tokc[:, 0:1024] = np.repeat(inp["l0_d_skip"], 64)[None, :]
    tokc[:, 1024:1040] = inp["l0_dt_bias"][None, :]
    tokc[:, 1040:1056] = inp["l0_a_log"][None, :]
    sh["ev_tokc"] = tokc
    v0 = sh["vec0"]
    v0[:, 120:168] = np.concatenate([_fm(inp["l0_conv_w"][k]) for k in range(4)], axis=1)
    v0[:, 168:180] = _fm(inp["l0_conv_b"])
    v0[:, 180:188] = _fm(inp["l0_ssd_norm_w"])
    v0[:, 188:191] = _fm(inp["l0_q_norm_w"])
    v0[:, 191:193] = _fm(inp["l0_kv_norm_w"])
    w_uv = inp["l1_w_uv"]
    sh["sg_wu"] = np.ascontiguousarray(w_uv[:, :2048].reshape(8, 128, 16, 128).transpose(2, 1, 0, 3).reshape(16, 128, 1024))
    sh["sg_wv"] = np.ascontiguousarray(w_uv[:, 2048:].reshape(8, 128, 2048))
    sh["sg_wo"] = np.ascontiguousarray(inp["l1_w_out"].reshape(16, 128, 8, 128).transpose(2, 1, 0, 3).reshape(8, 128, 2048))
    sh["sg_wsT"] = np.ascontiguousarray(inp["l1_w_s"].transpose(2, 0, 1).reshape(128, 2048))
    sh["sg_bsb"] = np.ascontiguousarray(np.broadcast_to(inp["l1_b_s"].reshape(1, 2048), (128, 2048)))
    sh["sg_bvrow"] = np.ascontiguousarray(inp["l1_b_uv"][2048:].reshape(1, 2048))
    v1 = sh["vec1"]
    v1[:, 120:136] = _fm(inp["l1_b_uv"][:2048])
    v1[:, 136:152] = _fm(inp["l1_sgu_ln_g"])
    v1[:, 152:168] = _fm(inp["l1_sgu_ln_b"])
    return sh


def prep_core(inp, bi):
    x = inp["x"][bi]
    xT = np.ascontiguousarray(x.reshape(S, 8, 128).transpose(2, 1, 0))
    cT = _fm(inp["c"][bi])
    return {"xT": xT, "cT": cT}


def kernel(**inputs):
    inp = {k: np.asarray(v) for k, v in inputs.items()}
    nc, _ = build_program()
    sh = prep_shared(inp)
    decl = set()
    for nm in list(sh.keys()) + ["xT", "cT"]:
        try:
            nc.lookup_mloc(nm)
            decl.add(nm)
        except Exception:
            pass
    in_maps = []
    for bi in range(8):
        m = dict(sh)
        m.update(prep_core(inp, bi))
        in_maps.append({k: v for k, v in m.items() if k in decl})
    res = run_bass_kernel_spmd(nc, in_maps, core_ids=list(range(8)))
    out = np.empty((8, S, D), np.float32)
    for bi in range(8):
        yT = res.results[bi]["yT"]
        out[bi] = yT.transpose(2, 1, 0).reshape(S, D)
    return out
```
